# Optimizing a Trainium2 kernel written in Bass

```python
import jax, jax.numpy as jnp
from jax import lax
import numpy as np

D_MODEL = 2048
BATCH = 2
SEQ = 8192
DEPTH = 4

D_MIX = D_MODEL
W_POOL = D_MIX // 4
W_CONV = D_MIX // 4
W_SGU = D_MIX // 4
W_SSM = D_MIX - W_POOL - W_CONV - W_SGU
POOL_WINDOWS = (2, 4, 8, 16)
POOL_GROUP = W_POOL // len(POOL_WINDOWS)
CONV_WIDTH = 31
SGU_CHUNK = 128
SGU_HEADS = 4
SGU_HEAD_DIM = W_SGU // SGU_HEADS
SSM_HEAD_DIM = 64
SSM_HEADS = W_SSM // SSM_HEAD_DIM
SSM_GROUPS = 2
SSM_STATE = 128
SSM_CONV = 4
SSM_CHUNK = 128
SSM_CONV_DIM = W_SSM + 2 * SSM_GROUPS * SSM_STATE
D_FF = ((8 * D_MODEL // 3 + 255) // 256) * 256
IN_SIZES = (W_POOL, W_CONV, W_CONV, W_SGU, W_SGU, W_SSM, SSM_CONV_DIM, SSM_HEADS)
IN_COLS = sum(IN_SIZES)
EPS = 1e-6

kernel_name = "hybrid_parallel_pool_conv_sgu_ssd_macaron"


def rmsnorm(x, g):
    xf = x.astype(jnp.float32)
    y = xf * lax.rsqrt(jnp.mean(xf * xf, axis=-1, keepdims=True) + EPS)
    return (y * g.astype(jnp.float32)).astype(x.dtype)


def layernorm(x, g, b):
    xf = x.astype(jnp.float32)
    mu = jnp.mean(xf, axis=-1, keepdims=True)
    var = jnp.mean(jnp.square(xf - mu), axis=-1, keepdims=True)
    y = (xf - mu) * lax.rsqrt(var + EPS)
    return (y * g.astype(jnp.float32) + b.astype(jnp.float32)).astype(x.dtype)


def swiglu(h, w_gate, w_up, w_down):
    return (jax.nn.silu(h @ w_gate) * (h @ w_up)) @ w_down


def causal_depthwise_conv(x, w, b):
    k, c = w.shape
    y = lax.conv_general_dilated(
        x, w[:, None, :].astype(x.dtype), window_strides=(1,), padding=[(k - 1, 0)],
        dimension_numbers=("NWC", "WIO", "NWC"), feature_group_count=c)
    return y + b


def pool_mixer(a, w, scale):
    bsz, s, _ = a.shape
    af = a.astype(jnp.float32).reshape(bsz, s, len(POOL_WINDOWS), POOL_GROUP)
    cs = jnp.cumsum(af, axis=1)
    outs = []
    for g, win in enumerate(POOL_WINDOWS):
        c = cs[:, :, g]
        shifted = jnp.pad(c, ((0, 0), (win, 0), (0, 0)))[:, :s]
        count = jnp.minimum(jnp.arange(1, s + 1), win).astype(jnp.float32)[None, :, None]
        outs.append((c - shifted) / count - af[:, :, g])
    p = jnp.stack(outs, axis=2).astype(a.dtype)
    y = jnp.einsum("bsgc,gcd->bsgd", p, w).reshape(bsz, s, W_POOL)
    return y * scale


def conv_module(val, gate, dw_w, dw_b, ln_g, ln_b, pw_w, pw_b):
    h = val * jax.nn.sigmoid(gate)
    h = causal_depthwise_conv(h, dw_w, dw_b)
    h = jax.nn.silu(layernorm(h, ln_g, ln_b))
    return h @ pw_w + pw_b


def sgu_mixer(u, v, ln_g, ln_b, w_s, b_s):
    bsz, s, _ = u.shape
    u = jax.nn.gelu(u)
    v = layernorm(jax.nn.gelu(v), ln_g, ln_b)
    nc = s // SGU_CHUNK
    vc = v.reshape(bsz, nc, SGU_CHUNK, SGU_HEADS, SGU_HEAD_DIM)
    mask = jnp.tril(jnp.ones((SGU_CHUNK, SGU_CHUNK), dtype=bool))
    ws = jnp.where(mask, w_s, jnp.zeros_like(w_s))
    g = jnp.einsum("hts,bcshd->bcthd", ws, vc) + b_s.T[None, None, :, :, None]
    return u * g.reshape(bsz, s, W_SGU)


def ssd_mixer(z, xbc, dt_raw, conv_w, conv_b, dt_bias, a_log, d_skip, norm_g):
    bsz, s, _ = z.shape
    H, P, N, Q = SSM_HEADS, SSM_HEAD_DIM, SSM_STATE, SSM_CHUNK
    rep = SSM_HEADS // SSM_GROUPS
    xbc = jax.nn.silu(causal_depthwise_conv(xbc, conv_w, conv_b))
    xs, bm, cm = jnp.split(xbc, [W_SSM, W_SSM + SSM_GROUPS * N], axis=-1)
    xs = xs.astype(jnp.float32).reshape(bsz, s, H, P)
    bm = jnp.repeat(bm.astype(jnp.float32).reshape(bsz, s, SSM_GROUPS, N), rep, axis=2)
    cm = jnp.repeat(cm.astype(jnp.float32).reshape(bsz, s, SSM_GROUPS, N), rep, axis=2)
    dt = jax.nn.softplus(dt_raw.astype(jnp.float32) + dt_bias.astype(jnp.float32))
    a = -jnp.exp(a_log.astype(jnp.float32))
    nc = s // Q
    x_c = (xs * dt[..., None]).reshape(bsz, nc, Q, H, P)
    b_c = bm.reshape(bsz, nc, Q, H, N)
    c_c = cm.reshape(bsz, nc, Q, H, N)
    a_cs = jnp.cumsum((dt * a).reshape(bsz, nc, Q, H).transpose(0, 3, 1, 2), axis=-1)
    mask = jnp.tril(jnp.ones((Q, Q), dtype=bool))
    seg = a_cs[..., :, None] - a_cs[..., None, :]
    decay = jnp.where(mask, jnp.exp(jnp.where(mask, seg, 0.0)), 0.0)
    scores = jnp.einsum("bclhn,bcshn->bhcls", c_c, b_c) * decay
    y_diag = jnp.einsum("bhcls,bcshp->bclhp", scores, x_c)
    decay_states = jnp.exp(a_cs[..., -1:] - a_cs)
    states = jnp.einsum("bclhn,bhcl,bclhp->bchpn", b_c, decay_states, x_c)
    chunk_decay = jnp.exp(a_cs[..., -1])

    def step(hstate, inp):
        s_c, dec_c = inp
        return hstate * dec_c[..., None, None] + s_c, hstate

    h0 = jnp.zeros((bsz, H, P, N), jnp.float32)
    _, prev = lax.scan(step, h0, (states.transpose(1, 0, 2, 3, 4), chunk_decay.transpose(2, 0, 1)))
    prev = prev.transpose(1, 0, 2, 3, 4)
    y_off = jnp.einsum("bclhn,bchpn,bhcl->bclhp", c_c, prev, jnp.exp(a_cs))
    y = (y_diag + y_off).reshape(bsz, s, H, P) + xs * d_skip.astype(jnp.float32)[:, None]
    y = y.reshape(bsz, s, W_SSM) * jax.nn.silu(z.astype(jnp.float32))
    yg = y.reshape(bsz, s, SSM_GROUPS, W_SSM // SSM_GROUPS)
    yg = yg * lax.rsqrt(jnp.mean(yg * yg, axis=-1, keepdims=True) + EPS)
    return (yg.reshape(bsz, s, W_SSM) * norm_g.astype(jnp.float32)).astype(z.dtype)


def setup_inputs(seed: int = 0) -> dict:
    key = jax.random.key(seed)
    ks = iter(jax.random.split(key, 40))
    f32 = jnp.float32

    def dense(shape, fan_in):
        return jax.random.normal(next(ks), shape, f32) * (fan_in ** -0.5)

    def gain(shape):
        return 1.0 + 0.02 * jax.random.normal(next(ks), shape, f32)

    def bias(shape):
        return 0.02 * jax.random.normal(next(ks), shape, f32)

    L = DEPTH
    x = jax.random.normal(next(ks), (BATCH, SEQ, D_MODEL), f32)
    ffn1_norm = gain((L, D_MODEL))
    ffn1_w_gate = dense((L, D_MODEL, D_FF), D_MODEL)
    ffn1_w_up = dense((L, D_MODEL, D_FF), D_MODEL)
    ffn1_w_down = dense((L, D_FF, D_MODEL), D_FF)
    mix_norm = gain((L, D_MODEL))
    w_in = dense((L, D_MODEL, IN_COLS), D_MODEL)
    pool_w = dense((L, len(POOL_WINDOWS), POOL_GROUP, POOL_GROUP), POOL_GROUP)
    pool_scale = 1.0 + 0.1 * jax.random.normal(next(ks), (L, W_POOL), f32)
    conv_dw_w = dense((L, CONV_WIDTH, W_CONV), CONV_WIDTH)
    conv_dw_b = bias((L, W_CONV))
    conv_ln_g = gain((L, W_CONV))
    conv_ln_b = bias((L, W_CONV))
    conv_pw_w = dense((L, W_CONV, W_CONV), W_CONV)
    conv_pw_b = bias((L, W_CONV))
    sgu_ln_g = gain((L, W_SGU))
    sgu_ln_b = bias((L, W_SGU))
    sgu_w_s = dense((L, SGU_HEADS, SGU_CHUNK, SGU_CHUNK), SGU_CHUNK)
    sgu_b = gain((L, SGU_HEADS, SGU_CHUNK))
    ssm_conv_w = dense((L, SSM_CONV, SSM_CONV_DIM), SSM_CONV)
    ssm_conv_b = bias((L, SSM_CONV_DIM))
    u = jax.random.uniform(next(ks), (L, SSM_HEADS), f32)
    dt0 = jnp.maximum(jnp.exp(u * (jnp.log(0.1) - jnp.log(0.001)) + jnp.log(0.001)), 1e-4)
    ssm_dt_bias = dt0 + jnp.log(-jnp.expm1(-dt0))
    ssm_a_log = jnp.log(jax.random.uniform(next(ks), (L, SSM_HEADS), f32, 1.0, 16.0))
    ssm_d = gain((L, SSM_HEADS))
    ssm_norm = gain((L, W_SSM))
    w_out = dense((L, D_MIX, D_MODEL), D_MIX)
    ffn2_norm = gain((L, D_MODEL))
    ffn2_w_gate = dense((L, D_MODEL, D_FF), D_MODEL)
    ffn2_w_up = dense((L, D_MODEL, D_FF), D_MODEL)
    ffn2_w_down = dense((L, D_FF, D_MODEL), D_FF)
    final_norm = gain((D_MODEL,))
    return {
        "x": x,
        "ffn1_norm": ffn1_norm, "ffn1_w_gate": ffn1_w_gate, "ffn1_w_up": ffn1_w_up, "ffn1_w_down": ffn1_w_down,
        "mix_norm": mix_norm, "w_in": w_in,
        "pool_w": pool_w, "pool_scale": pool_scale,
        "conv_dw_w": conv_dw_w, "conv_dw_b": conv_dw_b, "conv_ln_g": conv_ln_g, "conv_ln_b": conv_ln_b,
        "conv_pw_w": conv_pw_w, "conv_pw_b": conv_pw_b,
        "sgu_ln_g": sgu_ln_g, "sgu_ln_b": sgu_ln_b, "sgu_w_s": sgu_w_s, "sgu_b": sgu_b,
        "ssm_conv_w": ssm_conv_w, "ssm_conv_b": ssm_conv_b, "ssm_dt_bias": ssm_dt_bias,
        "ssm_a_log": ssm_a_log, "ssm_d": ssm_d, "ssm_norm": ssm_norm,
        "w_out": w_out,
        "ffn2_norm": ffn2_norm, "ffn2_w_gate": ffn2_w_gate, "ffn2_w_up": ffn2_w_up, "ffn2_w_down": ffn2_w_down,
        "final_norm": final_norm,
    }


def reference(x, ffn1_norm, ffn1_w_gate, ffn1_w_up, ffn1_w_down, mix_norm, w_in,
              pool_w, pool_scale, conv_dw_w, conv_dw_b, conv_ln_g, conv_ln_b, conv_pw_w, conv_pw_b,
              sgu_ln_g, sgu_ln_b, sgu_w_s, sgu_b, ssm_conv_w, ssm_conv_b, ssm_dt_bias, ssm_a_log,
              ssm_d, ssm_norm, w_out, ffn2_norm, ffn2_w_gate, ffn2_w_up, ffn2_w_down, final_norm):
    split_points = list(np.cumsum(IN_SIZES)[:-1])
    for l in range(DEPTH):
        h = rmsnorm(x, ffn1_norm[l])
        x = x + 0.5 * swiglu(h, ffn1_w_gate[l], ffn1_w_up[l], ffn1_w_down[l])
        h = rmsnorm(x, mix_norm[l])
        proj = h @ w_in[l]
        a, c_val, c_gate, s_u, s_v, m_z, m_xbc, m_dt = jnp.split(proj, split_points, axis=-1)
        y_a = pool_mixer(a, pool_w[l], pool_scale[l])
        y_b = conv_module(c_val, c_gate, conv_dw_w[l], conv_dw_b[l], conv_ln_g[l], conv_ln_b[l],
                          conv_pw_w[l], conv_pw_b[l])
        y_c = sgu_mixer(s_u, s_v, sgu_ln_g[l], sgu_ln_b[l], sgu_w_s[l], sgu_b[l])
        y_d = ssd_mixer(m_z, m_xbc, m_dt, ssm_conv_w[l], ssm_conv_b[l], ssm_dt_bias[l],
                        ssm_a_log[l], ssm_d[l], ssm_norm[l])
        y = jnp.concatenate([y_a, y_b.astype(y_a.dtype), y_c.astype(y_a.dtype), y_d.astype(y_a.dtype)], axis=-1)
        x = x + y @ w_out[l]
        h = rmsnorm(x, ffn2_norm[l])
        x = x + 0.5 * swiglu(h, ffn2_w_gate[l], ffn2_w_up[l], ffn2_w_down[l])
    return rmsnorm(x, final_norm)
```

```python
import contextlib
import numpy as np
import concourse.bass as bass
import concourse.mybir as mybir
from concourse.bass_utils import run_bass_kernel_spmd

F32 = mybir.dt.float32
BF16 = mybir.dt.bfloat16
AF = mybir.ActivationFunctionType
ALU = mybir.AluOpType

D = 2048
DC = 16
HALF = 1024
NT = 512
DFF = 5632
FC = 44
L = 4
EPS = 1e-6
SEM_LIMIT = 12000
SAME_ENGINE_SYNC = True
GR = 64
ARENA = 52800


class V:
    def __init__(self, ap, lo, hi, es):
        self.ap = ap
        self.lo = lo
        self.hi = hi
        self.es = es

    def sl(self, e0, e1):
        lo = self.lo + (e0 * self.es) // 4
        hi = self.lo + (e1 * self.es + 3) // 4
        return V(self.ap[:, e0:e1], lo, hi, self.es)


class Op:
    __slots__ = ("eng", "fn", "reads", "writes", "dma_key", "deps", "needs_inc", "sem", "val", "idx")

    def __init__(self, eng, fn, reads, writes, dma_key):
        self.eng = eng
        self.fn = fn
        self.reads = reads
        self.writes = writes
        self.dma_key = dma_key
        self.deps = ()
        self.needs_inc = dma_key is not None
        self.sem = None
        self.val = 0


def _keys(items):
    out = []
    for it in items:
        if isinstance(it, V):
            for g in range(it.lo // GR, (it.hi + GR - 1) // GR):
                out.append(g)
        elif isinstance(it, (list, tuple)) and it and isinstance(it[0], V):
            out.extend(_keys(it))
        else:
            out.append(it)
    return out


class Prog:
    ENGS = ("pe", "act", "dve", "pool", "sp")

    def __init__(self):
        self.ops = []

    def add(self, eng, fn, reads=(), writes=(), dma_key=None):
        op = Op(eng, fn, _keys(reads), _keys(writes), dma_key)
        self.ops.append(op)
        return op

    def pe(self, fn, reads=(), writes=()):
        return self.add("pe", fn, reads, writes)

    def act(self, fn, reads=(), writes=()):
        return self.add("act", fn, reads, writes)

    def dve(self, fn, reads=(), writes=()):
        return self.add("dve", fn, reads, writes)

    def pool(self, fn, reads=(), writes=()):
        return self.add("pool", fn, reads, writes)

    def dma(self, eng, out, in_, reads, writes, key):
        return self.add(eng, lambda e: e.dma_start(out=out, in_=in_), reads, writes, dma_key=key)

    def analyze(self):
        last_w = {}
        readers = {}
        last_dma = {}
        for idx, op in enumerate(self.ops):
            op.idx = idx
            deps = {}
            for r in op.reads:
                w = last_w.get(r)
                if w is not None:
                    deps[id(w)] = w
            for r in op.writes:
                w = last_w.get(r)
                if w is not None:
                    deps[id(w)] = w
                rl = readers.get(r)
                if rl:
                    for rd in rl:
                        deps[id(rd)] = rd
            if op.dma_key is not None:
                p = last_dma.get(op.dma_key)
                if p is not None:
                    deps[id(p)] = p
                last_dma[op.dma_key] = op
            deps.pop(id(op), None)
            dl = []
            for d in deps.values():
                if d.dma_key is None and d.eng == op.eng:
                    if op.eng == "pe" or not SAME_ENGINE_SYNC:
                        continue
                dl.append(d)
            op.deps = dl
            for d in dl:
                d.needs_inc = True
            for r in op.writes:
                last_w[r] = op
                readers[r] = None
            for r in op.reads:
                rl = readers.get(r)
                if rl is None:
                    readers[r] = [op]
                elif rl[-1] is not op:
                    rl.append(op)

    def emit(self, nc, stack):
        self.analyze()
        cur = {}
        cnt = {}
        dsem = {}
        dcnt = {}
        nsem = [0]

        def newsem():
            nsem[0] += 1
            return stack.enter_context(nc.semaphore("s%d" % nsem[0]))

        for op in self.ops:
            if op.dma_key is not None:
                k = op.dma_key
                if k not in dsem or dcnt[k] + 16 > SEM_LIMIT:
                    dsem[k] = newsem()
                    dcnt[k] = 0
                dcnt[k] += 16
                op.sem = dsem[k]
                op.val = dcnt[k]
            elif op.needs_inc:
                e = op.eng
                if e not in cur or cnt[e] + 1 > SEM_LIMIT:
                    cur[e] = newsem()
                    cnt[e] = 0
                cnt[e] += 1
                op.sem = cur[e]
                op.val = cnt[e]
        self.nsem = nsem[0]
        per_eng = {e: [o for o in self.ops if o.eng == e] for e in self.ENGS}

        def run_engine(eng_obj, ops):
            waited = {}
            for op in ops:
                for d in op.deps:
                    key = id(d.sem)
                    if waited.get(key, 0) >= d.val:
                        continue
                    eng_obj.wait_ge(d.sem, d.val)
                    waited[key] = d.val
                ins = op.fn(eng_obj)
                if op.sem is not None:
                    ins.then_inc(op.sem, 16 if op.dma_key is not None else 1)

        with nc.Block() as block:
            @block.tensor
            def _(e):
                run_engine(e, per_eng["pe"])

            @block.scalar
            def _(e):
                run_engine(e, per_eng["act"])

            @block.vector
            def _(e):
                run_engine(e, per_eng["dve"])

            @block.gpsimd
            def _(e):
                run_engine(e, per_eng["pool"])

            @block.sync
            def _(e):
                run_engine(e, per_eng["sp"])


class Arena:
    def __init__(self, big):
        self.big = big

    def f32(self, off, n):
        assert off % GR == 0 and off + n <= ARENA, (off, n)
        return V(self.big[:, off:off + n], off, off + n, 4)

    def bf16(self, off, n):
        assert off % GR == 0 and n % 2 == 0 and off + n // 2 <= ARENA, (off, n)
        return V(self.big[:, off:off + n // 2].bitcast(BF16), off, off + n // 2, 2)


class Bump:
    def __init__(self, arena, lo, hi):
        self.a = arena
        self.lo = lo
        self.hi = hi
        self.o = lo

    def reset(self, to=None):
        self.o = self.lo if to is None else to

    def f32(self, n):
        v = self.a.f32(self.o, n)
        self.o += (n + GR - 1) // GR * GR
        assert self.o <= self.hi, (self.o, self.hi)
        return v

    def bf16(self, n):
        v = self.a.bf16(self.o, n)
        self.o += (n // 2 + GR - 1) // GR * GR
        assert self.o <= self.hi, (self.o, self.hi)
        return v


class ColPack:
    def __init__(self):
        self.cols = []
        self.index = {}
        self.n = 0

    def add(self, name, arr):
        arr = np.ascontiguousarray(arr, dtype=np.float32).reshape(128, -1)
        self.index[name] = (self.n, arr.shape[1])
        self.cols.append(arr)
        self.n += arr.shape[1]

    def build(self):
        return np.ascontiguousarray(np.concatenate(self.cols, axis=1))


def chunk_cols(v):
    v = np.asarray(v, dtype=np.float32)
    return np.ascontiguousarray(v.reshape(-1, 128).T)


def rep_rows(v):
    v = np.asarray(v, dtype=np.float32).reshape(1, -1)
    return np.ascontiguousarray(np.repeat(v, 128, axis=0))


def make_packs(inp):
    cp = ColPack()
    cp.add("eps", np.full((128, 1), EPS, np.float32))
    cp.add("one", np.full((128, 1), 1.0, np.float32))
    cp.add("final_norm", chunk_cols(inp["final_norm"]))
    for l in range(L):
        for nm in ("ffn1_norm", "ffn2_norm", "mix_norm"):
            cp.add("%s_%d" % (nm, l), chunk_cols(inp[nm][l]))
        cp.add("pool_scale_%d" % l, chunk_cols(inp["pool_scale"][l]))
        w = np.asarray(inp["conv_dw_w"][l])
        cp.add("cdw_w_%d" % l, w.reshape(31, 4, 128).transpose(2, 1, 0).reshape(128, 124))
        cp.add("cdw_b_%d" % l, chunk_cols(inp["conv_dw_b"][l]))
        cp.add("cln_g_%d" % l, chunk_cols(inp["conv_ln_g"][l]))
        cp.add("cln_b_%d" % l, chunk_cols(inp["conv_ln_b"][l]))
        cp.add("cpw_b_%d" % l, chunk_cols(inp["conv_pw_b"][l]))
        cp.add("sln_g_%d" % l, chunk_cols(inp["sgu_ln_g"][l]))
        cp.add("sln_b_%d" % l, chunk_cols(inp["sgu_ln_b"][l]))
        w = np.asarray(inp["ssm_conv_w"][l])
        cp.add("scw_%d" % l, w.reshape(4, 8, 128).transpose(2, 1, 0).reshape(128, 32))
        cp.add("scb_%d" % l, chunk_cols(inp["ssm_conv_b"][l]))
        cp.add("snorm_%d" % l, chunk_cols(inp["ssm_norm"][l]))
    rp = ColPack()
    k = np.arange(16)
    invc = np.stack([1.0 / np.minimum(k + 1, w) for w in (2, 4, 8, 16)]).astype(np.float32)
    rp.add("invc", rep_rows(invc))
    for l in range(L):
        rp.add("dtb_%d" % l, rep_rows(inp["ssm_dt_bias"][l]))
        rp.add("alog_%d" % l, rep_rows(inp["ssm_a_log"][l]))
        rp.add("dskip_%d" % l, rep_rows(inp["ssm_d"][l]))
        rp.add("sgub_%d" % l, rep_rows(inp["sgu_b"][l]))
    return cp, rp


LW_POOL, LW_PW, LW_SGU, LW_DT = 0, 512, 2560, 3072
LW_N = 3200


def prep_weights(inp):
    out = {}
    for nm in ("ffn1", "ffn2"):
        wg = np.asarray(inp[nm + "_w_gate"]).reshape(L, DC, 128, FC, 128)
        wu = np.asarray(inp[nm + "_w_up"]).reshape(L, DC, 128, FC, 128)
        wgu = np.empty((L, FC, 128, 2, DC, 128), np.float32)
        wgu[:, :, :, 0] = wg.transpose(0, 3, 2, 1, 4)
        wgu[:, :, :, 1] = wu.transpose(0, 3, 2, 1, 4)
        out[nm + "_wgu"] = wgu.reshape(L, FC, 128, 2 * DC * 128)
        wd = np.asarray(inp[nm + "_w_down"]).reshape(L, 2, 22, 128, DC, 128)
        out[nm + "_wd"] = np.ascontiguousarray(wd.transpose(0, 4, 1, 3, 2, 5)).reshape(L, DC, 2, 128, 22 * 128)
    win = np.asarray(inp["w_in"])
    w32 = win[:, :, :4096].reshape(L, DC, 128, 32, 128)
    out["w_in"] = np.ascontiguousarray(w32.transpose(0, 3, 2, 1, 4)).reshape(L, 32, 128, DC * 128)
    wo = np.asarray(inp["w_out"]).reshape(L, DC, 128, DC, 128)
    out["w_out"] = np.ascontiguousarray(wo.transpose(0, 3, 2, 1, 4)).reshape(L, DC, 128, DC * 128)
    lw = np.empty((L, 128, LW_N), np.float32)
    lw[:, :, LW_POOL:LW_POOL + 512] = np.asarray(inp["pool_w"]).transpose(0, 2, 1, 3).reshape(L, 128, 512)
    lw[:, :, LW_PW:LW_PW + 2048] = np.asarray(inp["conv_pw_w"]).reshape(L, 4, 128, 512).transpose(0, 2, 1, 3).reshape(L, 128, 2048)
    lw[:, :, LW_SGU:LW_SGU + 512] = np.asarray(inp["sgu_w_s"]).transpose(0, 3, 1, 2).reshape(L, 128, 512)
    lw[:, :, LW_DT:LW_DT + 128] = win[:, :, 4096:4104].reshape(L, DC, 128, 8).transpose(0, 2, 1, 3).reshape(L, 128, 128)
    out["lw"] = lw
    return out


def make_cmat():
    c = np.zeros((128, 512), np.float32)
    c[:, 0:128] = 1.0
    c[:, 128:256] = np.eye(128, dtype=np.float32)
    k = np.arange(128)
    c[:, 256:384] = (k[:, None] <= k[None, :]).astype(np.float32)
    c[:, 384:512] = np.where(k[:, None] > k[None, :], -30000.0, 0.0)
    return c


class Builder:
    def __init__(self, stages, NH, cp, rp):
        self.stages = stages
        self.NH = NH
        self.T = NH * HALF
        self.ci = cp.index
        self.ri = rp.index
        self.ncp = (cp.n + GR - 1) // GR * GR
        self.nrp = (rp.n + GR - 1) // GR * GR
        nc = bass.Bass("TRN2", target_bir_lowering=False)
        self.nc = nc
        self.P = Prog()
        dt = nc.dram_tensor
        T = self.T
        self.xin = dt("xin", [DC, 128, T], F32, kind="ExternalInput").ap()
        self.out = dt("out", [DC, 128, T], F32, kind="ExternalOutput").ap()
        self.xt = dt("xt", [DC, 128, T], F32).ap()
        self.cols_d = dt("cols", [128, cp.n], F32, kind="ExternalInput").ap()
        self.rows_d = dt("rows", [128, rp.n], F32, kind="ExternalInput").ap()
        self.cmat_d = dt("cmat", [128, 512], F32, kind="ExternalInput").ap()
        self.w = {}
        for nm in ("ffn1", "ffn2"):
            if not any(st[0] == "ffn" and st[2] == nm for st in stages):
                continue
            self.w[nm + "_wgu"] = dt(nm + "_wgu", [L, FC, 128, 2 * DC * 128], F32, kind="ExternalInput").ap()
            self.w[nm + "_wd"] = dt(nm + "_wd", [L, DC, 2, 128, 22 * 128], F32, kind="ExternalInput").ap()
        self.w["w_in"] = dt("w_in", [L, 32, 128, DC * 128], F32, kind="ExternalInput").ap()
        self.w["w_out"] = dt("w_out", [L, DC, 128, DC * 128], F32, kind="ExternalInput").ap()
        self.w["lw"] = dt("lw", [L, 128, LW_N], F32, kind="ExternalInput").ap()

    def plan(self, stack):
        nc = self.nc
        big = stack.enter_context(nc.sbuf_tensor("arena", [128, ARENA], F32))
        A = Arena(big)
        self.A = A
        pb = Bump(A, 0, ARENA)
        self.cols = pb.f32(self.ncp)
        self.rows = pb.f32(self.nrp)
        self.cmat = pb.f32(512)
        c = self.cmat.ap
        self.ones, self.ident, self.U, self.NEG = c[:, 0:128], c[:, 128:256], c[:, 256:384], c[:, 384:512]
        self.LWB = pb.bf16(LW_N)
        self.HALO_A = [pb.f32(16) for _ in range(4)]
        self.HALO_B = [pb.f32(32) for _ in range(4)]
        self.HALO_D = [pb.f32(4) for _ in range(8)]
        self.HST = pb.f32(512)
        self.HSTb = pb.bf16(512)
        self.ABC = pb.f32(8)
        self.H = pb.bf16(DC * HALF)
        self.Hc = [self.H.sl(i * HALF, (i + 1) * HALF) for i in range(DC)]
        rlo = pb.o
        self.ACTb = pb.bf16(FC * HALF)
        self.R = Bump(A, rlo, pb.o)
        self.XH = A.f32(rlo, DC * HALF)
        self.XHc = [self.XH.sl(i * HALF, (i + 1) * HALF) for i in range(DC)]
        self.SQ = [A.f32(rlo + DC * HALF + k * HALF, HALF) for k in range(2)]
        self.RSTD = A.f32(rlo + DC * HALF + 2 * HALF, HALF)
        self.STMP = A.f32(rlo + DC * HALF + 3 * HALF, HALF)
        self.WST = [pb.f32(4096) for _ in range(2)]
        self.WBF = [pb.bf16(4096) for _ in range(2)]
        self.XS = [pb.f32(NT) for _ in range(3)]
        self.SG = [pb.bf16(NT) for _ in range(2)]
        self.ps = [stack.enter_context(nc.psum_tensor("ps%d" % b, [128, 512], F32)) for b in range(8)]

    def col(self, name, k=0, n=1):
        off, w = self.ci[name]
        assert k + n <= w, (name, k, n, w)
        return self.cols.ap[:, off + k: off + k + n]

    def row(self, name, k=0, n=None):
        off, w = self.ri[name]
        n = w - k if n is None else n
        assert k + n <= w
        return self.rows.ap[:, off + k: off + k + n]

    def load_consts(self):
        P = self.P
        P.dma("sp", self.cols.ap[:, 0:self.cols_d.shape[1]], self.cols_d, [], [self.cols, "cols"], "c0")
        P.dma("sp", self.rows.ap[:, 0:self.rows_d.shape[1]], self.rows_d, [], [self.rows, "rows"], "c1")
        P.dma("sp", self.cmat.ap, self.cmat_d, [], [self.cmat, "cmat"], "c2")

    def copy_x_in(self):
        P = self.P
        for i in range(DC):
            P.dma("pool", self.xt[i], self.xin[i], [], [("XT", i, gt) for gt in range(2 * self.NH)], ("xcp", i % 4))

    def dump(self):
        P = self.P
        for i in range(DC):
            P.dma("pool", self.out[i], self.xt[i], [("XT", i, gt) for gt in range(2 * self.NH)], [("OUT", i)], ("xcp", i % 4))

    def rms_stats(self, hf):
        P = self.P
        t0 = hf * HALF
        for i in range(DC):
            P.dma("pool", self.XHc[i].ap, self.xt[i, :, t0:t0 + HALF],
                  [("XT", i, 2 * hf), ("XT", i, 2 * hf + 1)], [self.XHc[i]], ("xh", i % 4))
        for i in range(DC):
            sq = self.SQ[i % 2]
            P.act(lambda e, i=i, sq=sq: e.activation(out=sq.ap, in_=self.XHc[i].ap, func=AF.Square),
                  [self.XHc[i]], [sq])
            for tt in range(2):
                P.pe(lambda e, i=i, sq=sq, tt=tt: e.matmul(self.ps[tt][:, :], self.ones, sq.ap[:, tt * NT:(tt + 1) * NT],
                                                      start=(i == 0), stop=(i == DC - 1)),
                     [sq, "cmat"], [("ps", tt)])
        for tt in range(2):
            st = self.STMP.sl(tt * NT, (tt + 1) * NT)
            rs = self.RSTD.sl(tt * NT, (tt + 1) * NT)
            P.act(lambda e, tt=tt, st=st: e.activation(out=st.ap, in_=self.ps[tt][:, :], func=AF.Sqrt,
                                                      bias=self.col("eps"), scale=1.0 / D),
                  [("ps", tt), "cols"], [st])
            P.dve(lambda e, st=st, rs=rs: e.reciprocal(out=rs.ap, in_=st.ap), [st], [rs])

    def norm_pass(self, gname, hf):
        P = self.P
        self.rms_stats(hf)
        for i in range(DC):
            P.dve(lambda e, i=i: e.scalar_tensor_tensor(out=self.Hc[i].ap, in0=self.XHc[i].ap,
                                                       scalar=self.col(gname, i), in1=self.RSTD.ap,
                                                       op0=ALU.mult, op1=ALU.mult),
                  [self.XHc[i], self.RSTD, "cols"], [self.Hc[i]])

    def final_norm(self, hf):
        P = self.P
        t0 = hf * HALF
        self.rms_stats(hf)
        for i in range(DC):
            P.dve(lambda e, i=i: e.scalar_tensor_tensor(out=self.XHc[i].ap, in0=self.XHc[i].ap,
                                                       scalar=self.col("final_norm", i), in1=self.RSTD.ap,
                                                       op0=ALU.mult, op1=ALU.mult),
                  [self.XHc[i], self.RSTD, "cols"], [self.XHc[i]])
            P.dma("pool", self.out[i, :, t0:t0 + HALF], self.XHc[i].ap, [self.XHc[i]], [("OUT", i, hf)], ("xh", i % 4))

    def residual_update(self, i, gt, bank, factor):
        P = self.P
        k = self.xslot
        self.xslot = (self.xslot + 1) % 3
        xs = self.XS[k]
        dr = self.xt[i, :, gt * NT:(gt + 1) * NT]
        P.dma("pool", xs.ap, dr, [("XT", i, gt)], [xs], ("xs", k))
        P.dve(lambda e, xs=xs, bank=bank: e.scalar_tensor_tensor(out=xs.ap, in0=self.ps[bank][:, :], scalar=float(factor),
                                                               in1=xs.ap, op0=ALU.mult, op1=ALU.add),
              [("ps", bank), xs], [xs])
        P.dma("pool", dr, xs.ap, [xs], [("XT", i, gt)], ("xs", k))

    def stream_w(self, src, ncols, split=(0.5, 0.75)):
        P = self.P
        s = self.wslot
        self.wslot ^= 1
        st = self.WST[s].sl(0, ncols)
        wb = self.WBF[s].sl(0, ncols)
        P.dma("sp", st.ap, src, [], [st], ("wst", s))
        a = int(ncols * split[0]) // 128 * 128
        b = int(ncols * split[1]) // 128 * 128
        if a > 0:
            P.act(lambda e: e.activation(out=wb.ap[:, 0:a], in_=st.ap[:, 0:a], func=AF.Identity), [st.sl(0, a)], [wb.sl(0, a)])
        if b > a:
            P.pool(lambda e: e.tensor_copy(out=wb.ap[:, a:b], in_=st.ap[:, a:b]), [st.sl(a, b)], [wb.sl(a, b)])
        if ncols > b:
            P.dve(lambda e: e.tensor_copy(out=wb.ap[:, b:ncols], in_=st.ap[:, b:ncols]), [st.sl(b, ncols)], [wb.sl(b, ncols)])
        return wb

    def ffn(self, l, nm, hf):
        P = self.P
        wgu = self.w[nm + "_wgu"]
        wd = self.w[nm + "_wd"]
        self.norm_pass("%s_norm_%d" % (nm, l), hf)
        ACTc = [self.ACTb.sl(j * HALF, (j + 1) * HALF) for j in range(FC)]
        tasks = [("gu", j, 0) for j in range(FC)] + [("dn", i, fh) for i in range(DC) for fh in range(2)]

        def fetch(t):
            kind, a, b = tasks[t]
            if kind == "gu":
                return self.stream_w(wgu[l, a], 4096)
            return self.stream_w(wd[l, a, b], 2816, split=(0.5, 0.5))

        nxt = fetch(0)
        for t, (kind, a, b) in enumerate(tasks):
            wb = nxt
            if t + 1 < len(tasks):
                nxt = fetch(t + 1)
            if kind == "gu":
                j = a
                wv = wb.ap.rearrange("p (g c f) -> p g c f", g=2, c=DC)
                for tt in range(2):
                    bg = (j % 2) * 4 + tt
                    bu = (j % 2) * 4 + 2 + tt
                    sl = slice(tt * NT, (tt + 1) * NT)

                    def mm_chain(e, g, bank, wv=wv, sl=sl):
                        ins = None
                        for c in range(DC):
                            ins = e.matmul(self.ps[bank][:, :], wv[:, g, c, :], self.Hc[c].ap[:, sl],
                                           start=(c == 0), stop=(c == DC - 1))
                        return ins
                    P.pe(lambda e, f=mm_chain, b_=bg: f(e, 0, b_), [wb.sl(0, 2048), self.H], [("ps", bg)])
                    P.pe(lambda e, f=mm_chain, b_=bu: f(e, 1, b_), [wb.sl(2048, 4096), self.H], [("ps", bu)])
                    sg = self.SG[tt]
                    P.act(lambda e, sg=sg, bg=bg: e.activation(out=sg.ap, in_=self.ps[bg][:, :], func=AF.Silu),
                          [("ps", bg)], [sg])
                    dst = ACTc[j].sl(tt * NT, (tt + 1) * NT)
                    P.dve(lambda e, sg=sg, bu=bu, dst=dst: e.tensor_tensor(out=dst.ap, in0=self.ps[bu][:, :], in1=sg.ap, op=ALU.mult),
                          [("ps", bu), sg], [dst])
            else:
                i, fh = a, b
                wv = wb.ap.rearrange("p (c f) -> p c f", c=22)
                for tt in range(2):
                    bank = (i % 2) * 2 + tt
                    sl = slice(tt * NT, (tt + 1) * NT)

                    def mm_chain(e, wv=wv, sl=sl, bank=bank, fh=fh):
                        ins = None
                        for c in range(22):
                            ins = e.matmul(self.ps[bank][:, :], wv[:, c, :], ACTc[fh * 22 + c].ap[:, sl],
                                           start=(fh == 0 and c == 0), stop=(fh == 1 and c == 21))
                        return ins
                    P.pe(mm_chain, [wb] + [ACTc[fh * 22 + c].sl(tt * NT, (tt + 1) * NT) for c in range(22)], [("ps", bank)])
                if fh == 1:
                    for tt in range(2):
                        self.residual_update(i, 2 * hf + tt, (i % 2) * 2 + tt, 0.5)

    def layer_setup(self, l):
        P = self.P
        s = self.wslot
        self.wslot ^= 1
        st = self.WST[s].sl(0, LW_N)
        P.dma("sp", st.ap, self.w["lw"][l], [], [st], ("wst", s))
        lw = self.LWB
        for (a, b) in ((LW_POOL, LW_PW + 2048), (LW_DT, LW_N)):
            P.dve(lambda e, a=a, b=b: e.tensor_copy(out=lw.ap[:, a:b], in_=st.ap[:, a:b]), [st], [lw.sl(a, b)])
        P.dve(lambda e: e.tensor_tensor(out=lw.ap[:, LW_SGU:LW_SGU + 512].rearrange("p (h t) -> p h t", h=4),
                                        in0=st.ap[:, LW_SGU:LW_SGU + 512].rearrange("p (h t) -> p h t", h=4),
                                        in1=self.U.rearrange("p (o t) -> p o t", o=1).broadcast_to([128, 4, 128]),
                                        op=ALU.mult),
              [st, "cmat"], [lw.sl(LW_SGU, LW_SGU + 512)])
        P.act(lambda e: e.activation(out=self.ABC.ap, in_=self.row("alog_%d" % l), func=AF.Exp), ["rows"], [self.ABC])
        P.dve(lambda e: e.tensor_scalar(out=self.ABC.ap, in0=self.ABC.ap, scalar1=-1.0, scalar2=None, op0=ALU.mult),
              [self.ABC], [self.ABC])
        P.dve(lambda e: e.memset(self.HST.ap, 0.0), [], [self.HST])
        P.dve(lambda e: e.memset(self.HSTb.ap, 0.0), [], [self.HSTb])

    def start_wseq(self, l, groups):
        order = []
        if "a" in groups:
            order += [0, 1, 2, 3]
        if "b" in groups:
            for cc in range(4):
                order += [4 + cc, 8 + cc]
        if "c" in groups:
            order += list(range(12, 20))
        if "d" in groups:
            order += list(range(20, 32))
        self.wseq = [("in", c) for c in order] + [("out", i) for i in range(DC)]
        self.wl = l
        self.wpos = 0
        self.wpending = self._fetch_w(0)

    def _fetch_w(self, pos):
        kind, c = self.wseq[pos]
        if kind == "in":
            return self.stream_w(self.w["w_in"][self.wl, c], 2048, split=(0.375, 1.0))
        return self.stream_w(self.w["w_out"][self.wl, c], 2048, split=(0.75, 0.75))

    def next_w(self, tag):
        assert self.wseq[self.wpos] == tag, (self.wseq[self.wpos], tag)
        wb = self.wpending
        self.wpos += 1
        if self.wpos < len(self.wseq):
            self.wpending = self._fetch_w(self.wpos)
        return wb

    def proj(self, l, c):
        P = self.P
        wb = self.next_w(("in", c))
        wv = wb.ap.rearrange("p (c f) -> p c f", c=DC)
        pb = self.pbank
        self.pbank ^= 1
        banks = (pb * 2, pb * 2 + 1)
        for tt in range(2):
            sl = slice(tt * NT, (tt + 1) * NT)

            def mm_chain(e, wv=wv, sl=sl, bank=banks[tt]):
                ins = None
                for c2 in range(DC):
                    ins = e.matmul(self.ps[bank][:, :], wv[:, c2, :], self.Hc[c2].ap[:, sl], start=(c2 == 0), stop=(c2 == DC - 1))
                return ins
            P.pe(mm_chain, [wb, self.H], [("ps", banks[tt])])
        return banks

    def ln_stats(self, X, SQt, MEAN, RSTD, TMP):
        P = self.P
        for cc in range(4):
            P.act(lambda e, cc=cc: e.activation(out=SQt.ap, in_=X[cc].ap, func=AF.Square), [X[cc]], [SQt])
            for tt in range(2):
                sl = slice(tt * NT, (tt + 1) * NT)
                P.pe(lambda e, cc=cc, tt=tt, sl=sl: e.matmul(self.ps[4 + tt][:, :], self.ones, X[cc].ap[:, sl], start=(cc == 0), stop=(cc == 3)),
                     [X[cc], "cmat"], [("ps", 4 + tt)])
                P.pe(lambda e, cc=cc, tt=tt, sl=sl: e.matmul(self.ps[6 + tt][:, :], self.ones, SQt.ap[:, sl], start=(cc == 0), stop=(cc == 3)),
                     [SQt, "cmat"], [("ps", 6 + tt)])
        for tt in range(2):
            a, b = tt * NT, (tt + 1) * NT
            me, rs, tm = MEAN.sl(a, b), RSTD.sl(a, b), TMP.sl(a, b)
            P.dve(lambda e, tt=tt, me=me: e.tensor_scalar(out=me.ap, in0=self.ps[4 + tt][:, :], scalar1=1.0 / 512, scalar2=None, op0=ALU.mult),
                  [("ps", 4 + tt)], [me])
            P.dve(lambda e, me=me, tm=tm: e.tensor_tensor(out=tm.ap, in0=me.ap, in1=me.ap, op=ALU.mult), [me], [tm])
            P.dve(lambda e, tt=tt, tm=tm, rs=rs: e.scalar_tensor_tensor(out=rs.ap, in0=self.ps[6 + tt][:, :], scalar=1.0 / 512, in1=tm.ap,
                                                                    op0=ALU.mult, op1=ALU.subtract),
                  [("ps", 6 + tt), tm], [rs])
            P.act(lambda e, rs=rs: e.activation(out=rs.ap, in_=rs.ap, func=AF.Sqrt, bias=self.col("eps"), scale=1.0), [rs, "cols"], [rs])
            P.dve(lambda e, rs=rs: e.reciprocal(out=rs.ap, in_=rs.ap), [rs], [rs])

    def ln_apply(self, x, MEAN, RSTD, TMP, out, func, gcol, bcol):
        P = self.P
        P.dve(lambda e: e.tensor_tensor(out=TMP.ap, in0=x.ap, in1=MEAN.ap, op=ALU.subtract), [x, MEAN], [TMP])
        P.dve(lambda e: e.tensor_tensor(out=TMP.ap, in0=TMP.ap, in1=RSTD.ap, op=ALU.mult), [TMP, RSTD], [TMP])
        P.act(lambda e: e.activation(out=out.ap, in_=TMP.ap, func=func, bias=bcol, scale=gcol), [TMP, "cols"], [out])

    def mixer(self, l, hf, groups="abcd"):
        P = self.P
        A = self.A
        self.norm_pass("mix_norm_%d" % l, hf)
        self.start_wseq(l, groups)
        R = self.R
        R.reset()
        Y = R.bf16(DC * HALF)
        Yc = [Y.sl(c * HALF, (c + 1) * HALF) for c in range(DC)]
        base = R.o
        lw = self.LWB
        POOLW = lw.ap[:, LW_POOL:LW_POOL + 512].rearrange("p (g d) -> p g d", g=4)
        PWW = lw.ap[:, LW_PW:LW_PW + 2048].rearrange("p (c d) -> p c d", c=4)
        SGW = lw.ap[:, LW_SGU:LW_SGU + 512].rearrange("p (h t) -> p h t", h=4)
        WDT = lw.ap[:, LW_DT:LW_DT + 128].rearrange("p (c m) -> p c m", c=DC)
        T2 = (slice(0, NT), slice(NT, 2 * NT))

        def zero_y(c0, c1):
            for c in range(c0, c1):
                P.dve(lambda e, c=c: e.memset(Yc[c].ap, 0.0), [], [Yc[c]])

        if "a" in groups:
            AB = [R.f32(1040) for _ in range(4)]
            T = [R.f32(1040), R.f32(1040)]
            PB = [R.bf16(HALF) for _ in range(4)]
            TM = R.f32(16)
            for g in range(4):
                banks = self.proj(l, g)
                ab = AB[g]
                if hf == 0:
                    P.dve(lambda e, ab=ab: e.memset(ab.ap[:, 0:16], 0.0), [], [ab.sl(0, 16)])
                else:
                    P.dve(lambda e, ab=ab, g=g: e.tensor_copy(out=ab.ap[:, 0:16], in_=self.HALO_A[g].ap), [self.HALO_A[g]], [ab.sl(0, 16)])
                for tt in range(2):
                    d = ab.sl(16 + tt * NT, 16 + (tt + 1) * NT)
                    P.act(lambda e, d=d, b=banks[tt]: e.activation(out=d.ap, in_=self.ps[b][:, :], func=AF.Identity), [("ps", banks[tt])], [d])
                P.dve(lambda e, ab=ab, g=g: e.tensor_copy(out=self.HALO_A[g].ap, in_=ab.ap[:, 1024:1040]), [ab.sl(1024, 1040)], [self.HALO_A[g]])
                cur = ab
                for lev in range(g + 1):
                    k = 1 << lev
                    s0 = 2 * k - 1
                    dst = T[lev % 2]
                    P.dve(lambda e, cur=cur, dst=dst, k=k, s0=s0: e.tensor_tensor(out=dst.ap[:, s0:1040], in0=cur.ap[:, s0:1040],
                                                                               in1=cur.ap[:, s0 - k:1040 - k], op=ALU.add),
                          [cur], [dst])
                    cur = dst
                wdt = float(2 << g)
                P.dve(lambda e, cur=cur, ab=ab, g=g, wdt=wdt: e.scalar_tensor_tensor(out=PB[g].ap, in0=cur.ap[:, 16:1040], scalar=1.0 / wdt,
                                                                                  in1=ab.ap[:, 16:1040], op0=ALU.mult, op1=ALU.subtract),
                      [cur, ab], [PB[g]])
                if hf == 0:
                    P.dve(lambda e, cur=cur, g=g: e.tensor_tensor(out=TM.ap, in0=cur.ap[:, 16:32], in1=self.row("invc", g * 16, 16), op=ALU.mult),
                          [cur, "rows"], [TM])
                    P.dve(lambda e, ab=ab, g=g: e.tensor_tensor(out=PB[g].ap[:, 0:16], in0=TM.ap, in1=ab.ap[:, 16:32], op=ALU.subtract),
                          [TM, ab], [PB[g]])
                for tt in range(2):
                    P.pe(lambda e, g=g, tt=tt: e.matmul(self.ps[4 + tt][:, :], POOLW[:, g, :], PB[g].ap[:, T2[tt]], start=True, stop=True),
                         [PB[g], lw], [("ps", 4 + tt)])
                    d = Yc[g].sl(tt * NT, (tt + 1) * NT)
                    P.dve(lambda e, d=d, g=g, tt=tt: e.tensor_scalar(out=d.ap, in0=self.ps[4 + tt][:, :], scalar1=self.col("pool_scale_%d" % l, g),
                                                                  scalar2=None, op0=ALU.mult),
                          [("ps", 4 + tt), "cols"], [d])
        else:
            zero_y(0, 4)

        if "b" in groups:
            R.reset(base)
            HG = [R.f32(1056) for _ in range(4)]
            CV = [R.f32(HALF) for _ in range(4)]
            SQt, MEAN, RSTD, TMP = R.f32(HALF), R.f32(HALF), R.f32(HALF), R.f32(HALF)
            SIGT = R.f32(NT)
            LNS = [A.bf16(HG[0].lo + cc * 512, HALF) for cc in range(4)]
            for cc in range(4):
                bv = self.proj(l, 4 + cc)
                bgt = self.proj(l, 8 + cc)
                hg = HG[cc]
                if hf == 0:
                    P.dve(lambda e, hg=hg: e.memset(hg.ap[:, 0:32], 0.0), [], [hg.sl(0, 32)])
                else:
                    P.dve(lambda e, hg=hg, cc=cc: e.tensor_copy(out=hg.ap[:, 0:32], in_=self.HALO_B[cc].ap), [self.HALO_B[cc]], [hg.sl(0, 32)])
                for tt in range(2):
                    P.act(lambda e, b=bgt[tt]: e.activation(out=SIGT.ap, in_=self.ps[b][:, :], func=AF.Sigmoid), [("ps", bgt[tt])], [SIGT])
                    d = hg.sl(32 + tt * NT, 32 + (tt + 1) * NT)
                    P.dve(lambda e, d=d, b=bv[tt]: e.tensor_tensor(out=d.ap, in0=self.ps[b][:, :], in1=SIGT.ap, op=ALU.mult),
                          [("ps", bv[tt]), SIGT], [d])
                P.dve(lambda e, hg=hg, cc=cc: e.tensor_copy(out=self.HALO_B[cc].ap, in_=hg.ap[:, 1024:1056]), [hg.sl(1024, 1056)], [self.HALO_B[cc]])
                cv = CV[cc]
                P.dve(lambda e, hg=hg, cv=cv, cc=cc: e.tensor_scalar(out=cv.ap, in0=hg.ap[:, 2:1026], scalar1=self.col("cdw_w_%d" % l, cc * 31),
                                                                  scalar2=self.col("cdw_b_%d" % l, cc), op0=ALU.mult, op1=ALU.add),
                      [hg, "cols"], [cv])
                for k in range(1, 31):
                    P.dve(lambda e, hg=hg, cv=cv, cc=cc, k=k: e.scalar_tensor_tensor(out=cv.ap, in0=hg.ap[:, 2 + k:1026 + k],
                                                                                  scalar=self.col("cdw_w_%d" % l, cc * 31 + k), in1=cv.ap,
                                                                                  op0=ALU.mult, op1=ALU.add),
                          [hg, cv, "cols"], [cv])
            self.ln_stats(CV, SQt, MEAN, RSTD, TMP)
            for cc in range(4):
                self.ln_apply(CV[cc], MEAN, RSTD, TMP, LNS[cc], AF.Silu, self.col("cln_g_%d" % l, cc), self.col("cln_b_%d" % l, cc))
            for dd in range(4):
                for tt in range(2):
                    bank = 4 + (dd % 2) * 2 + tt

                    def pw_chain(e, dd=dd, tt=tt, bank=bank):
                        ins = None
                        for cc in range(4):
                            ins = e.matmul(self.ps[bank][:, :], PWW[:, cc, dd * 128:(dd + 1) * 128], LNS[cc].ap[:, T2[tt]], start=(cc == 0), stop=(cc == 3))
                        return ins
                    P.pe(pw_chain, LNS + [lw], [("ps", bank)])
                    d = Yc[4 + dd].sl(tt * NT, (tt + 1) * NT)
                    P.dve(lambda e, d=d, bank=bank, dd=dd: e.tensor_scalar(out=d.ap, in0=self.ps[bank][:, :], scalar1=self.col("cpw_b_%d" % l, dd),
                                                                        scalar2=None, op0=ALU.add),
                          [("ps", bank), "cols"], [d])
        else:
            zero_y(4, 8)

        if "c" in groups:
            R.reset(base)
            G1, G2, G3, G4 = R.f32(HALF), R.f32(HALF), R.f32(HALF), R.f32(HALF)
            Ub = [R.bf16(HALF) for _ in range(4)]
            Vf = [R.f32(HALF) for _ in range(4)]
            VT = [R.bf16(NT) for _ in range(8)]
            GT = R.f32(NT)

            def gelu(banks, dst):
                for tt in range(2):
                    d = G1.sl(tt * NT, (tt + 1) * NT)
                    P.act(lambda e, d=d, b=banks[tt]: e.activation(out=d.ap, in_=self.ps[b][:, :], func=AF.Identity), [("ps", banks[tt])], [d])
                P.act(lambda e: e.activation(out=G2.ap, in_=G1.ap, func=AF.Square), [G1], [G2])
                P.dve(lambda e: e.tensor_scalar(out=G2.ap, in0=G2.ap, scalar1=0.044715, scalar2=1.0, op0=ALU.mult, op1=ALU.add), [G2], [G2])
                P.dve(lambda e: e.tensor_tensor(out=G2.ap, in0=G2.ap, in1=G1.ap, op=ALU.mult), [G1, G2], [G2])
                P.act(lambda e: e.activation(out=G2.ap, in_=G2.ap, func=AF.Sigmoid, scale=1.5957691216057308), [G2], [G2])
                P.dve(lambda e: e.tensor_tensor(out=dst.ap, in0=G1.ap, in1=G2.ap, op=ALU.mult), [G1, G2], [dst])

            for hd in range(4):
                gelu(self.proj(l, 12 + hd), Ub[hd])
            for hd in range(4):
                gelu(self.proj(l, 16 + hd), Vf[hd])
            self.ln_stats(Vf, G1, G2, G3, G4)
            for hd in range(4):
                self.ln_apply(Vf[hd], G2, G3, G4, Vf[hd], AF.Identity, self.col("sln_g_%d" % l, hd), self.col("sln_b_%d" % l, hd))
            for q in range(8):
                bank = 4 + (q % 2)

                def tr4(e, q=q, bank=bank):
                    ins = None
                    for hd in range(4):
                        ins = e.transpose(out=self.ps[bank][:, hd * 128:(hd + 1) * 128], in_=Vf[hd].ap[:, q * 128:(q + 1) * 128], identity=self.ident)
                    return ins
                P.pe(tr4, [v.sl(q * 128, (q + 1) * 128) for v in Vf], [("ps", bank)])
                P.act(lambda e, q=q, bank=bank: e.activation(out=VT[q].ap, in_=self.ps[bank][:, :], func=AF.Identity), [("ps", bank)], [VT[q]])
            for tt in range(2):
                for hd in range(4):
                    bank = 6 + (hd % 2)

                    def sp4(e, tt=tt, hd=hd, bank=bank):
                        ins = None
                        for qq in range(4):
                            ins = e.matmul(self.ps[bank][:, qq * 128:(qq + 1) * 128], VT[tt * 4 + qq].ap[:, hd * 128:(hd + 1) * 128], SGW[:, hd, :],
                                           start=True, stop=True)
                        return ins
                    P.pe(sp4, [VT[tt * 4 + qq] for qq in range(4)] + [lw], [("ps", bank)])
                    P.dve(lambda e, hd=hd, bank=bank: e.tensor_tensor(
                        out=GT.ap.rearrange("p (q t) -> p q t", q=4), in0=self.ps[bank][:, :].rearrange("p (q t) -> p q t", q=4),
                        in1=self.row("sgub_%d" % l, hd * 128, 128).rearrange("p (o t) -> p o t", o=1).broadcast_to([128, 4, 128]), op=ALU.add),
                        [("ps", bank), "rows"], [GT])
                    d = Yc[8 + hd].sl(tt * NT, (tt + 1) * NT)
                    P.dve(lambda e, d=d, hd=hd, tt=tt: e.tensor_tensor(out=d.ap, in0=GT.ap, in1=Ub[hd].ap[:, T2[tt]], op=ALU.mult), [GT, Ub[hd]], [d])
        else:
            zero_y(8, 12)

        if "d" in groups:
            self.ssd(l, hf, Yc, base)
        else:
            zero_y(12, 16)

        for i in range(DC):
            wb = self.next_w(("out", i))
            wv = wb.ap.rearrange("p (c f) -> p c f", c=DC)
            for tt in range(2):
                bank = (i % 2) * 2 + tt

                def mm_chain(e, wv=wv, tt=tt, bank=bank):
                    ins = None
                    for rc in range(DC):
                        ins = e.matmul(self.ps[bank][:, :], wv[:, rc, :], Yc[rc].ap[:, T2[tt]], start=(rc == 0), stop=(rc == DC - 1))
                    return ins
                P.pe(mm_chain, [wb, Y], [("ps", bank)])
                self.residual_update(i, 2 * hf + tt, bank, 1.0)

    def ssd(self, l, hf, Yc, base):
        P = self.P
        A = self.A
        R = self.R
        R.reset(base)
        lw = self.LWB
        WDT = lw.ap[:, LW_DT:LW_DT + 128].rearrange("p (c m) -> p c m", c=DC)
        T2 = (slice(0, NT), slice(NT, 2 * NT))
        SZ = [R.bf16(HALF) for _ in range(4)]
        XR = [R.f32(1028)]
        XC = [R.f32(HALF) for _ in range(6)]
        CT = R.f32(HALF)
        CF = [R.bf16(HALF) for _ in range(2)]
        BF = [R.bf16(HALF) for _ in range(2)]
        DTR = R.f32(HALF)
        dtr8 = A.big[0:8, DTR.lo:DTR.lo + HALF]
        for i in range(4):
            bz = self.proj(l, 20 + i)
            for tt in range(2):
                d = SZ[i].sl(tt * NT, (tt + 1) * NT)
                P.act(lambda e, d=d, b=bz[tt]: e.activation(out=d.ap, in_=self.ps[b][:, :], func=AF.Silu), [("ps", bz[tt])], [d])
        for i in range(8):
            bx = self.proj(l, 24 + i)
            xr = XR[0]
            if hf == 0:
                P.dve(lambda e, xr=xr: e.memset(xr.ap[:, 0:4], 0.0), [], [xr.sl(0, 4)])
            else:
                P.dve(lambda e, xr=xr, i=i: e.tensor_copy(out=xr.ap[:, 0:4], in_=self.HALO_D[i].ap), [self.HALO_D[i]], [xr.sl(0, 4)])
            for tt in range(2):
                d = xr.sl(4 + tt * NT, 4 + (tt + 1) * NT)
                P.act(lambda e, d=d, b=bx[tt]: e.activation(out=d.ap, in_=self.ps[b][:, :], func=AF.Identity), [("ps", bx[tt])], [d])
            P.dve(lambda e, xr=xr, i=i: e.tensor_copy(out=self.HALO_D[i].ap, in_=xr.ap[:, 1024:1028]), [xr.sl(1024, 1028)], [self.HALO_D[i]])
            dst = XC[i] if i < 6 else CT
            P.dve(lambda e, xr=xr, dst=dst, i=i: e.tensor_scalar(out=dst.ap, in0=xr.ap[:, 1:1025], scalar1=self.col("scw_%d" % l, i * 4),
                                                              scalar2=self.col("scb_%d" % l, i), op0=ALU.mult, op1=ALU.add),
                  [xr, "cols"], [dst])
            for k in range(1, 4):
                P.dve(lambda e, xr=xr, dst=dst, i=i, k=k: e.scalar_tensor_tensor(out=dst.ap, in0=xr.ap[:, 1 + k:1025 + k],
                                                                              scalar=self.col("scw_%d" % l, i * 4 + k), in1=dst.ap,
                                                                              op0=ALU.mult, op1=ALU.add),
                      [xr, dst, "cols"], [dst])
            if i < 6:
                P.act(lambda e, dst=dst: e.activation(out=dst.ap, in_=dst.ap, func=AF.Silu), [dst], [dst])
                if i >= 4:
                    P.dve(lambda e, dst=dst, i=i: e.tensor_copy(out=BF[i - 4].ap, in_=dst.ap), [dst], [BF[i - 4]])
            else:
                P.act(lambda e, i=i: e.activation(out=CF[i - 6].ap, in_=CT.ap, func=AF.Silu), [CT], [CF[i - 6]])
        for tt in range(2):
            bank = 4 + tt

            def dt_chain(e, tt=tt, bank=bank):
                ins = None
                for c2 in range(DC):
                    ins = e.matmul(self.ps[bank][0:8, :], WDT[:, c2, :], self.Hc[c2].ap[:, T2[tt]], start=(c2 == 0), stop=(c2 == DC - 1))
                return ins
            P.pe(dt_chain, [lw, self.H], [("ps", bank)])
            d = DTR.sl(tt * NT, (tt + 1) * NT)
            P.act(lambda e, tt=tt, bank=bank: e.activation(out=dtr8[:, T2[tt]], in_=self.ps[bank][0:8, :], func=AF.Identity), [("ps", bank)], [d])

        Hs = Bump(A, self.H.lo, self.H.hi)
        DT1, DTE, DTT, DA, ACS, DIF, DS, EA, ECD, NACS = [Hs.f32(16) for _ in range(10)]
        XTOK = Hs.f32(512)
        XDT = Hs.bf16(512)
        XDEC = Hs.bf16(512)
        XSK = Hs.f32(512)
        BTOK = Hs.bf16(256)
        SCT = Hs.f32(256)
        DAREP = Hs.f32(1024)
        LH = Hs.bf16(1024)
        MH = Hs.bf16(1024)
        YT = Hs.f32(512)
        HT = Hs.f32(512)
        SQg = Hs.f32(HALF)
        RS = Hs.f32(HALF)
        YF = A.big[:, XC[0].lo:XC[0].lo + 4 * HALF].rearrange("p (i t) -> p i t", i=4)
        ps = self.ps

        def b864(ap8):
            return ap8.rearrange("p (h o) -> p h o", o=1).broadcast_to([128, 8, 64])

        def v864(ap512):
            return ap512.rearrange("p (h q) -> p h q", h=8)

        for q in range(8):
            qs = slice(q * 128, (q + 1) * 128)
            P.pe(lambda e, qs=qs: e.transpose(out=ps[4][:, 0:8], in_=dtr8[:, qs], identity=self.ident[0:8, 0:8]), [DTR.sl(q * 128, (q + 1) * 128)], [("ps", 4)])
            P.dve(lambda e: e.tensor_tensor(out=DT1.ap[:, 0:8], in0=ps[4][:, 0:8], in1=self.row("dtb_%d" % l), op=ALU.add), [("ps", 4), "rows"], [DT1])
            P.act(lambda e: e.activation(out=DTE.ap[:, 0:8], in_=DT1.ap[:, 0:8], func=AF.Exp), [DT1], [DTE])
            P.act(lambda e: e.activation(out=DTT.ap[:, 0:8], in_=DTE.ap[:, 0:8], func=AF.Ln, bias=self.col("one"), scale=1.0), [DTE, "cols"], [DTT])
            P.dve(lambda e: e.tensor_tensor(out=DA.ap[:, 0:8], in0=DTT.ap[:, 0:8], in1=self.ABC.ap, op=ALU.mult), [DTT, self.ABC], [DA])
            P.pe(lambda e: e.matmul(ps[4][:, 16:24], self.U, DA.ap[:, 0:8], start=True, stop=True), [DA, "cmat"], [("ps", 4)])
            P.pe(lambda e: e.matmul(ps[4][:, 24:32], self.ones, DA.ap[:, 0:8], start=True, stop=True), [DA, "cmat"], [("ps", 4)])
            P.act(lambda e: e.activation(out=ACS.ap, in_=ps[4][:, 16:32], func=AF.Identity), [("ps", 4)], [ACS])
            P.dve(lambda e: e.tensor_tensor(out=DIF.ap[:, 0:8], in0=ACS.ap[:, 8:16], in1=ACS.ap[:, 0:8], op=ALU.subtract), [ACS], [DIF])
            P.act(lambda e: e.activation(out=DS.ap[:, 0:8], in_=DIF.ap[:, 0:8], func=AF.Exp), [DIF], [DS])
            P.act(lambda e: e.activation(out=EA.ap[:, 0:8], in_=ACS.ap[:, 0:8], func=AF.Exp), [ACS], [EA])
            P.act(lambda e: e.activation(out=ECD.ap[:, 0:8], in_=ACS.ap[:, 8:16], func=AF.Exp), [ACS], [ECD])
            P.dve(lambda e: e.tensor_scalar(out=NACS.ap[:, 0:8], in0=ACS.ap[:, 0:8], scalar1=-1.0, scalar2=None, op0=ALU.mult), [ACS], [NACS])
            def trx(e, qs=qs):
                ins = None
                for i in range(4):
                    ins = e.transpose(out=ps[5][:, i * 128:(i + 1) * 128], in_=XC[i].ap[:, qs], identity=self.ident)
                return ins
            P.pe(trx, [XC[i].sl(q * 128, (q + 1) * 128) for i in range(4)], [("ps", 5)])
            P.act(lambda e: e.activation(out=XTOK.ap, in_=ps[5][:, :], func=AF.Identity), [("ps", 5)], [XTOK])
            P.dve(lambda e: e.tensor_tensor(out=v864(XDT.ap), in0=v864(XTOK.ap), in1=b864(DTT.ap[:, 0:8]), op=ALU.mult), [XTOK, DTT], [XDT])
            P.dve(lambda e: e.tensor_tensor(out=v864(XDEC.ap), in0=v864(XDT.ap), in1=b864(DS.ap[:, 0:8]), op=ALU.mult), [XDT, DS], [XDEC])
            P.dve(lambda e: e.tensor_tensor(out=v864(XSK.ap), in0=v864(XTOK.ap), in1=b864(self.row("dskip_%d" % l)), op=ALU.mult), [XTOK, "rows"], [XSK])
            def trb(e, qs=qs):
                ins = None
                for g in range(2):
                    ins = e.transpose(out=ps[6][:, g * 128:(g + 1) * 128], in_=XC[4 + g].ap[:, qs], identity=self.ident)
                return ins
            P.pe(trb, [XC[4].sl(q * 128, (q + 1) * 128), XC[5].sl(q * 128, (q + 1) * 128)], [("ps", 6)])
            P.act(lambda e: e.activation(out=BTOK.ap, in_=ps[6][:, 0:256], func=AF.Identity), [("ps", 6)], [BTOK])

            def sc(e, qs=qs):
                ins = None
                for g in range(2):
                    ins = e.matmul(ps[6][:, 256 + g * 128:256 + (g + 1) * 128], BF[g].ap[:, qs], CF[g].ap[:, qs], start=True, stop=True)
                return ins
            P.pe(sc, [BF[0].sl(q * 128, (q + 1) * 128), BF[1].sl(q * 128, (q + 1) * 128),
                      CF[0].sl(q * 128, (q + 1) * 128), CF[1].sl(q * 128, (q + 1) * 128)], [("ps", 6)])
            P.act(lambda e: e.activation(out=SCT.ap, in_=ps[6][:, 256:512], func=AF.Identity), [("ps", 6)], [SCT])
            P.dve(lambda e: e.tensor_copy(out=DAREP.ap.rearrange("p (h s) -> p h s", h=8),
                                          in_=DA.ap[:, 0:8].rearrange("p (h o) -> p h o", o=1).broadcast_to([128, 8, 128])), [DA], [DAREP])
            for h in range(8):
                bank = 2 + h // 4
                cs = slice((h % 4) * 128, (h % 4 + 1) * 128)

                def lmm(e, h=h, bank=bank, cs=cs):
                    e.matmul(ps[bank][:, cs], DAREP.ap[:, h * 128:(h + 1) * 128], self.U, start=True, stop=False)
                    return e.matmul(ps[bank][:, cs], self.ident, self.NEG, start=False, stop=True)
                P.pe(lmm, [DAREP, "cmat"], [("ps", bank)])
                P.act(lambda e, h=h, bank=bank, cs=cs: e.activation(out=LH.ap[:, h * 128:(h + 1) * 128], in_=ps[bank][:, cs], func=AF.Exp,
                                                                 bias=NACS.ap[:, h:h + 1], scale=1.0),
                      [("ps", bank), NACS], [LH.sl(h * 128, (h + 1) * 128)])
            for g in range(2):
                P.dve(lambda e, g=g: e.tensor_tensor(out=MH.ap[:, g * 512:(g + 1) * 512].rearrange("p (h t) -> p h t", h=4),
                                                    in0=LH.ap[:, g * 512:(g + 1) * 512].rearrange("p (h t) -> p h t", h=4),
                                                    in1=SCT.ap[:, g * 128:(g + 1) * 128].rearrange("p (o t) -> p o t", o=1).broadcast_to([128, 4, 128]),
                                                    op=ALU.mult),
                      [LH.sl(g * 512, (g + 1) * 512), SCT], [MH.sl(g * 512, (g + 1) * 512)])
            def ydiag(e):
                ins = None
                for h in range(8):
                    ins = e.matmul(ps[0][:, h * 64:(h + 1) * 64], MH.ap[:, h * 128:(h + 1) * 128], XDT.ap[:, h * 64:(h + 1) * 64], start=True, stop=True)
                return ins
            P.pe(ydiag, [MH, XDT], [("ps", 0)])

            def yoff(e, qs=qs):
                ins = None
                for g in range(2):
                    ins = e.matmul(ps[1][:, g * 256:(g + 1) * 256], CF[g].ap[:, qs], self.HSTb.ap[:, g * 256:(g + 1) * 256], start=True, stop=True)
                return ins
            P.pe(yoff, [CF[0].sl(q * 128, (q + 1) * 128), CF[1].sl(q * 128, (q + 1) * 128), self.HSTb], [("ps", 1)])

            def stt(e):
                ins = None
                for g in range(2):
                    ins = e.matmul(ps[7][:, g * 256:(g + 1) * 256], BTOK.ap[:, g * 128:(g + 1) * 128], XDEC.ap[:, g * 256:(g + 1) * 256], start=True, stop=True)
                return ins
            P.pe(stt, [BTOK, XDEC], [("ps", 7)])
            P.dve(lambda e: e.tensor_tensor(out=v864(YT.ap), in0=v864(ps[1][:, :]), in1=b864(EA.ap[:, 0:8]), op=ALU.mult), [("ps", 1), EA], [YT])
            P.dve(lambda e: e.tensor_tensor(out=YT.ap, in0=YT.ap, in1=ps[0][:, :], op=ALU.add), [YT, ("ps", 0)], [YT])
            P.dve(lambda e: e.tensor_tensor(out=YT.ap, in0=YT.ap, in1=XSK.ap, op=ALU.add), [YT, XSK], [YT])
            P.dve(lambda e: e.tensor_tensor(out=v864(HT.ap), in0=v864(self.HST.ap), in1=b864(ECD.ap[:, 0:8]), op=ALU.mult), [self.HST, ECD], [HT])
            P.dve(lambda e: e.tensor_tensor(out=self.HST.ap, in0=HT.ap, in1=ps[7][:, :], op=ALU.add), [HT, ("ps", 7)], [self.HST])
            P.act(lambda e: e.activation(out=self.HSTb.ap, in_=self.HST.ap, func=AF.Identity), [self.HST], [self.HSTb])
            def try_(e):
                ins = None
                for i in range(4):
                    ins = e.transpose(out=ps[5][:, i * 128:(i + 1) * 128], in_=YT.ap[:, i * 128:(i + 1) * 128], identity=self.ident)
                return ins
            P.pe(try_, [YT], [("ps", 5)])
            P.act(lambda e, qs=qs: e.activation(out=YF[:, :, qs], in_=ps[5][:, :].rearrange("p (i t) -> p i t", i=4), func=AF.Identity),
                  [("ps", 5)], [XC[i].sl(q * 128, (q + 1) * 128) for i in range(4)])
        for i in range(4):
            P.dve(lambda e, i=i: e.tensor_tensor(out=XC[i].ap, in0=XC[i].ap, in1=SZ[i].ap, op=ALU.mult), [XC[i], SZ[i]], [XC[i]])
        for g in range(2):
            for ii in range(2):
                i = 2 * g + ii
                P.act(lambda e, i=i: e.activation(out=SQg.ap, in_=XC[i].ap, func=AF.Square), [XC[i]], [SQg])
                for tt in range(2):
                    P.pe(lambda e, tt=tt, ii=ii: e.matmul(ps[4 + tt][:, :], self.ones, SQg.ap[:, T2[tt]], start=(ii == 0), stop=(ii == 1)),
                         [SQg, "cmat"], [("ps", 4 + tt)])
            for tt in range(2):
                rs = RS.sl(tt * NT, (tt + 1) * NT)
                P.act(lambda e, tt=tt, rs=rs: e.activation(out=rs.ap, in_=ps[4 + tt][:, :], func=AF.Sqrt, bias=self.col("eps"), scale=1.0 / 256),
                      [("ps", 4 + tt), "cols"], [rs])
                P.dve(lambda e, rs=rs: e.reciprocal(out=rs.ap, in_=rs.ap), [rs], [rs])
            for ii in range(2):
                i = 2 * g + ii
                P.dve(lambda e, i=i: e.scalar_tensor_tensor(out=Yc[12 + i].ap, in0=XC[i].ap, scalar=self.col("snorm_%d" % l, i), in1=RS.ap,
                                                           op0=ALU.mult, op1=ALU.mult),
                      [XC[i], RS, "cols"], [Yc[12 + i]])

    def finish(self):
        P = self.P
        keys = [("OUT", i) for i in range(DC)] + [("OUT", i, hf) for i in range(DC) for hf in range(self.NH)]
        P.pool(lambda e: e.memset(self.XS[0].ap[:, 0:1], 0.0), keys + [self.XS[0]], [self.XS[0]])

    def build(self):
        self.wslot = 0
        self.xslot = 0
        self.pbank = 0
        with contextlib.ExitStack() as stack:
            self.plan(stack)
            self.load_consts()
            self.copy_x_in()
            for st in self.stages:
                kind = st[0]
                if kind == "ffn":
                    self.ffn(st[1], st[2], st[3])
                elif kind == "setup":
                    self.layer_setup(st[1])
                elif kind == "mix":
                    self.mixer(st[1], st[2], *(st[3:]))
                elif kind == "final":
                    self.final_norm(st[1])
                elif kind == "dump":
                    self.dump()
            self.finish()
            self.P.emit(self.nc, stack)
        return self.nc


def run(inputs, stages, NH=8, ncores=2, trace=False):
    inp = {k: np.asarray(v) for k, v in inputs.items()}
    cp, rp = make_packs(inp)
    cols = cp.build()
    rows = rp.build()
    W = prep_weights(inp)
    b = Builder(stages, NH, cp, rp)
    nc = b.build()
    T = NH * HALF
    cm = make_cmat()
    in_maps = []
    for c in range(ncores):
        xin = np.ascontiguousarray(inp["x"][c, :T, :].T).reshape(DC, 128, T)
        m = {"xin": xin, "cols": cols, "rows": rows, "cmat": cm}
        m.update({k: v for k, v in W.items() if k in b.w})
        in_maps.append(m)
    res = run_bass_kernel_spmd(nc, in_maps, core_ids=list(range(ncores)), **({"trace": True} if trace else {}))
    if trace:
        print("EXEC_NS", res.exec_time_ns, flush=True)
    outs = [np.asarray(r["out"]).reshape(D, T).T for r in res.results]
    return np.ascontiguousarray(np.stack(outs)).astype(np.float32)


def full_stages(NH=8):
    st = []
    for l in range(L):
        st.append(("setup", l))
        for hf in range(NH):
            st.append(("ffn", l, "ffn1", hf))
            st.append(("mix", l, hf))
            st.append(("ffn", l, "ffn2", hf))
    for hf in range(NH):
        st.append(("final", hf))
    return st


def kernel(**inputs):
    return run(inputs, full_stages(8), NH=8, ncores=2)
```

```python
import contextlib
import numpy as np
import concourse.bass as bass
import concourse.mybir as mybir
from concourse.bass_utils import run_bass_kernel_spmd

F32 = mybir.dt.float32
BF16 = mybir.dt.bfloat16
AF = mybir.ActivationFunctionType
ALU = mybir.AluOpType

D = 2048
DC = 16
HALF = 1024
NT = 512
DFF = 5632
FC = 44
L = 4
EPS = 1e-6
SEM_LIMIT = 12000
SAME_ENGINE_SYNC = True
GR = 64
ARENA = 52800


class V:
    def __init__(self, ap, lo, hi, es):
        self.ap = ap
        self.lo = lo
        self.hi = hi
        self.es = es

    def sl(self, e0, e1):
        lo = self.lo + (e0 * self.es) // 4
        hi = self.lo + (e1 * self.es + 3) // 4
        return V(self.ap[:, e0:e1], lo, hi, self.es)


class Op:
    __slots__ = ("eng", "fn", "reads", "writes", "dma_key", "deps", "needs_inc", "sem", "val", "idx")

    def __init__(self, eng, fn, reads, writes, dma_key):
        self.eng = eng
        self.fn = fn
        self.reads = reads
        self.writes = writes
        self.dma_key = dma_key
        self.deps = ()
        self.needs_inc = dma_key is not None
        self.sem = None
        self.val = 0


def _keys(items):
    out = []
    for it in items:
        if isinstance(it, V):
            for g in range(it.lo // GR, (it.hi + GR - 1) // GR):
                out.append(g)
        elif isinstance(it, (list, tuple)) and it and isinstance(it[0], V):
            out.extend(_keys(it))
        else:
            out.append(it)
    return out


class Prog:
    ENGS = ("pe", "act", "dve", "pool", "sp")

    def __init__(self):
        self.ops = []

    def add(self, eng, fn, reads=(), writes=(), dma_key=None):
        op = Op(eng, fn, _keys(reads), _keys(writes), dma_key)
        self.ops.append(op)
        return op

    def pe(self, fn, reads=(), writes=()):
        return self.add("pe", fn, reads, writes)

    def act(self, fn, reads=(), writes=()):
        return self.add("act", fn, reads, writes)

    def dve(self, fn, reads=(), writes=()):
        return self.add("dve", fn, reads, writes)

    def pool(self, fn, reads=(), writes=()):
        return self.add("pool", fn, reads, writes)

    def dma(self, eng, out, in_, reads, writes, key):
        return self.add(eng, lambda e: e.dma_start(out=out, in_=in_), reads, writes, dma_key=key)

    def analyze(self):
        last_w = {}
        readers = {}
        last_dma = {}
        for idx, op in enumerate(self.ops):
            op.idx = idx
            deps = {}
            for r in op.reads:
                w = last_w.get(r)
                if w is not None:
                    deps[id(w)] = w
            for r in op.writes:
                w = last_w.get(r)
                if w is not None:
                    deps[id(w)] = w
                rl = readers.get(r)
                if rl:
                    for rd in rl:
                        deps[id(rd)] = rd
            if op.dma_key is not None:
                p = last_dma.get(op.dma_key)
                if p is not None:
                    deps[id(p)] = p
                last_dma[op.dma_key] = op
            deps.pop(id(op), None)
            dl = []
            for d in deps.values():
                if d.dma_key is None and d.eng == op.eng:
                    if op.eng == "pe" or not SAME_ENGINE_SYNC:
                        continue
                dl.append(d)
            op.deps = dl
            for d in dl:
                d.needs_inc = True
            for r in op.writes:
                last_w[r] = op
                readers[r] = None
            for r in op.reads:
                rl = readers.get(r)
                if rl is None:
                    readers[r] = [op]
                elif rl[-1] is not op:
                    rl.append(op)

    def emit(self, nc, stack):
        self.analyze()
        cur = {}
        cnt = {}
        dsem = {}
        dcnt = {}
        nsem = [0]

        def newsem():
            nsem[0] += 1
            return stack.enter_context(nc.semaphore("s%d" % nsem[0]))

        for op in self.ops:
            if op.dma_key is not None:
                k = op.dma_key
                if k not in dsem or dcnt[k] + 16 > SEM_LIMIT:
                    dsem[k] = newsem()
                    dcnt[k] = 0
                dcnt[k] += 16
                op.sem = dsem[k]
                op.val = dcnt[k]
            elif op.needs_inc:
                e = op.eng
                if e not in cur or cnt[e] + 1 > SEM_LIMIT:
                    cur[e] = newsem()
                    cnt[e] = 0
                cnt[e] += 1
                op.sem = cur[e]
                op.val = cnt[e]
        self.nsem = nsem[0]
        per_eng = {e: [o for o in self.ops if o.eng == e] for e in self.ENGS}

        def run_engine(eng_obj, ops):
            waited = {}
            for op in ops:
                for d in op.deps:
                    key = id(d.sem)
                    if waited.get(key, 0) >= d.val:
                        continue
                    eng_obj.wait_ge(d.sem, d.val)
                    waited[key] = d.val
                ins = op.fn(eng_obj)
                if op.sem is not None:
                    ins.then_inc(op.sem, 16 if op.dma_key is not None else 1)

        with nc.Block() as block:
            @block.tensor
            def _(e):
                run_engine(e, per_eng["pe"])

            @block.scalar
            def _(e):
                run_engine(e, per_eng["act"])

            @block.vector
            def _(e):
                run_engine(e, per_eng["dve"])

            @block.gpsimd
            def _(e):
                run_engine(e, per_eng["pool"])

            @block.sync
            def _(e):
                run_engine(e, per_eng["sp"])


class Arena:
    def __init__(self, big):
        self.big = big

    def f32(self, off, n):
        assert off % GR == 0 and off + n <= ARENA, (off, n)
        return V(self.big[:, off:off + n], off, off + n, 4)

    def bf16(self, off, n):
        assert off % GR == 0 and n % 2 == 0 and off + n // 2 <= ARENA, (off, n)
        return V(self.big[:, off:off + n // 2].bitcast(BF16), off, off + n // 2, 2)


class Bump:
    def __init__(self, arena, lo, hi):
        self.a = arena
        self.lo = lo
        self.hi = hi
        self.o = lo

    def reset(self, to=None):
        self.o = self.lo if to is None else to

    def f32(self, n):
        v = self.a.f32(self.o, n)
        self.o += (n + GR - 1) // GR * GR
        assert self.o <= self.hi, (self.o, self.hi)
        return v

    def bf16(self, n):
        v = self.a.bf16(self.o, n)
        self.o += (n // 2 + GR - 1) // GR * GR
        assert self.o <= self.hi, (self.o, self.hi)
        return v


class ColPack:
    def __init__(self):
        self.cols = []
        self.index = {}
        self.n = 0

    def add(self, name, arr):
        arr = np.ascontiguousarray(arr, dtype=np.float32).reshape(128, -1)
        self.index[name] = (self.n, arr.shape[1])
        self.cols.append(arr)
        self.n += arr.shape[1]

    def build(self):
        return np.ascontiguousarray(np.concatenate(self.cols, axis=1))


def chunk_cols(v):
    v = np.asarray(v, dtype=np.float32)
    return np.ascontiguousarray(v.reshape(-1, 128).T)


def rep_rows(v):
    v = np.asarray(v, dtype=np.float32).reshape(1, -1)
    return np.ascontiguousarray(np.repeat(v, 128, axis=0))


def make_packs(inp):
    cp = ColPack()
    cp.add("eps", np.full((128, 1), EPS, np.float32))
    cp.add("one", np.full((128, 1), 1.0, np.float32))
    cp.add("final_norm", chunk_cols(inp["final_norm"]))
    for l in range(L):
        for nm in ("ffn1_norm", "ffn2_norm", "mix_norm"):
            cp.add("%s_%d" % (nm, l), chunk_cols(inp[nm][l]))
        cp.add("pool_scale_%d" % l, chunk_cols(inp["pool_scale"][l]))
        w = np.asarray(inp["conv_dw_w"][l])
        cp.add("cdw_w_%d" % l, w.reshape(31, 4, 128).transpose(2, 1, 0).reshape(128, 124))
        cp.add("cdw_b_%d" % l, chunk_cols(inp["conv_dw_b"][l]))
        cp.add("cln_g_%d" % l, chunk_cols(inp["conv_ln_g"][l]))
        cp.add("cln_b_%d" % l, chunk_cols(inp["conv_ln_b"][l]))
        cp.add("cpw_b_%d" % l, chunk_cols(inp["conv_pw_b"][l]))
        cp.add("sln_g_%d" % l, chunk_cols(inp["sgu_ln_g"][l]))
        cp.add("sln_b_%d" % l, chunk_cols(inp["sgu_ln_b"][l]))
        w = np.asarray(inp["ssm_conv_w"][l])
        cp.add("scw_%d" % l, w.reshape(4, 8, 128).transpose(2, 1, 0).reshape(128, 32))
        cp.add("scb_%d" % l, chunk_cols(inp["ssm_conv_b"][l]))
        cp.add("snorm_%d" % l, chunk_cols(inp["ssm_norm"][l]))
    rp = ColPack()
    k = np.arange(16)
    invc = np.stack([1.0 / np.minimum(k + 1, w) for w in (2, 4, 8, 16)]).astype(np.float32)
    rp.add("invc", rep_rows(invc))
    for l in range(L):
        rp.add("dtb_%d" % l, rep_rows(inp["ssm_dt_bias"][l]))
        rp.add("alog_%d" % l, rep_rows(inp["ssm_a_log"][l]))
        rp.add("dskip_%d" % l, rep_rows(inp["ssm_d"][l]))
        rp.add("sgub_%d" % l, rep_rows(inp["sgu_b"][l]))
    return cp, rp


LW_POOL, LW_PW, LW_SGU, LW_DT = 0, 512, 2560, 3072
LW_N = 3200


def prep_weights(inp):
    out = {}
    for nm in ("ffn1", "ffn2"):
        wg = np.asarray(inp[nm + "_w_gate"]).reshape(L, DC, 128, FC, 128)
        wu = np.asarray(inp[nm + "_w_up"]).reshape(L, DC, 128, FC, 128)
        wgu = np.empty((L, FC, 128, 2, DC, 128), np.float32)
        wgu[:, :, :, 0] = wg.transpose(0, 3, 2, 1, 4)
        wgu[:, :, :, 1] = wu.transpose(0, 3, 2, 1, 4)
        out[nm + "_wgu"] = wgu.reshape(L, FC, 128, 2 * DC * 128)
        wd = np.asarray(inp[nm + "_w_down"]).reshape(L, 2, 22, 128, DC, 128)
        out[nm + "_wd"] = np.ascontiguousarray(wd.transpose(0, 4, 1, 3, 2, 5)).reshape(L, DC, 2, 128, 22 * 128)
    win = np.asarray(inp["w_in"])
    w32 = win[:, :, :4096].reshape(L, DC, 128, 32, 128)
    out["w_in"] = np.ascontiguousarray(w32.transpose(0, 3, 2, 1, 4)).reshape(L, 32, 128, DC * 128)
    wo = np.asarray(inp["w_out"]).reshape(L, DC, 128, DC, 128)
    out["w_out"] = np.ascontiguousarray(wo.transpose(0, 3, 2, 1, 4)).reshape(L, DC, 128, DC * 128)
    lw = np.empty((L, 128, LW_N), np.float32)
    lw[:, :, LW_POOL:LW_POOL + 512] = np.asarray(inp["pool_w"]).transpose(0, 2, 1, 3).reshape(L, 128, 512)
    lw[:, :, LW_PW:LW_PW + 2048] = np.asarray(inp["conv_pw_w"]).reshape(L, 4, 128, 512).transpose(0, 2, 1, 3).reshape(L, 128, 2048)
    lw[:, :, LW_SGU:LW_SGU + 512] = np.asarray(inp["sgu_w_s"]).transpose(0, 3, 1, 2).reshape(L, 128, 512)
    lw[:, :, LW_DT:LW_DT + 128] = win[:, :, 4096:4104].reshape(L, DC, 128, 8).transpose(0, 2, 1, 3).reshape(L, 128, 128)
    out["lw"] = lw
    return out


def make_cmat():
    c = np.zeros((128, 512), np.float32)
    c[:, 0:128] = 1.0
    c[:, 128:256] = np.eye(128, dtype=np.float32)
    k = np.arange(128)
    c[:, 256:384] = (k[:, None] <= k[None, :]).astype(np.float32)
    c[:, 384:512] = np.where(k[:, None] > k[None, :], -30000.0, 0.0)
    return c


class Builder:
    def __init__(self, stages, NH, cp, rp):
        self.stages = stages
        self.NH = NH
        self.T = NH * HALF
        self.ci = cp.index
        self.ri = rp.index
        self.ncp = (cp.n + GR - 1) // GR * GR
        self.nrp = (rp.n + GR - 1) // GR * GR
        nc = bass.Bass("TRN2", target_bir_lowering=False)
        self.nc = nc
        self.P = Prog()
        dt = nc.dram_tensor
        T = self.T
        self.xin = dt("xin", [DC, 128, T], F32, kind="ExternalInput").ap()
        self.out = dt("out", [DC, 128, T], F32, kind="ExternalOutput").ap()
        self.xt = dt("xt", [DC, 128, T], F32).ap()
        self.cols_d = dt("cols", [128, cp.n], F32, kind="ExternalInput").ap()
        self.rows_d = dt("rows", [128, rp.n], F32, kind="ExternalInput").ap()
        self.cmat_d = dt("cmat", [128, 512], F32, kind="ExternalInput").ap()
        self.w = {}
        for nm in ("ffn1", "ffn2"):
            if not any(st[0] == "ffn" and st[2] == nm for st in stages):
                continue
            self.w[nm + "_wgu"] = dt(nm + "_wgu", [L, FC, 128, 2 * DC * 128], F32, kind="ExternalInput").ap()
            self.w[nm + "_wd"] = dt(nm + "_wd", [L, DC, 2, 128, 22 * 128], F32, kind="ExternalInput").ap()
        self.w["w_in"] = dt("w_in", [L, 32, 128, DC * 128], F32, kind="ExternalInput").ap()
        self.w["w_out"] = dt("w_out", [L, DC, 128, DC * 128], F32, kind="ExternalInput").ap()
        self.w["lw"] = dt("lw", [L, 128, LW_N], F32, kind="ExternalInput").ap()

    def plan(self, stack):
        nc = self.nc
        big = stack.enter_context(nc.sbuf_tensor("arena", [128, ARENA], F32))
        A = Arena(big)
        self.A = A
        pb = Bump(A, 0, ARENA)
        self.cols = pb.f32(self.ncp)
        self.rows = pb.f32(self.nrp)
        self.cmat = pb.f32(512)
        c = self.cmat.ap
        self.ones, self.ident, self.U, self.NEG = c[:, 0:128], c[:, 128:256], c[:, 256:384], c[:, 384:512]
        self.LWB = pb.bf16(LW_N)
        self.HALO_A = [pb.f32(16) for _ in range(4)]
        self.HALO_B = [pb.f32(32) for _ in range(4)]
        self.HALO_D = [pb.f32(4) for _ in range(8)]
        self.HST = pb.f32(512)
        self.HSTb = pb.bf16(512)
        self.ABC = pb.f32(8)
        self.ONESB = pb.bf16(128)
        self.HALO_Bb = [pb.bf16(32) for _ in range(4)]
        self.H = pb.bf16(DC * HALF)
        self.Hc = [self.H.sl(i * HALF, (i + 1) * HALF) for i in range(DC)]
        rlo = pb.o
        self.ACTb = pb.bf16(FC * HALF)
        self.R = Bump(A, rlo, pb.o)
        self.XH = A.f32(rlo, DC * HALF)
        self.XHc = [self.XH.sl(i * HALF, (i + 1) * HALF) for i in range(DC)]
        self.SQ = [A.bf16(rlo + DC * HALF + k * HALF, HALF) for k in range(2)]
        self.RSTD = A.f32(rlo + DC * HALF + 2 * HALF, HALF)
        self.STMP = A.f32(rlo + DC * HALF + 3 * HALF, HALF)
        self.WST = [pb.f32(4096) for _ in range(2)]
        self.WBF = [pb.bf16(4096) for _ in range(2)]
        self.XS = [pb.f32(NT) for _ in range(3)]
        self.SG = [pb.bf16(NT) for _ in range(2)]
        self.ps = [stack.enter_context(nc.psum_tensor("ps%d" % b, [128, 512], F32)) for b in range(8)]

    def col(self, name, k=0, n=1):
        off, w = self.ci[name]
        assert k + n <= w, (name, k, n, w)
        return self.cols.ap[:, off + k: off + k + n]

    def row(self, name, k=0, n=None):
        off, w = self.ri[name]
        n = w - k if n is None else n
        assert k + n <= w
        return self.rows.ap[:, off + k: off + k + n]

    def load_consts(self):
        P = self.P
        P.dma("sp", self.cols.ap[:, 0:self.cols_d.shape[1]], self.cols_d, [], [self.cols, "cols"], "c0")
        P.dma("sp", self.rows.ap[:, 0:self.rows_d.shape[1]], self.rows_d, [], [self.rows, "rows"], "c1")
        P.dma("sp", self.cmat.ap, self.cmat_d, [], [self.cmat, "cmat"], "c2")
        P.dve(lambda e: e.tensor_copy(out=self.ONESB.ap, in_=self.ones), ["cmat"], [self.ONESB, "onesb"])

    def copy_x_in(self):
        P = self.P
        for i in range(DC):
            P.dma("pool", self.xt[i], self.xin[i], [], [("XT", i, gt) for gt in range(2 * self.NH)], ("xcp", i % 4))

    def dump(self):
        P = self.P
        for i in range(DC):
            P.dma("pool", self.out[i], self.xt[i], [("XT", i, gt) for gt in range(2 * self.NH)], [("OUT", i)], ("xcp", i % 4))

    def rms_stats(self, hf):
        P = self.P
        t0 = hf * HALF
        for i in range(DC):
            P.dma("pool", self.XHc[i].ap, self.xt[i, :, t0:t0 + HALF],
                  [("XT", i, 2 * hf), ("XT", i, 2 * hf + 1)], [self.XHc[i]], ("xh", i % 4))
        for i in range(DC):
            sq = self.SQ[i % 2]
            P.act(lambda e, i=i, sq=sq: e.activation(out=sq.ap, in_=self.XHc[i].ap, func=AF.Square),
                  [self.XHc[i]], [sq])
            for tt in range(2):
                P.pe(lambda e, i=i, sq=sq, tt=tt: e.matmul(self.ps[tt][:, :], self.ONESB.ap, sq.ap[:, tt * NT:(tt + 1) * NT],
                                                      start=(i == 0), stop=(i == DC - 1)),
                     [sq, "onesb"], [("ps", tt)])
        for tt in range(2):
            st = self.STMP.sl(tt * NT, (tt + 1) * NT)
            rs = self.RSTD.sl(tt * NT, (tt + 1) * NT)
            P.act(lambda e, tt=tt, st=st: e.activation(out=st.ap, in_=self.ps[tt][:, :], func=AF.Sqrt,
                                                      bias=self.col("eps"), scale=1.0 / D),
                  [("ps", tt), "cols"], [st])
            P.dve(lambda e, st=st, rs=rs: e.reciprocal(out=rs.ap, in_=st.ap), [st], [rs])

    def norm_pass(self, gname, hf):
        P = self.P
        self.rms_stats(hf)
        for i in range(DC):
            P.dve(lambda e, i=i: e.scalar_tensor_tensor(out=self.Hc[i].ap, in0=self.XHc[i].ap,
                                                       scalar=self.col(gname, i), in1=self.RSTD.ap,
                                                       op0=ALU.mult, op1=ALU.mult),
                  [self.XHc[i], self.RSTD, "cols"], [self.Hc[i]])

    def final_norm(self, hf):
        P = self.P
        t0 = hf * HALF
        self.rms_stats(hf)
        for i in range(DC):
            P.dve(lambda e, i=i: e.scalar_tensor_tensor(out=self.XHc[i].ap, in0=self.XHc[i].ap,
                                                       scalar=self.col("final_norm", i), in1=self.RSTD.ap,
                                                       op0=ALU.mult, op1=ALU.mult),
                  [self.XHc[i], self.RSTD, "cols"], [self.XHc[i]])
            P.dma("pool", self.out[i, :, t0:t0 + HALF], self.XHc[i].ap, [self.XHc[i]], [("OUT", i, hf)], ("xh", i % 4))

    def residual_update(self, i, gt, bank, factor):
        P = self.P
        k = self.xslot
        self.xslot = (self.xslot + 1) % 3
        xs = self.XS[k]
        dr = self.xt[i, :, gt * NT:(gt + 1) * NT]
        P.dma("pool", xs.ap, dr, [("XT", i, gt)], [xs], ("xs", k))
        P.dve(lambda e, xs=xs, bank=bank: e.scalar_tensor_tensor(out=xs.ap, in0=self.ps[bank][:, :], scalar=float(factor),
                                                               in1=xs.ap, op0=ALU.mult, op1=ALU.add),
              [("ps", bank), xs], [xs])
        P.dma("pool", dr, xs.ap, [xs], [("XT", i, gt)], ("xs", k))

    def stream_w(self, src, ncols, split=(0.5, 0.75)):
        P = self.P
        s = self.wslot
        self.wslot ^= 1
        st = self.WST[s].sl(0, ncols)
        wb = self.WBF[s].sl(0, ncols)
        P.dma("sp", st.ap, src, [], [st], ("wst", s))
        a = int(ncols * split[0]) // 128 * 128
        b = int(ncols * split[1]) // 128 * 128
        if a > 0:
            P.act(lambda e: e.activation(out=wb.ap[:, 0:a], in_=st.ap[:, 0:a], func=AF.Identity), [st.sl(0, a)], [wb.sl(0, a)])
        if b > a:
            P.pool(lambda e: e.tensor_copy(out=wb.ap[:, a:b], in_=st.ap[:, a:b]), [st.sl(a, b)], [wb.sl(a, b)])
        if ncols > b:
            P.dve(lambda e: e.tensor_copy(out=wb.ap[:, b:ncols], in_=st.ap[:, b:ncols]), [st.sl(b, ncols)], [wb.sl(b, ncols)])
        return wb

    def ffn(self, l, nm, hf):
        P = self.P
        wgu = self.w[nm + "_wgu"]
        wd = self.w[nm + "_wd"]
        self.norm_pass("%s_norm_%d" % (nm, l), hf)
        ACTc = [self.ACTb.sl(j * HALF, (j + 1) * HALF) for j in range(FC)]
        tasks = [("gu", j, 0) for j in range(FC)] + [("dn", i, fh) for i in range(DC) for fh in range(2)]

        def fetch(t):
            kind, a, b = tasks[t]
            if kind == "gu":
                return self.stream_w(wgu[l, a], 4096)
            return self.stream_w(wd[l, a, b], 2816, split=(0.5, 0.5))

        nxt = fetch(0)
        for t, (kind, a, b) in enumerate(tasks):
            wb = nxt
            if t + 1 < len(tasks):
                nxt = fetch(t + 1)
            if kind == "gu":
                j = a
                wv = wb.ap.rearrange("p (g c f) -> p g c f", g=2, c=DC)
                for tt in range(2):
                    bg = (j % 2) * 4 + tt
                    bu = (j % 2) * 4 + 2 + tt
                    sl = slice(tt * NT, (tt + 1) * NT)

                    def mm_chain(e, g, bank, wv=wv, sl=sl):
                        ins = None
                        for c in range(DC):
                            ins = e.matmul(self.ps[bank][:, :], wv[:, g, c, :], self.Hc[c].ap[:, sl],
                                           start=(c == 0), stop=(c == DC - 1))
                        return ins
                    P.pe(lambda e, f=mm_chain, b_=bg: f(e, 0, b_), [wb.sl(0, 2048), self.H], [("ps", bg)])
                    P.pe(lambda e, f=mm_chain, b_=bu: f(e, 1, b_), [wb.sl(2048, 4096), self.H], [("ps", bu)])
                    sg = self.SG[tt]
                    P.act(lambda e, sg=sg, bg=bg: e.activation(out=sg.ap, in_=self.ps[bg][:, :], func=AF.Silu),
                          [("ps", bg)], [sg])
                    dst = ACTc[j].sl(tt * NT, (tt + 1) * NT)
                    P.dve(lambda e, sg=sg, bu=bu, dst=dst: e.tensor_tensor(out=dst.ap, in0=self.ps[bu][:, :], in1=sg.ap, op=ALU.mult),
                          [("ps", bu), sg], [dst])
            else:
                i, fh = a, b
                wv = wb.ap.rearrange("p (c f) -> p c f", c=22)
                for tt in range(2):
                    bank = (i % 2) * 2 + tt
                    sl = slice(tt * NT, (tt + 1) * NT)

                    def mm_chain(e, wv=wv, sl=sl, bank=bank, fh=fh):
                        ins = None
                        for c in range(22):
                            ins = e.matmul(self.ps[bank][:, :], wv[:, c, :], ACTc[fh * 22 + c].ap[:, sl],
                                           start=(fh == 0 and c == 0), stop=(fh == 1 and c == 21))
                        return ins
                    P.pe(mm_chain, [wb] + [ACTc[fh * 22 + c].sl(tt * NT, (tt + 1) * NT) for c in range(22)], [("ps", bank)])
                if fh == 1:
                    for tt in range(2):
                        self.residual_update(i, 2 * hf + tt, (i % 2) * 2 + tt, 0.5)

    def layer_setup(self, l):
        P = self.P
        s = self.wslot
        self.wslot ^= 1
        st = self.WST[s].sl(0, LW_N)
        P.dma("sp", st.ap, self.w["lw"][l], [], [st], ("wst", s))
        lw = self.LWB
        for (a, b) in ((LW_POOL, LW_PW + 2048), (LW_DT, LW_N)):
            P.dve(lambda e, a=a, b=b: e.tensor_copy(out=lw.ap[:, a:b], in_=st.ap[:, a:b]), [st], [lw.sl(a, b)])
        P.dve(lambda e: e.tensor_tensor(out=lw.ap[:, LW_SGU:LW_SGU + 512].rearrange("p (h t) -> p h t", h=4),
                                        in0=st.ap[:, LW_SGU:LW_SGU + 512].rearrange("p (h t) -> p h t", h=4),
                                        in1=self.U.rearrange("p (o t) -> p o t", o=1).broadcast_to([128, 4, 128]),
                                        op=ALU.mult),
              [st, "cmat"], [lw.sl(LW_SGU, LW_SGU + 512)])
        P.act(lambda e: e.activation(out=self.ABC.ap, in_=self.row("alog_%d" % l), func=AF.Exp), ["rows"], [self.ABC])
        P.dve(lambda e: e.tensor_scalar(out=self.ABC.ap, in0=self.ABC.ap, scalar1=-1.0, scalar2=None, op0=ALU.mult),
              [self.ABC], [self.ABC])
        P.dve(lambda e: e.memset(self.HST.ap, 0.0), [], [self.HST])
        P.dve(lambda e: e.memset(self.HSTb.ap, 0.0), [], [self.HSTb])

    def start_wseq(self, l, groups):
        order = []
        if "a" in groups:
            order += [0, 1, 2, 3]
        if "b" in groups:
            for cc in range(4):
                order += [4 + cc, 8 + cc]
        if "c" in groups:
            order += list(range(12, 20))
        if "d" in groups:
            order += list(range(20, 32))
        self.wseq = [("in", c) for c in order] + [("out", i) for i in range(DC)]
        self.wl = l
        self.wpos = 0
        self.wpending = self._fetch_w(0)

    def _fetch_w(self, pos):
        kind, c = self.wseq[pos]
        if kind == "in":
            return self.stream_w(self.w["w_in"][self.wl, c], 2048, split=(0.375, 1.0))
        return self.stream_w(self.w["w_out"][self.wl, c], 2048, split=(0.75, 0.75))

    def next_w(self, tag):
        assert self.wseq[self.wpos] == tag, (self.wseq[self.wpos], tag)
        wb = self.wpending
        self.wpos += 1
        if self.wpos < len(self.wseq):
            self.wpending = self._fetch_w(self.wpos)
        return wb

    def proj(self, l, c):
        P = self.P
        wb = self.next_w(("in", c))
        wv = wb.ap.rearrange("p (c f) -> p c f", c=DC)
        pb = self.pbank
        self.pbank ^= 1
        banks = (pb * 2, pb * 2 + 1)
        for tt in range(2):
            sl = slice(tt * NT, (tt + 1) * NT)

            def mm_chain(e, wv=wv, sl=sl, bank=banks[tt]):
                ins = None
                for c2 in range(DC):
                    ins = e.matmul(self.ps[bank][:, :], wv[:, c2, :], self.Hc[c2].ap[:, sl], start=(c2 == 0), stop=(c2 == DC - 1))
                return ins
            P.pe(mm_chain, [wb, self.H], [("ps", banks[tt])])
        return banks

    def ln_stats(self, X, SQt, MEAN, RSTD, TMP):
        P = self.P
        for cc in range(4):
            P.act(lambda e, cc=cc: e.activation(out=SQt.ap, in_=X[cc].ap, func=AF.Square), [X[cc]], [SQt])
            for tt in range(2):
                sl = slice(tt * NT, (tt + 1) * NT)
                P.pe(lambda e, cc=cc, tt=tt, sl=sl: e.matmul(self.ps[4 + tt][:, :], self.ones, X[cc].ap[:, sl], start=(cc == 0), stop=(cc == 3)),
                     [X[cc], "cmat"], [("ps", 4 + tt)])
                P.pe(lambda e, cc=cc, tt=tt, sl=sl: e.matmul(self.ps[6 + tt][:, :], self.ones, SQt.ap[:, sl], start=(cc == 0), stop=(cc == 3)),
                     [SQt, "cmat"], [("ps", 6 + tt)])
        for tt in range(2):
            a, b = tt * NT, (tt + 1) * NT
            me, rs, tm = MEAN.sl(a, b), RSTD.sl(a, b), TMP.sl(a, b)
            P.dve(lambda e, tt=tt, me=me: e.tensor_scalar(out=me.ap, in0=self.ps[4 + tt][:, :], scalar1=1.0 / 512, scalar2=None, op0=ALU.mult),
                  [("ps", 4 + tt)], [me])
            P.dve(lambda e, me=me, tm=tm: e.tensor_tensor(out=tm.ap, in0=me.ap, in1=me.ap, op=ALU.mult), [me], [tm])
            P.dve(lambda e, tt=tt, tm=tm, rs=rs: e.scalar_tensor_tensor(out=rs.ap, in0=self.ps[6 + tt][:, :], scalar=1.0 / 512, in1=tm.ap,
                                                                    op0=ALU.mult, op1=ALU.subtract),
                  [("ps", 6 + tt), tm], [rs])
            P.act(lambda e, rs=rs: e.activation(out=rs.ap, in_=rs.ap, func=AF.Sqrt, bias=self.col("eps"), scale=1.0), [rs, "cols"], [rs])
            P.dve(lambda e, rs=rs: e.reciprocal(out=rs.ap, in_=rs.ap), [rs], [rs])

    def ln_apply(self, x, MEAN, RSTD, TMP, out, func, gcol, bcol):
        P = self.P
        P.dve(lambda e: e.tensor_tensor(out=TMP.ap, in0=x.ap, in1=MEAN.ap, op=ALU.subtract), [x, MEAN], [TMP])
        P.dve(lambda e: e.tensor_tensor(out=TMP.ap, in0=TMP.ap, in1=RSTD.ap, op=ALU.mult), [TMP, RSTD], [TMP])
        P.act(lambda e: e.activation(out=out.ap, in_=TMP.ap, func=func, bias=bcol, scale=gcol), [TMP, "cols"], [out])

    def mixer(self, l, hf, groups="abcd"):
        P = self.P
        A = self.A
        self.norm_pass("mix_norm_%d" % l, hf)
        self.start_wseq(l, groups)
        R = self.R
        R.reset()
        Y = R.bf16(DC * HALF)
        Yc = [Y.sl(c * HALF, (c + 1) * HALF) for c in range(DC)]
        base = R.o
        lw = self.LWB
        POOLW = lw.ap[:, LW_POOL:LW_POOL + 512].rearrange("p (g d) -> p g d", g=4)
        PWW = lw.ap[:, LW_PW:LW_PW + 2048].rearrange("p (c d) -> p c d", c=4)
        SGW = lw.ap[:, LW_SGU:LW_SGU + 512].rearrange("p (h t) -> p h t", h=4)
        WDT = lw.ap[:, LW_DT:LW_DT + 128].rearrange("p (c m) -> p c m", c=DC)
        T2 = (slice(0, NT), slice(NT, 2 * NT))

        def zero_y(c0, c1):
            for c in range(c0, c1):
                P.dve(lambda e, c=c: e.memset(Yc[c].ap, 0.0), [], [Yc[c]])

        if "a" in groups:
            AB = [R.f32(1040) for _ in range(4)]
            T = [R.f32(1040), R.f32(1040)]
            PB = [R.bf16(HALF) for _ in range(4)]
            TM = R.f32(16)
            for g in range(4):
                banks = self.proj(l, g)
                ab = AB[g]
                if hf == 0:
                    P.dve(lambda e, ab=ab: e.memset(ab.ap[:, 0:16], 0.0), [], [ab.sl(0, 16)])
                else:
                    P.dve(lambda e, ab=ab, g=g: e.tensor_copy(out=ab.ap[:, 0:16], in_=self.HALO_A[g].ap), [self.HALO_A[g]], [ab.sl(0, 16)])
                for tt in range(2):
                    d = ab.sl(16 + tt * NT, 16 + (tt + 1) * NT)
                    P.act(lambda e, d=d, b=banks[tt]: e.activation(out=d.ap, in_=self.ps[b][:, :], func=AF.Identity), [("ps", banks[tt])], [d])
                P.dve(lambda e, ab=ab, g=g: e.tensor_copy(out=self.HALO_A[g].ap, in_=ab.ap[:, 1024:1040]), [ab.sl(1024, 1040)], [self.HALO_A[g]])
                cur = ab
                for lev in range(g + 1):
                    k = 1 << lev
                    s0 = 2 * k - 1
                    dst = T[lev % 2]
                    P.dve(lambda e, cur=cur, dst=dst, k=k, s0=s0: e.tensor_tensor(out=dst.ap[:, s0:1040], in0=cur.ap[:, s0:1040],
                                                                               in1=cur.ap[:, s0 - k:1040 - k], op=ALU.add),
                          [cur], [dst])
                    cur = dst
                wdt = float(2 << g)
                P.dve(lambda e, cur=cur, ab=ab, g=g, wdt=wdt: e.scalar_tensor_tensor(out=PB[g].ap, in0=cur.ap[:, 16:1040], scalar=1.0 / wdt,
                                                                                  in1=ab.ap[:, 16:1040], op0=ALU.mult, op1=ALU.subtract),
                      [cur, ab], [PB[g]])
                if hf == 0:
                    P.dve(lambda e, cur=cur, g=g: e.tensor_tensor(out=TM.ap, in0=cur.ap[:, 16:32], in1=self.row("invc", g * 16, 16), op=ALU.mult),
                          [cur, "rows"], [TM])
                    P.dve(lambda e, ab=ab, g=g: e.tensor_tensor(out=PB[g].ap[:, 0:16], in0=TM.ap, in1=ab.ap[:, 16:32], op=ALU.subtract),
                          [TM, ab], [PB[g]])
                for tt in range(2):
                    P.pe(lambda e, g=g, tt=tt: e.matmul(self.ps[4 + tt][:, :], POOLW[:, g, :], PB[g].ap[:, T2[tt]], start=True, stop=True),
                         [PB[g], lw], [("ps", 4 + tt)])
                    d = Yc[g].sl(tt * NT, (tt + 1) * NT)
                    P.dve(lambda e, d=d, g=g, tt=tt: e.tensor_scalar(out=d.ap, in0=self.ps[4 + tt][:, :], scalar1=self.col("pool_scale_%d" % l, g),
                                                                  scalar2=None, op0=ALU.mult),
                          [("ps", 4 + tt), "cols"], [d])
        else:
            zero_y(0, 4)

        if "b" in groups:
            R.reset(base)
            HG = [R.bf16(1056) for _ in range(4)]
            CV = [R.f32(HALF) for _ in range(4)]
            SQt, MEAN, RSTD, TMP = R.f32(HALF), R.f32(HALF), R.f32(HALF), R.f32(HALF)
            SIGT = R.f32(NT)
            DG = [R.bf16(128) for _ in range(6)]
            LNS = [R.bf16(HALF) for _ in range(4)]
            ndg = 0
            for cc in range(4):
                bv = self.proj(l, 4 + cc)
                bgt = self.proj(l, 8 + cc)
                hg = HG[cc]
                if hf == 0:
                    P.dve(lambda e, hg=hg: e.memset(hg.ap[:, 0:32], 0.0), [], [hg.sl(0, 32)])
                else:
                    P.dve(lambda e, hg=hg, cc=cc: e.tensor_copy(out=hg.ap[:, 0:32], in_=self.HALO_Bb[cc].ap), [self.HALO_Bb[cc]], [hg.sl(0, 32)])
                for tt in range(2):
                    P.act(lambda e, b=bgt[tt]: e.activation(out=SIGT.ap, in_=self.ps[b][:, :], func=AF.Sigmoid), [("ps", bgt[tt])], [SIGT])
                    d = hg.sl(32 + tt * NT, 32 + (tt + 1) * NT)
                    P.dve(lambda e, d=d, b=bv[tt]: e.tensor_tensor(out=d.ap, in0=self.ps[b][:, :], in1=SIGT.ap, op=ALU.mult),
                          [("ps", bv[tt]), SIGT], [d])
                P.dve(lambda e, hg=hg, cc=cc: e.tensor_copy(out=self.HALO_Bb[cc].ap, in_=hg.ap[:, 1024:1056]), [hg.sl(1024, 1056)], [self.HALO_Bb[cc]])
                cb = (4 + (cc % 2) * 2, 5 + (cc % 2) * 2)
                for k in range(31):
                    dg = DG[ndg % 6]
                    ndg += 1
                    P.dve(lambda e, dg=dg, cc=cc, k=k: e.tensor_scalar(out=dg.ap, in0=self.ident, scalar1=self.col("cdw_w_%d" % l, cc * 31 + k),
                                                                   scalar2=None, op0=ALU.mult),
                          ["cmat", "cols"], [dg])

                    def tap(e, dg=dg, hg=hg, k=k, cb=cb):
                        ins = None
                        for tt in range(2):
                            ins = e.matmul(self.ps[cb[tt]][:, :], dg.ap, hg.ap[:, 2 + k + tt * NT:2 + k + (tt + 1) * NT], start=(k == 0), stop=(k == 30))
                        return ins
                    P.pe(tap, [dg, hg], [("ps", cb[0]), ("ps", cb[1])])
                cv = CV[cc]
                for tt in range(2):
                    d = cv.sl(tt * NT, (tt + 1) * NT)
                    P.act(lambda e, d=d, b=cb[tt], cc=cc: e.activation(out=d.ap, in_=self.ps[b][:, :], func=AF.Identity,
                                                                    bias=self.col("cdw_b_%d" % l, cc), scale=1.0),
                          [("ps", cb[tt]), "cols"], [d])
            self.ln_stats(CV, SQt, MEAN, RSTD, TMP)
            for cc in range(4):
                self.ln_apply(CV[cc], MEAN, RSTD, TMP, LNS[cc], AF.Silu, self.col("cln_g_%d" % l, cc), self.col("cln_b_%d" % l, cc))
            for dd in range(4):
                for tt in range(2):
                    bank = 4 + (dd % 2) * 2 + tt

                    def pw_chain(e, dd=dd, tt=tt, bank=bank):
                        ins = None
                        for cc in range(4):
                            ins = e.matmul(self.ps[bank][:, :], PWW[:, cc, dd * 128:(dd + 1) * 128], LNS[cc].ap[:, T2[tt]], start=(cc == 0), stop=(cc == 3))
                        return ins
                    P.pe(pw_chain, LNS + [lw], [("ps", bank)])
                    d = Yc[4 + dd].sl(tt * NT, (tt + 1) * NT)
                    P.dve(lambda e, d=d, bank=bank, dd=dd: e.tensor_scalar(out=d.ap, in0=self.ps[bank][:, :], scalar1=self.col("cpw_b_%d" % l, dd),
                                                                        scalar2=None, op0=ALU.add),
                          [("ps", bank), "cols"], [d])
        else:
            zero_y(4, 8)

        if "c" in groups:
            R.reset(base)
            G1, G2, G3, G4 = R.f32(HALF), R.f32(HALF), R.f32(HALF), R.f32(HALF)
            Ub = [R.bf16(HALF) for _ in range(4)]
            Vf = [R.f32(HALF) for _ in range(4)]
            VT = [R.bf16(NT) for _ in range(8)]
            GT = R.f32(NT)

            def gelu(banks, dst):
                for tt in range(2):
                    d = G1.sl(tt * NT, (tt + 1) * NT)
                    P.act(lambda e, d=d, b=banks[tt]: e.activation(out=d.ap, in_=self.ps[b][:, :], func=AF.Identity), [("ps", banks[tt])], [d])
                P.act(lambda e: e.activation(out=G2.ap, in_=G1.ap, func=AF.Square), [G1], [G2])
                P.dve(lambda e: e.tensor_scalar(out=G2.ap, in0=G2.ap, scalar1=0.044715, scalar2=1.0, op0=ALU.mult, op1=ALU.add), [G2], [G2])
                P.dve(lambda e: e.tensor_tensor(out=G2.ap, in0=G2.ap, in1=G1.ap, op=ALU.mult), [G1, G2], [G2])
                P.act(lambda e: e.activation(out=G2.ap, in_=G2.ap, func=AF.Sigmoid, scale=1.5957691216057308), [G2], [G2])
                P.dve(lambda e: e.tensor_tensor(out=dst.ap, in0=G1.ap, in1=G2.ap, op=ALU.mult), [G1, G2], [dst])

            for hd in range(4):
                gelu(self.proj(l, 12 + hd), Ub[hd])
            for hd in range(4):
                gelu(self.proj(l, 16 + hd), Vf[hd])
            self.ln_stats(Vf, G1, G2, G3, G4)
            for hd in range(4):
                self.ln_apply(Vf[hd], G2, G3, G4, Vf[hd], AF.Identity, self.col("sln_g_%d" % l, hd), self.col("sln_b_%d" % l, hd))
            for q in range(8):
                bank = 4 + (q % 2)

                def tr4(e, q=q, bank=bank):
                    ins = None
                    for hd in range(4):
                        ins = e.transpose(out=self.ps[bank][:, hd * 128:(hd + 1) * 128], in_=Vf[hd].ap[:, q * 128:(q + 1) * 128], identity=self.ident)
                    return ins
                P.pe(tr4, [v.sl(q * 128, (q + 1) * 128) for v in Vf], [("ps", bank)])
                P.act(lambda e, q=q, bank=bank: e.activation(out=VT[q].ap, in_=self.ps[bank][:, :], func=AF.Identity), [("ps", bank)], [VT[q]])
            for tt in range(2):
                for hd in range(4):
                    bank = 6 + (hd % 2)

                    def sp4(e, tt=tt, hd=hd, bank=bank):
                        ins = None
                        for qq in range(4):
                            ins = e.matmul(self.ps[bank][:, qq * 128:(qq + 1) * 128], VT[tt * 4 + qq].ap[:, hd * 128:(hd + 1) * 128], SGW[:, hd, :],
                                           start=True, stop=True)
                        return ins
                    P.pe(sp4, [VT[tt * 4 + qq] for qq in range(4)] + [lw], [("ps", bank)])
                    P.dve(lambda e, hd=hd, bank=bank: e.tensor_tensor(
                        out=GT.ap.rearrange("p (q t) -> p q t", q=4), in0=self.ps[bank][:, :].rearrange("p (q t) -> p q t", q=4),
                        in1=self.row("sgub_%d" % l, hd * 128, 128).rearrange("p (o t) -> p o t", o=1).broadcast_to([128, 4, 128]), op=ALU.add),
                        [("ps", bank), "rows"], [GT])
                    d = Yc[8 + hd].sl(tt * NT, (tt + 1) * NT)
                    P.dve(lambda e, d=d, hd=hd, tt=tt: e.tensor_tensor(out=d.ap, in0=GT.ap, in1=Ub[hd].ap[:, T2[tt]], op=ALU.mult), [GT, Ub[hd]], [d])
        else:
            zero_y(8, 12)

        if "d" in groups:
            self.ssd(l, hf, Yc, base)
        else:
            zero_y(12, 16)

        for i in range(DC):
            wb = self.next_w(("out", i))
            wv = wb.ap.rearrange("p (c f) -> p c f", c=DC)
            for tt in range(2):
                bank = (i % 2) * 2 + tt

                def mm_chain(e, wv=wv, tt=tt, bank=bank):
                    ins = None
                    for rc in range(DC):
                        ins = e.matmul(self.ps[bank][:, :], wv[:, rc, :], Yc[rc].ap[:, T2[tt]], start=(rc == 0), stop=(rc == DC - 1))
                    return ins
                P.pe(mm_chain, [wb, Y], [("ps", bank)])
                self.residual_update(i, 2 * hf + tt, bank, 1.0)

    def ssd(self, l, hf, Yc, base):
        P = self.P
        A = self.A
        R = self.R
        R.reset(base)
        lw = self.LWB
        WDT = lw.ap[:, LW_DT:LW_DT + 128].rearrange("p (c m) -> p c m", c=DC)
        T2 = (slice(0, NT), slice(NT, 2 * NT))
        SZ = [R.bf16(HALF) for _ in range(4)]
        XR = [R.f32(1028)]
        XC = [R.f32(HALF) for _ in range(6)]
        CT = R.f32(HALF)
        CF = [R.bf16(HALF) for _ in range(2)]
        BF = [R.bf16(HALF) for _ in range(2)]
        DTR = R.f32(HALF)
        dtr8 = A.big[0:8, DTR.lo:DTR.lo + HALF]
        for i in range(4):
            bz = self.proj(l, 20 + i)
            for tt in range(2):
                d = SZ[i].sl(tt * NT, (tt + 1) * NT)
                P.act(lambda e, d=d, b=bz[tt]: e.activation(out=d.ap, in_=self.ps[b][:, :], func=AF.Silu), [("ps", bz[tt])], [d])
        for i in range(8):
            bx = self.proj(l, 24 + i)
            xr = XR[0]
            if hf == 0:
                P.dve(lambda e, xr=xr: e.memset(xr.ap[:, 0:4], 0.0), [], [xr.sl(0, 4)])
            else:
                P.dve(lambda e, xr=xr, i=i: e.tensor_copy(out=xr.ap[:, 0:4], in_=self.HALO_D[i].ap), [self.HALO_D[i]], [xr.sl(0, 4)])
            for tt in range(2):
                d = xr.sl(4 + tt * NT, 4 + (tt + 1) * NT)
                P.act(lambda e, d=d, b=bx[tt]: e.activation(out=d.ap, in_=self.ps[b][:, :], func=AF.Identity), [("ps", bx[tt])], [d])
            P.dve(lambda e, xr=xr, i=i: e.tensor_copy(out=self.HALO_D[i].ap, in_=xr.ap[:, 1024:1028]), [xr.sl(1024, 1028)], [self.HALO_D[i]])
            dst = XC[i] if i < 6 else CT
            P.dve(lambda e, xr=xr, dst=dst, i=i: e.tensor_scalar(out=dst.ap, in0=xr.ap[:, 1:1025], scalar1=self.col("scw_%d" % l, i * 4),
                                                              scalar2=self.col("scb_%d" % l, i), op0=ALU.mult, op1=ALU.add),
                  [xr, "cols"], [dst])
            for k in range(1, 4):
                P.dve(lambda e, xr=xr, dst=dst, i=i, k=k: e.scalar_tensor_tensor(out=dst.ap, in0=xr.ap[:, 1 + k:1025 + k],
                                                                              scalar=self.col("scw_%d" % l, i * 4 + k), in1=dst.ap,
                                                                              op0=ALU.mult, op1=ALU.add),
                      [xr, dst, "cols"], [dst])
            if i < 6:
                P.act(lambda e, dst=dst: e.activation(out=dst.ap, in_=dst.ap, func=AF.Silu), [dst], [dst])
                if i >= 4:
                    P.dve(lambda e, dst=dst, i=i: e.tensor_copy(out=BF[i - 4].ap, in_=dst.ap), [dst], [BF[i - 4]])
            else:
                P.act(lambda e, i=i: e.activation(out=CF[i - 6].ap, in_=CT.ap, func=AF.Silu), [CT], [CF[i - 6]])
        for tt in range(2):
            bank = 4 + tt

            def dt_chain(e, tt=tt, bank=bank):
                ins = None
                for c2 in range(DC):
                    ins = e.matmul(self.ps[bank][0:8, :], WDT[:, c2, :], self.Hc[c2].ap[:, T2[tt]], start=(c2 == 0), stop=(c2 == DC - 1))
                return ins
            P.pe(dt_chain, [lw, self.H], [("ps", bank)])
            d = DTR.sl(tt * NT, (tt + 1) * NT)
            P.act(lambda e, tt=tt, bank=bank: e.activation(out=dtr8[:, T2[tt]], in_=self.ps[bank][0:8, :], func=AF.Identity), [("ps", bank)], [d])

        Hs = Bump(A, self.H.lo, self.H.hi)
        DT1, DTE, DTT, DA, ACS, DIF, DS, NACS = [Hs.f32(16) for _ in range(8)]
        XTOK = Hs.f32(512)
        SCT = Hs.f32(256)
        DAREP = Hs.f32(1024)
        LH = Hs.bf16(1024)
        s1_end = Hs.o
        SETS = []
        for _ in range(2):
            SETS.append(dict(EA=Hs.f32(16), ECD=Hs.f32(16), XDT=Hs.bf16(512), XDEC=Hs.bf16(512), XSK=Hs.f32(512),
                             BTOK=Hs.bf16(256), MH=Hs.bf16(1024)))
        YT = Hs.f32(512)
        HT = Hs.f32(512)
        SQg = A.f32(self.H.lo, HALF)
        RS = A.f32(self.H.lo + HALF, HALF)
        assert self.H.lo + 2 * HALF <= s1_end + 4096
        YF = A.big[:, XC[0].lo:XC[0].lo + 4 * HALF].rearrange("p (i t) -> p i t", i=4)
        ps = self.ps

        def b864(ap8):
            return ap8.rearrange("p (h o) -> p h o", o=1).broadcast_to([128, 8, 64])

        def v864(ap512):
            return ap512.rearrange("p (h q) -> p h q", h=8)

        def stage1(q):
            S = SETS[q % 2]
            EA, ECD, XDT, XDEC, XSK, BTOK, MH = S["EA"], S["ECD"], S["XDT"], S["XDEC"], S["XSK"], S["BTOK"], S["MH"]
            qs = slice(q * 128, (q + 1) * 128)
            P.pe(lambda e: e.transpose(out=ps[4][:, 0:8], in_=dtr8[:, qs], identity=self.ident[0:8, 0:8]), [DTR.sl(q * 128, (q + 1) * 128)], [("ps", 4)])
            P.dve(lambda e: e.tensor_tensor(out=DT1.ap[:, 0:8], in0=ps[4][:, 0:8], in1=self.row("dtb_%d" % l), op=ALU.add), [("ps", 4), "rows"], [DT1])
            P.act(lambda e: e.activation(out=DTE.ap[:, 0:8], in_=DT1.ap[:, 0:8], func=AF.Exp), [DT1], [DTE])
            P.act(lambda e: e.activation(out=DTT.ap[:, 0:8], in_=DTE.ap[:, 0:8], func=AF.Ln, bias=self.col("one"), scale=1.0), [DTE, "cols"], [DTT])
            P.dve(lambda e: e.tensor_tensor(out=DA.ap[:, 0:8], in0=DTT.ap[:, 0:8], in1=self.ABC.ap, op=ALU.mult), [DTT, self.ABC], [DA])
            P.pe(lambda e: e.matmul(ps[4][:, 16:24], self.U, DA.ap[:, 0:8], start=True, stop=True), [DA, "cmat"], [("ps", 4)])
            P.pe(lambda e: e.matmul(ps[4][:, 24:32], self.ones, DA.ap[:, 0:8], start=True, stop=True), [DA, "cmat"], [("ps", 4)])
            P.act(lambda e: e.activation(out=ACS.ap, in_=ps[4][:, 16:32], func=AF.Identity), [("ps", 4)], [ACS])
            P.dve(lambda e: e.tensor_tensor(out=DIF.ap[:, 0:8], in0=ACS.ap[:, 8:16], in1=ACS.ap[:, 0:8], op=ALU.subtract), [ACS], [DIF])
            P.act(lambda e: e.activation(out=DS.ap[:, 0:8], in_=DIF.ap[:, 0:8], func=AF.Exp), [DIF], [DS])
            P.act(lambda e: e.activation(out=EA.ap[:, 0:8], in_=ACS.ap[:, 0:8], func=AF.Exp), [ACS], [EA])
            P.act(lambda e: e.activation(out=ECD.ap[:, 0:8], in_=ACS.ap[:, 8:16], func=AF.Exp), [ACS], [ECD])
            P.dve(lambda e: e.tensor_scalar(out=NACS.ap[:, 0:8], in0=ACS.ap[:, 0:8], scalar1=-1.0, scalar2=None, op0=ALU.mult), [ACS], [NACS])

            def trx(e):
                ins = None
                for i in range(4):
                    ins = e.transpose(out=ps[5][:, i * 128:(i + 1) * 128], in_=XC[i].ap[:, qs], identity=self.ident)
                return ins
            P.pe(trx, [XC[i].sl(q * 128, (q + 1) * 128) for i in range(4)], [("ps", 5)])
            P.act(lambda e: e.activation(out=XTOK.ap, in_=ps[5][:, :], func=AF.Identity), [("ps", 5)], [XTOK])
            P.dve(lambda e: e.tensor_tensor(out=v864(XDT.ap), in0=v864(XTOK.ap), in1=b864(DTT.ap[:, 0:8]), op=ALU.mult), [XTOK, DTT], [XDT])
            P.dve(lambda e: e.tensor_tensor(out=v864(XDEC.ap), in0=v864(XDT.ap), in1=b864(DS.ap[:, 0:8]), op=ALU.mult), [XDT, DS], [XDEC])
            P.dve(lambda e: e.tensor_tensor(out=v864(XSK.ap), in0=v864(XTOK.ap), in1=b864(self.row("dskip_%d" % l)), op=ALU.mult), [XTOK, "rows"], [XSK])

            def trb(e):
                ins = None
                for g in range(2):
                    ins = e.transpose(out=ps[6][:, g * 128:(g + 1) * 128], in_=XC[4 + g].ap[:, qs], identity=self.ident)
                return ins
            P.pe(trb, [XC[4].sl(q * 128, (q + 1) * 128), XC[5].sl(q * 128, (q + 1) * 128)], [("ps", 6)])
            P.act(lambda e: e.activation(out=BTOK.ap, in_=ps[6][:, 0:256], func=AF.Identity), [("ps", 6)], [BTOK])

            def sc(e):
                ins = None
                for g in range(2):
                    ins = e.matmul(ps[6][:, 256 + g * 128:256 + (g + 1) * 128], BF[g].ap[:, qs], CF[g].ap[:, qs], start=True, stop=True)
                return ins
            P.pe(sc, [BF[0].sl(q * 128, (q + 1) * 128), BF[1].sl(q * 128, (q + 1) * 128),
                      CF[0].sl(q * 128, (q + 1) * 128), CF[1].sl(q * 128, (q + 1) * 128)], [("ps", 6)])
            P.act(lambda e: e.activation(out=SCT.ap, in_=ps[6][:, 256:512], func=AF.Identity), [("ps", 6)], [SCT])
            P.dve(lambda e: e.tensor_copy(out=DAREP.ap.rearrange("p (h s) -> p h s", h=8),
                                          in_=DA.ap[:, 0:8].rearrange("p (h o) -> p h o", o=1).broadcast_to([128, 8, 128])), [DA], [DAREP])
            for h in range(8):
                bank = 2 + h // 4
                cs = slice((h % 4) * 128, (h % 4 + 1) * 128)

                def lmm(e, h=h, bank=bank, cs=cs):
                    e.matmul(ps[bank][:, cs], DAREP.ap[:, h * 128:(h + 1) * 128], self.U, start=True, stop=False)
                    return e.matmul(ps[bank][:, cs], self.ident, self.NEG, start=False, stop=True)
                P.pe(lmm, [DAREP, "cmat"], [("ps", bank)])
                P.act(lambda e, h=h, bank=bank, cs=cs: e.activation(out=LH.ap[:, h * 128:(h + 1) * 128], in_=ps[bank][:, cs], func=AF.Exp,
                                                                 bias=NACS.ap[:, h:h + 1], scale=1.0),
                      [("ps", bank), NACS], [LH.sl(h * 128, (h + 1) * 128)])
            for g in range(2):
                P.dve(lambda e, g=g: e.tensor_tensor(out=MH.ap[:, g * 512:(g + 1) * 512].rearrange("p (h t) -> p h t", h=4),
                                                    in0=LH.ap[:, g * 512:(g + 1) * 512].rearrange("p (h t) -> p h t", h=4),
                                                    in1=SCT.ap[:, g * 128:(g + 1) * 128].rearrange("p (o t) -> p o t", o=1).broadcast_to([128, 4, 128]),
                                                    op=ALU.mult),
                      [LH.sl(g * 512, (g + 1) * 512), SCT], [MH.sl(g * 512, (g + 1) * 512)])

        def stage2(q):
            S = SETS[q % 2]
            EA, ECD, XDT, XDEC, XSK, BTOK, MH = S["EA"], S["ECD"], S["XDT"], S["XDEC"], S["XSK"], S["BTOK"], S["MH"]
            qs = slice(q * 128, (q + 1) * 128)

            def ydiag(e):
                ins = None
                for h in range(8):
                    ins = e.matmul(ps[0][:, h * 64:(h + 1) * 64], MH.ap[:, h * 128:(h + 1) * 128], XDT.ap[:, h * 64:(h + 1) * 64], start=True, stop=True)
                return ins
            P.pe(ydiag, [MH, XDT], [("ps", 0)])

            def yoff(e):
                ins = None
                for g in range(2):
                    ins = e.matmul(ps[1][:, g * 256:(g + 1) * 256], CF[g].ap[:, qs], self.HSTb.ap[:, g * 256:(g + 1) * 256], start=True, stop=True)
                return ins
            P.pe(yoff, [CF[0].sl(q * 128, (q + 1) * 128), CF[1].sl(q * 128, (q + 1) * 128), self.HSTb], [("ps", 1)])

            def stt(e):
                ins = None
                for g in range(2):
                    ins = e.matmul(ps[7][:, g * 256:(g + 1) * 256], BTOK.ap[:, g * 128:(g + 1) * 128], XDEC.ap[:, g * 256:(g + 1) * 256], start=True, stop=True)
                return ins
            P.pe(stt, [BTOK, XDEC], [("ps", 7)])
            P.dve(lambda e: e.tensor_tensor(out=v864(YT.ap), in0=v864(ps[1][:, :]), in1=b864(EA.ap[:, 0:8]), op=ALU.mult), [("ps", 1), EA], [YT])
            P.dve(lambda e: e.tensor_tensor(out=YT.ap, in0=YT.ap, in1=ps[0][:, :], op=ALU.add), [YT, ("ps", 0)], [YT])
            P.dve(lambda e: e.tensor_tensor(out=YT.ap, in0=YT.ap, in1=XSK.ap, op=ALU.add), [YT, XSK], [YT])
            P.dve(lambda e: e.tensor_tensor(out=v864(HT.ap), in0=v864(self.HST.ap), in1=b864(ECD.ap[:, 0:8]), op=ALU.mult), [self.HST, ECD], [HT])
            P.dve(lambda e: e.tensor_tensor(out=self.HST.ap, in0=HT.ap, in1=ps[7][:, :], op=ALU.add), [HT, ("ps", 7)], [self.HST])
            P.act(lambda e: e.activation(out=self.HSTb.ap, in_=self.HST.ap, func=AF.Identity), [self.HST], [self.HSTb])

            def try_(e):
                ins = None
                for i in range(4):
                    ins = e.transpose(out=ps[1][:, i * 128:(i + 1) * 128], in_=YT.ap[:, i * 128:(i + 1) * 128], identity=self.ident)
                return ins
            P.pe(try_, [YT], [("ps", 1)])
            P.act(lambda e: e.activation(out=YF[:, :, qs], in_=ps[1][:, :].rearrange("p (i t) -> p i t", i=4), func=AF.Identity),
                  [("ps", 1)], [XC[i].sl(q * 128, (q + 1) * 128) for i in range(4)])

        stage1(0)
        for q in range(8):
            if q + 1 < 8:
                stage1(q + 1)
            stage2(q)
        for i in range(4):
            P.dve(lambda e, i=i: e.tensor_tensor(out=XC[i].ap, in0=XC[i].ap, in1=SZ[i].ap, op=ALU.mult), [XC[i], SZ[i]], [XC[i]])
        for g in range(2):
            for ii in range(2):
                i = 2 * g + ii
                P.act(lambda e, i=i: e.activation(out=SQg.ap, in_=XC[i].ap, func=AF.Square), [XC[i]], [SQg])
                for tt in range(2):
                    P.pe(lambda e, tt=tt, ii=ii: e.matmul(ps[4 + tt][:, :], self.ones, SQg.ap[:, T2[tt]], start=(ii == 0), stop=(ii == 1)),
                         [SQg, "cmat"], [("ps", 4 + tt)])
            for tt in range(2):
                rs = RS.sl(tt * NT, (tt + 1) * NT)
                P.act(lambda e, tt=tt, rs=rs: e.activation(out=rs.ap, in_=ps[4 + tt][:, :], func=AF.Sqrt, bias=self.col("eps"), scale=1.0 / 256),
                      [("ps", 4 + tt), "cols"], [rs])
                P.dve(lambda e, rs=rs: e.reciprocal(out=rs.ap, in_=rs.ap), [rs], [rs])
            for ii in range(2):
                i = 2 * g + ii
                P.dve(lambda e, i=i: e.scalar_tensor_tensor(out=Yc[12 + i].ap, in0=XC[i].ap, scalar=self.col("snorm_%d" % l, i), in1=RS.ap,
                                                           op0=ALU.mult, op1=ALU.mult),
                      [XC[i], RS, "cols"], [Yc[12 + i]])

    def finish(self):
        P = self.P
        keys = [("OUT", i) for i in range(DC)] + [("OUT", i, hf) for i in range(DC) for hf in range(self.NH)]
        P.pool(lambda e: e.memset(self.XS[0].ap[:, 0:1], 0.0), keys + [self.XS[0]], [self.XS[0]])

    def build(self):
        self.wslot = 0
        self.xslot = 0
        self.pbank = 0
        with contextlib.ExitStack() as stack:
            self.plan(stack)
            self.load_consts()
            self.copy_x_in()
            for st in self.stages:
                kind = st[0]
                if kind == "ffn":
                    self.ffn(st[1], st[2], st[3])
                elif kind == "setup":
                    self.layer_setup(st[1])
                elif kind == "mix":
                    self.mixer(st[1], st[2], *(st[3:]))
                elif kind == "final":
                    self.final_norm(st[1])
                elif kind == "dump":
                    self.dump()
            self.finish()
            self.P.emit(self.nc, stack)
        return self.nc


def run(inputs, stages, NH=8, ncores=2, trace=False):
    inp = {k: np.asarray(v) for k, v in inputs.items()}
    cp, rp = make_packs(inp)
    cols = cp.build()
    rows = rp.build()
    W = prep_weights(inp)
    b = Builder(stages, NH, cp, rp)
    nc = b.build()
    T = NH * HALF
    cm = make_cmat()
    in_maps = []
    for c in range(ncores):
        xin = np.ascontiguousarray(inp["x"][c, :T, :].T).reshape(DC, 128, T)
        m = {"xin": xin, "cols": cols, "rows": rows, "cmat": cm}
        m.update({k: v for k, v in W.items() if k in b.w})
        in_maps.append(m)
    res = run_bass_kernel_spmd(nc, in_maps, core_ids=list(range(ncores)), **({"trace": True} if trace else {}))
    if trace:
        print("EXEC_NS", res.exec_time_ns, flush=True)
    outs = [np.asarray(r["out"]).reshape(D, T).T for r in res.results]
    return np.ascontiguousarray(np.stack(outs)).astype(np.float32)


def full_stages(NH=8):
    st = []
    for l in range(L):
        st.append(("setup", l))
        for hf in range(NH):
            st.append(("ffn", l, "ffn1", hf))
            st.append(("mix", l, hf))
            st.append(("ffn", l, "ffn2", hf))
    for hf in range(NH):
        st.append(("final", hf))
    return st


def kernel(**inputs):
    return run(inputs, full_stages(8), NH=8, ncores=2)
```

```python
import contextlib
import numpy as np
import concourse.bass as bass
import concourse.mybir as mybir
from concourse.bass_utils import run_bass_kernel_spmd

F32 = mybir.dt.float32
BF16 = mybir.dt.bfloat16
AF = mybir.ActivationFunctionType
ALU = mybir.AluOpType

D = 2048
DC = 16
HALF = 1024
NT = 512
DFF = 5632
FC = 44
L = 4
EPS = 1e-6
SEM_LIMIT = 12000
SAME_ENGINE_SYNC = True
GR = 64
ARENA = 53184


class V:
    def __init__(self, ap, lo, hi, es):
        self.ap = ap
        self.lo = lo
        self.hi = hi
        self.es = es

    def sl(self, e0, e1):
        lo = self.lo + (e0 * self.es) // 4
        hi = self.lo + (e1 * self.es + 3) // 4
        return V(self.ap[:, e0:e1], lo, hi, self.es)


class Op:
    __slots__ = ("eng", "fn", "reads", "writes", "dma_key", "deps", "needs_inc", "sem", "val", "idx")

    def __init__(self, eng, fn, reads, writes, dma_key):
        self.eng = eng
        self.fn = fn
        self.reads = reads
        self.writes = writes
        self.dma_key = dma_key
        self.deps = ()
        self.needs_inc = dma_key is not None
        self.sem = None
        self.val = 0


def _keys(items):
    out = []
    for it in items:
        if isinstance(it, V):
            for g in range(it.lo // GR, (it.hi + GR - 1) // GR):
                out.append(g)
        elif isinstance(it, (list, tuple)) and it and isinstance(it[0], V):
            out.extend(_keys(it))
        else:
            out.append(it)
    return out


class Prog:
    ENGS = ("pe", "act", "dve", "pool", "sp")

    def __init__(self):
        self.ops = []

    def add(self, eng, fn, reads=(), writes=(), dma_key=None):
        op = Op(eng, fn, _keys(reads), _keys(writes), dma_key)
        self.ops.append(op)
        return op

    def pe(self, fn, reads=(), writes=()):
        return self.add("pe", fn, reads, writes)

    def act(self, fn, reads=(), writes=()):
        return self.add("act", fn, reads, writes)

    def dve(self, fn, reads=(), writes=()):
        return self.add("dve", fn, reads, writes)

    def pool(self, fn, reads=(), writes=()):
        return self.add("pool", fn, reads, writes)

    def dma(self, eng, out, in_, reads, writes, key):
        return self.add(eng, lambda e: e.dma_start(out=out, in_=in_), reads, writes, dma_key=key)

    def analyze(self):
        last_w = {}
        readers = {}
        last_dma = {}
        for idx, op in enumerate(self.ops):
            op.idx = idx
            deps = {}
            for r in op.reads:
                w = last_w.get(r)
                if w is not None:
                    deps[id(w)] = w
            for r in op.writes:
                w = last_w.get(r)
                if w is not None:
                    deps[id(w)] = w
                rl = readers.get(r)
                if rl:
                    for rd in rl:
                        deps[id(rd)] = rd
            if op.dma_key is not None:
                p = last_dma.get(op.dma_key)
                if p is not None:
                    deps[id(p)] = p
                last_dma[op.dma_key] = op
            deps.pop(id(op), None)
            dl = []
            for d in deps.values():
                if d.dma_key is None and d.eng == op.eng:
                    if op.eng == "pe" or not SAME_ENGINE_SYNC:
                        continue
                dl.append(d)
            op.deps = dl
            for d in dl:
                d.needs_inc = True
            for r in op.writes:
                last_w[r] = op
                readers[r] = None
            for r in op.reads:
                rl = readers.get(r)
                if rl is None:
                    readers[r] = [op]
                elif rl[-1] is not op:
                    rl.append(op)

    def emit(self, nc, stack):
        self.analyze()
        cur = {}
        cnt = {}
        dsem = {}
        dcnt = {}
        nsem = [0]

        def newsem():
            nsem[0] += 1
            return stack.enter_context(nc.semaphore("s%d" % nsem[0]))

        for op in self.ops:
            if op.dma_key is not None:
                k = op.dma_key
                if k not in dsem or dcnt[k] + 16 > SEM_LIMIT:
                    dsem[k] = newsem()
                    dcnt[k] = 0
                dcnt[k] += 16
                op.sem = dsem[k]
                op.val = dcnt[k]
            elif op.needs_inc:
                e = op.eng
                if e not in cur or cnt[e] + 1 > SEM_LIMIT:
                    cur[e] = newsem()
                    cnt[e] = 0
                cnt[e] += 1
                op.sem = cur[e]
                op.val = cnt[e]
        self.nsem = nsem[0]
        per_eng = {e: [o for o in self.ops if o.eng == e] for e in self.ENGS}

        def run_engine(eng_obj, ops):
            waited = {}
            for op in ops:
                for d in op.deps:
                    key = id(d.sem)
                    if waited.get(key, 0) >= d.val:
                        continue
                    eng_obj.wait_ge(d.sem, d.val)
                    waited[key] = d.val
                ins = op.fn(eng_obj)
                if op.sem is not None:
                    ins.then_inc(op.sem, 16 if op.dma_key is not None else 1)

        with nc.Block() as block:
            @block.tensor
            def _(e):
                run_engine(e, per_eng["pe"])

            @block.scalar
            def _(e):
                run_engine(e, per_eng["act"])

            @block.vector
            def _(e):
                run_engine(e, per_eng["dve"])

            @block.gpsimd
            def _(e):
                run_engine(e, per_eng["pool"])

            @block.sync
            def _(e):
                run_engine(e, per_eng["sp"])


class Arena:
    def __init__(self, big):
        self.big = big

    def f32(self, off, n):
        assert off % GR == 0 and off + n <= ARENA, (off, n)
        return V(self.big[:, off:off + n], off, off + n, 4)

    def bf16(self, off, n):
        assert off % GR == 0 and n % 2 == 0 and off + n // 2 <= ARENA, (off, n)
        return V(self.big[:, off:off + n // 2].bitcast(BF16), off, off + n // 2, 2)


class Bump:
    def __init__(self, arena, lo, hi):
        self.a = arena
        self.lo = lo
        self.hi = hi
        self.o = lo

    def reset(self, to=None):
        self.o = self.lo if to is None else to

    def f32(self, n):
        v = self.a.f32(self.o, n)
        self.o += (n + GR - 1) // GR * GR
        assert self.o <= self.hi, (self.o, self.hi)
        return v

    def bf16(self, n):
        v = self.a.bf16(self.o, n)
        self.o += (n // 2 + GR - 1) // GR * GR
        assert self.o <= self.hi, (self.o, self.hi)
        return v


class ColPack:
    def __init__(self):
        self.cols = []
        self.index = {}
        self.n = 0

    def add(self, name, arr):
        arr = np.ascontiguousarray(arr, dtype=np.float32).reshape(128, -1)
        self.index[name] = (self.n, arr.shape[1])
        self.cols.append(arr)
        self.n += arr.shape[1]

    def build(self):
        return np.ascontiguousarray(np.concatenate(self.cols, axis=1))


def chunk_cols(v):
    v = np.asarray(v, dtype=np.float32)
    return np.ascontiguousarray(v.reshape(-1, 128).T)


def rep_rows(v):
    v = np.asarray(v, dtype=np.float32).reshape(1, -1)
    return np.ascontiguousarray(np.repeat(v, 128, axis=0))


def make_packs(inp):
    cp = ColPack()
    cp.add("eps", np.full((128, 1), EPS, np.float32))
    cp.add("one", np.full((128, 1), 1.0, np.float32))
    cp.add("final_norm", chunk_cols(inp["final_norm"]))
    for l in range(L):
        for nm in ("ffn1_norm", "ffn2_norm", "mix_norm"):
            cp.add("%s_%d" % (nm, l), chunk_cols(inp[nm][l]))
        cp.add("pool_scale_%d" % l, chunk_cols(inp["pool_scale"][l]))
        w = np.asarray(inp["conv_dw_w"][l])
        cp.add("cdw_w_%d" % l, w.reshape(31, 4, 128).transpose(2, 1, 0).reshape(128, 124))
        cp.add("cdw_b_%d" % l, chunk_cols(inp["conv_dw_b"][l]))
        cp.add("cln_g_%d" % l, chunk_cols(inp["conv_ln_g"][l]))
        cp.add("cln_b_%d" % l, chunk_cols(inp["conv_ln_b"][l]))
        cp.add("cpw_b_%d" % l, chunk_cols(inp["conv_pw_b"][l]))
        cp.add("sln_g_%d" % l, chunk_cols(inp["sgu_ln_g"][l]))
        cp.add("sln_b_%d" % l, chunk_cols(inp["sgu_ln_b"][l]))
        w = np.asarray(inp["ssm_conv_w"][l])
        cp.add("scw_%d" % l, w.reshape(4, 8, 128).transpose(2, 1, 0).reshape(128, 32))
        cp.add("scb_%d" % l, chunk_cols(inp["ssm_conv_b"][l]))
        cp.add("snorm_%d" % l, chunk_cols(inp["ssm_norm"][l]))
    rp = ColPack()
    k = np.arange(16)
    invc = np.stack([1.0 / np.minimum(k + 1, w) for w in (2, 4, 8, 16)]).astype(np.float32)
    rp.add("invc", rep_rows(invc))
    for l in range(L):
        rp.add("dtb_%d" % l, rep_rows(inp["ssm_dt_bias"][l]))
        rp.add("alog_%d" % l, rep_rows(inp["ssm_a_log"][l]))
        rp.add("dskip_%d" % l, rep_rows(inp["ssm_d"][l]))
        rp.add("sgub_%d" % l, rep_rows(inp["sgu_b"][l]))
    return cp, rp


LW_POOL, LW_PW, LW_SGU, LW_DT = 0, 512, 2560, 3072
LW_N = 3200


def prep_weights(inp):
    out = {}
    for nm in ("ffn1", "ffn2"):
        wg = np.asarray(inp[nm + "_w_gate"]).reshape(L, DC, 128, FC, 128)
        wu = np.asarray(inp[nm + "_w_up"]).reshape(L, DC, 128, FC, 128)
        wgu = np.empty((L, FC, 128, 2, DC, 128), np.float32)
        wgu[:, :, :, 0] = wg.transpose(0, 3, 2, 1, 4)
        wgu[:, :, :, 1] = wu.transpose(0, 3, 2, 1, 4)
        out[nm + "_wgu"] = wgu.reshape(L, FC, 128, 2 * DC * 128)
        wd = np.asarray(inp[nm + "_w_down"]).reshape(L, 2, 22, 128, DC, 128)
        out[nm + "_wd"] = np.ascontiguousarray(wd.transpose(0, 4, 1, 3, 2, 5)).reshape(L, DC, 2, 128, 22 * 128)
    win = np.asarray(inp["w_in"])
    w32 = win[:, :, :4096].reshape(L, DC, 128, 32, 128)
    out["w_in"] = np.ascontiguousarray(w32.transpose(0, 3, 2, 1, 4)).reshape(L, 32, 128, DC * 128)
    wo = np.asarray(inp["w_out"]).reshape(L, DC, 128, DC, 128)
    out["w_out"] = np.ascontiguousarray(wo.transpose(0, 3, 2, 1, 4)).reshape(L, DC, 128, DC * 128)
    lw = np.empty((L, 128, LW_N), np.float32)
    lw[:, :, LW_POOL:LW_POOL + 512] = np.asarray(inp["pool_w"]).transpose(0, 2, 1, 3).reshape(L, 128, 512)
    lw[:, :, LW_PW:LW_PW + 2048] = np.asarray(inp["conv_pw_w"]).reshape(L, 4, 128, 512).transpose(0, 2, 1, 3).reshape(L, 128, 2048)
    lw[:, :, LW_SGU:LW_SGU + 512] = np.asarray(inp["sgu_w_s"]).transpose(0, 3, 1, 2).reshape(L, 128, 512)
    lw[:, :, LW_DT:LW_DT + 128] = win[:, :, 4096:4104].reshape(L, DC, 128, 8).transpose(0, 2, 1, 3).reshape(L, 128, 128)
    out["lw"] = lw
    return out


def make_cmat():
    c = np.zeros((128, 512), np.float32)
    c[:, 0:128] = 1.0
    c[:, 128:256] = np.eye(128, dtype=np.float32)
    k = np.arange(128)
    c[:, 256:384] = (k[:, None] <= k[None, :]).astype(np.float32)
    c[:, 384:512] = np.where(k[:, None] > k[None, :], -30000.0, 0.0)
    return c


class Builder:
    def __init__(self, stages, NH, cp, rp):
        self.stages = stages
        self.NH = NH
        self.T = NH * HALF
        self.ci = cp.index
        self.ri = rp.index
        self.ncp = (cp.n + GR - 1) // GR * GR
        self.nrp = (rp.n + GR - 1) // GR * GR
        nc = bass.Bass("TRN2", target_bir_lowering=False)
        self.nc = nc
        self.P = Prog()
        dt = nc.dram_tensor
        T = self.T
        self.xin = dt("xin", [DC, 128, T], F32, kind="ExternalInput").ap()
        self.out = dt("out", [DC, 128, T], F32, kind="ExternalOutput").ap()
        self.xt = dt("xt", [DC, 128, T], F32).ap()
        self.cols_d = dt("cols", [128, cp.n], F32, kind="ExternalInput").ap()
        self.rows_d = dt("rows", [128, rp.n], F32, kind="ExternalInput").ap()
        self.cmat_d = dt("cmat", [128, 512], F32, kind="ExternalInput").ap()
        self.w = {}
        for nm in ("ffn1", "ffn2"):
            if not any(st[0] == "ffn" and st[2] == nm for st in stages):
                continue
            self.w[nm + "_wgu"] = dt(nm + "_wgu", [L, FC, 128, 2 * DC * 128], F32, kind="ExternalInput").ap()
            self.w[nm + "_wd"] = dt(nm + "_wd", [L, DC, 2, 128, 22 * 128], F32, kind="ExternalInput").ap()
        self.w["w_in"] = dt("w_in", [L, 32, 128, DC * 128], F32, kind="ExternalInput").ap()
        self.w["w_out"] = dt("w_out", [L, DC, 128, DC * 128], F32, kind="ExternalInput").ap()
        self.w["lw"] = dt("lw", [L, 128, LW_N], F32, kind="ExternalInput").ap()

    def plan(self, stack):
        nc = self.nc
        big = stack.enter_context(nc.sbuf_tensor("arena", [128, ARENA], F32))
        A = Arena(big)
        self.A = A
        pb = Bump(A, 0, ARENA)
        self.cols = pb.f32(self.ncp)
        self.rows = pb.f32(self.nrp)
        self.cmat = pb.f32(512)
        c = self.cmat.ap
        self.ones, self.ident, self.U, self.NEG = c[:, 0:128], c[:, 128:256], c[:, 256:384], c[:, 384:512]
        self.LWB = pb.bf16(LW_N)
        self.HALO_A = [pb.f32(16) for _ in range(4)]
        self.HALO_D = [pb.f32(4) for _ in range(8)]
        self.HST = pb.f32(512)
        self.HSTb = pb.bf16(512)
        self.ABC = pb.f32(8)
        self.ONESB = pb.bf16(128)
        self.HALO_Bb = [pb.bf16(32) for _ in range(4)]
        self.H = pb.bf16(DC * HALF)
        self.Hc = [self.H.sl(i * HALF, (i + 1) * HALF) for i in range(DC)]
        rlo = pb.o
        self.ACTb = pb.bf16(FC * HALF)
        self.R = Bump(A, rlo, pb.o)
        self.XH = A.f32(rlo, DC * HALF)
        self.XHc = [self.XH.sl(i * HALF, (i + 1) * HALF) for i in range(DC)]
        self.SQ = [A.bf16(rlo + DC * HALF + k * HALF, HALF) for k in range(2)]
        self.RSTD = A.f32(rlo + DC * HALF + 2 * HALF, HALF)
        self.STMP = A.f32(rlo + DC * HALF + 3 * HALF, HALF)
        self.WST = [pb.f32(4096) for _ in range(2)]
        self.WBF = [pb.bf16(4096) for _ in range(2)]
        self.XS = [pb.f32(NT) for _ in range(4)]
        self.SG = [pb.bf16(NT) for _ in range(2)]
        self.ps = [stack.enter_context(nc.psum_tensor("ps%d" % b, [128, 512], F32)) for b in range(8)]

    def col(self, name, k=0, n=1):
        off, w = self.ci[name]
        assert k + n <= w, (name, k, n, w)
        return self.cols.ap[:, off + k: off + k + n]

    def row(self, name, k=0, n=None):
        off, w = self.ri[name]
        n = w - k if n is None else n
        assert k + n <= w
        return self.rows.ap[:, off + k: off + k + n]

    def load_consts(self):
        P = self.P
        P.dma("sp", self.cols.ap[:, 0:self.cols_d.shape[1]], self.cols_d, [], [self.cols, "cols"], "c0")
        P.dma("sp", self.rows.ap[:, 0:self.rows_d.shape[1]], self.rows_d, [], [self.rows, "rows"], "c1")
        P.dma("sp", self.cmat.ap, self.cmat_d, [], [self.cmat, "cmat"], "c2")
        P.dve(lambda e: e.tensor_copy(out=self.ONESB.ap, in_=self.ones), ["cmat"], [self.ONESB, "onesb"])

    def copy_x_in(self):
        P = self.P
        for i in range(DC):
            P.dma("pool", self.xt[i], self.xin[i], [], [("XT", i, gt) for gt in range(2 * self.NH)], ("xcp", i % 4))

    def dump(self):
        P = self.P
        for i in range(DC):
            P.dma("pool", self.out[i], self.xt[i], [("XT", i, gt) for gt in range(2 * self.NH)], [("OUT", i)], ("xcp", i % 4))

    def rms_stats(self, hf):
        P = self.P
        t0 = hf * HALF
        for i in range(DC):
            P.dma("pool", self.XHc[i].ap, self.xt[i, :, t0:t0 + HALF],
                  [("XT", i, 2 * hf), ("XT", i, 2 * hf + 1)], [self.XHc[i]], ("xh", i % 4))
        for i in range(DC):
            sq = self.SQ[i % 2]
            P.act(lambda e, i=i, sq=sq: e.activation(out=sq.ap, in_=self.XHc[i].ap, func=AF.Square),
                  [self.XHc[i]], [sq])
            for tt in range(2):
                P.pe(lambda e, i=i, sq=sq, tt=tt: e.matmul(self.ps[tt][:, :], self.ONESB.ap, sq.ap[:, tt * NT:(tt + 1) * NT],
                                                      start=(i == 0), stop=(i == DC - 1)),
                     [sq, "onesb"], [("ps", tt)])
        for tt in range(2):
            st = self.STMP.sl(tt * NT, (tt + 1) * NT)
            rs = self.RSTD.sl(tt * NT, (tt + 1) * NT)
            P.act(lambda e, tt=tt, st=st: e.activation(out=st.ap, in_=self.ps[tt][:, :], func=AF.Sqrt,
                                                      bias=self.col("eps"), scale=1.0 / D),
                  [("ps", tt), "cols"], [st])
            P.dve(lambda e, st=st, rs=rs: e.reciprocal(out=rs.ap, in_=st.ap), [st], [rs])

    def norm_pass(self, gname, hf):
        P = self.P
        self.rms_stats(hf)
        for i in range(DC):
            P.dve(lambda e, i=i: e.scalar_tensor_tensor(out=self.Hc[i].ap, in0=self.XHc[i].ap,
                                                       scalar=self.col(gname, i), in1=self.RSTD.ap,
                                                       op0=ALU.mult, op1=ALU.mult),
                  [self.XHc[i], self.RSTD, "cols"], [self.Hc[i]])

    def final_norm(self, hf):
        P = self.P
        t0 = hf * HALF
        self.rms_stats(hf)
        for i in range(DC):
            P.dve(lambda e, i=i: e.scalar_tensor_tensor(out=self.XHc[i].ap, in0=self.XHc[i].ap,
                                                       scalar=self.col("final_norm", i), in1=self.RSTD.ap,
                                                       op0=ALU.mult, op1=ALU.mult),
                  [self.XHc[i], self.RSTD, "cols"], [self.XHc[i]])
            P.dma("pool", self.out[i, :, t0:t0 + HALF], self.XHc[i].ap, [self.XHc[i]], [("OUT", i, hf)], ("xh", i % 4))

    def residual_pair(self, i, hf, banks, factor):
        P = self.P
        ks = (self.xslot, self.xslot + 1)
        self.xslot = (self.xslot + 2) % 4
        drs = [self.xt[i, :, (2 * hf + tt) * NT:(2 * hf + tt + 1) * NT] for tt in range(2)]
        for tt in range(2):
            P.dma("pool", self.XS[ks[tt]].ap, drs[tt], [("XT", i, 2 * hf + tt)], [self.XS[ks[tt]]], ("xs", ks[tt]))
        for tt in range(2):
            xs = self.XS[ks[tt]]
            P.dve(lambda e, xs=xs, bank=banks[tt]: e.scalar_tensor_tensor(out=xs.ap, in0=self.ps[bank][:, :], scalar=float(factor),
                                                                       in1=xs.ap, op0=ALU.mult, op1=ALU.add),
                  [("ps", banks[tt]), xs], [xs])
            P.dma("pool", drs[tt], xs.ap, [xs], [("XT", i, 2 * hf + tt)], ("xs", ks[tt]))

    def residual_update(self, i, gt, bank, factor):
        P = self.P
        k = self.xslot
        self.xslot = (self.xslot + 1) % 4
        xs = self.XS[k]
        dr = self.xt[i, :, gt * NT:(gt + 1) * NT]
        P.dma("pool", xs.ap, dr, [("XT", i, gt)], [xs], ("xs", k))
        P.dve(lambda e, xs=xs, bank=bank: e.scalar_tensor_tensor(out=xs.ap, in0=self.ps[bank][:, :], scalar=float(factor),
                                                               in1=xs.ap, op0=ALU.mult, op1=ALU.add),
              [("ps", bank), xs], [xs])
        P.dma("pool", dr, xs.ap, [xs], [("XT", i, gt)], ("xs", k))

    def stream_w(self, src, ncols, split=(0.5, 0.75)):
        P = self.P
        s = self.wslot
        self.wslot ^= 1
        st = self.WST[s].sl(0, ncols)
        wb = self.WBF[s].sl(0, ncols)
        P.dma("sp", st.ap, src, [], [st], ("wst", s))
        a = int(ncols * split[0]) // 128 * 128
        b = int(ncols * split[1]) // 128 * 128
        if a > 0:
            P.act(lambda e: e.activation(out=wb.ap[:, 0:a], in_=st.ap[:, 0:a], func=AF.Identity), [st.sl(0, a)], [wb.sl(0, a)])
        if b > a:
            P.pool(lambda e: e.tensor_copy(out=wb.ap[:, a:b], in_=st.ap[:, a:b]), [st.sl(a, b)], [wb.sl(a, b)])
        if ncols > b:
            P.dve(lambda e: e.tensor_copy(out=wb.ap[:, b:ncols], in_=st.ap[:, b:ncols]), [st.sl(b, ncols)], [wb.sl(b, ncols)])
        return wb

    def ffn(self, l, nm, hf):
        P = self.P
        wgu = self.w[nm + "_wgu"]
        wd = self.w[nm + "_wd"]
        self.norm_pass("%s_norm_%d" % (nm, l), hf)
        ACTc = [self.ACTb.sl(j * HALF, (j + 1) * HALF) for j in range(FC)]
        tasks = [("gu", j, 0) for j in range(FC)] + [("dn", i, fh) for i in range(DC) for fh in range(2)]

        def fetch(t):
            kind, a, b = tasks[t]
            if kind == "gu":
                return self.stream_w(wgu[l, a], 4096)
            return self.stream_w(wd[l, a, b], 2816, split=(1.0, 1.0))

        nxt = fetch(0)
        for t, (kind, a, b) in enumerate(tasks):
            wb = nxt
            if t + 1 < len(tasks):
                nxt = fetch(t + 1)
            if kind == "gu":
                j = a
                wv = wb.ap.rearrange("p (g c f) -> p g c f", g=2, c=DC)
                for tt in range(2):
                    bg = (j % 2) * 4 + tt
                    bu = (j % 2) * 4 + 2 + tt
                    sl = slice(tt * NT, (tt + 1) * NT)

                    def mm_chain(e, g, bank, wv=wv, sl=sl):
                        ins = None
                        for c in range(DC):
                            ins = e.matmul(self.ps[bank][:, :], wv[:, g, c, :], self.Hc[c].ap[:, sl],
                                           start=(c == 0), stop=(c == DC - 1))
                        return ins
                    P.pe(lambda e, f=mm_chain, b_=bg: f(e, 0, b_), [wb.sl(0, 2048), self.H], [("ps", bg)])
                    P.pe(lambda e, f=mm_chain, b_=bu: f(e, 1, b_), [wb.sl(2048, 4096), self.H], [("ps", bu)])
                    sg = self.SG[tt]
                    P.act(lambda e, sg=sg, bg=bg: e.activation(out=sg.ap, in_=self.ps[bg][:, :], func=AF.Silu),
                          [("ps", bg)], [sg])
                    dst = ACTc[j].sl(tt * NT, (tt + 1) * NT)
                    P.dve(lambda e, sg=sg, bu=bu, dst=dst: e.tensor_tensor(out=dst.ap, in0=self.ps[bu][:, :], in1=sg.ap, op=ALU.mult),
                          [("ps", bu), sg], [dst])
            else:
                i, fh = a, b
                wv = wb.ap.rearrange("p (c f) -> p c f", c=22)
                for tt in range(2):
                    bank = (i % 2) * 2 + tt
                    sl = slice(tt * NT, (tt + 1) * NT)

                    def mm_chain(e, wv=wv, sl=sl, bank=bank, fh=fh):
                        ins = None
                        for c in range(22):
                            ins = e.matmul(self.ps[bank][:, :], wv[:, c, :], ACTc[fh * 22 + c].ap[:, sl],
                                           start=(fh == 0 and c == 0), stop=(fh == 1 and c == 21))
                        return ins
                    P.pe(mm_chain, [wb] + [ACTc[fh * 22 + c].sl(tt * NT, (tt + 1) * NT) for c in range(22)], [("ps", bank)])
                if fh == 1:
                    self.residual_pair(i, hf, ((i % 2) * 2, (i % 2) * 2 + 1), 0.5)

    def layer_setup(self, l):
        P = self.P
        s = self.wslot
        self.wslot ^= 1
        st = self.WST[s].sl(0, LW_N)
        P.dma("sp", st.ap, self.w["lw"][l], [], [st], ("wst", s))
        lw = self.LWB
        for (a, b) in ((LW_POOL, LW_PW + 2048), (LW_DT, LW_N)):
            P.dve(lambda e, a=a, b=b: e.tensor_copy(out=lw.ap[:, a:b], in_=st.ap[:, a:b]), [st], [lw.sl(a, b)])
        P.dve(lambda e: e.tensor_tensor(out=lw.ap[:, LW_SGU:LW_SGU + 512].rearrange("p (h t) -> p h t", h=4),
                                        in0=st.ap[:, LW_SGU:LW_SGU + 512].rearrange("p (h t) -> p h t", h=4),
                                        in1=self.U.rearrange("p (o t) -> p o t", o=1).broadcast_to([128, 4, 128]),
                                        op=ALU.mult),
              [st, "cmat"], [lw.sl(LW_SGU, LW_SGU + 512)])
        P.act(lambda e: e.activation(out=self.ABC.ap, in_=self.row("alog_%d" % l), func=AF.Exp), ["rows"], [self.ABC])
        P.dve(lambda e: e.tensor_scalar(out=self.ABC.ap, in0=self.ABC.ap, scalar1=-1.0, scalar2=None, op0=ALU.mult),
              [self.ABC], [self.ABC])
        P.dve(lambda e: e.memset(self.HST.ap, 0.0), [], [self.HST])
        P.dve(lambda e: e.memset(self.HSTb.ap, 0.0), [], [self.HSTb])

    def start_wseq(self, l, groups):
        order = []
        if "a" in groups:
            order += [0, 1, 2, 3]
        if "b" in groups:
            for cc in range(4):
                order += [4 + cc, 8 + cc]
        if "c" in groups:
            order += list(range(12, 20))
        if "d" in groups:
            order += list(range(20, 32))
        self.wseq = [("in", c) for c in order] + [("out", i) for i in range(DC)]
        self.wl = l
        self.wpos = 0
        self.wpending = self._fetch_w(0)

    def _fetch_w(self, pos):
        kind, c = self.wseq[pos]
        if kind == "in":
            return self.stream_w(self.w["w_in"][self.wl, c], 2048, split=(0.375, 1.0))
        return self.stream_w(self.w["w_out"][self.wl, c], 2048, split=(1.0, 1.0))

    def next_w(self, tag):
        assert self.wseq[self.wpos] == tag, (self.wseq[self.wpos], tag)
        wb = self.wpending
        self.wpos += 1
        if self.wpos < len(self.wseq):
            self.wpending = self._fetch_w(self.wpos)
        return wb

    def proj(self, l, c):
        P = self.P
        wb = self.next_w(("in", c))
        wv = wb.ap.rearrange("p (c f) -> p c f", c=DC)
        pb = self.pbank
        self.pbank ^= 1
        banks = (pb * 2, pb * 2 + 1)
        for tt in range(2):
            sl = slice(tt * NT, (tt + 1) * NT)

            def mm_chain(e, wv=wv, sl=sl, bank=banks[tt]):
                ins = None
                for c2 in range(DC):
                    ins = e.matmul(self.ps[bank][:, :], wv[:, c2, :], self.Hc[c2].ap[:, sl], start=(c2 == 0), stop=(c2 == DC - 1))
                return ins
            P.pe(mm_chain, [wb, self.H], [("ps", banks[tt])])
        return banks

    def ln_stats(self, X, SQt, MEAN, RSTD, TMP):
        P = self.P
        for cc in range(4):
            P.act(lambda e, cc=cc: e.activation(out=SQt.ap, in_=X[cc].ap, func=AF.Square), [X[cc]], [SQt])
            for tt in range(2):
                sl = slice(tt * NT, (tt + 1) * NT)
                P.pe(lambda e, cc=cc, tt=tt, sl=sl: e.matmul(self.ps[4 + tt][:, :], self.ones, X[cc].ap[:, sl], start=(cc == 0), stop=(cc == 3)),
                     [X[cc], "cmat"], [("ps", 4 + tt)])
                P.pe(lambda e, cc=cc, tt=tt, sl=sl: e.matmul(self.ps[6 + tt][:, :], self.ones, SQt.ap[:, sl], start=(cc == 0), stop=(cc == 3)),
                     [SQt, "cmat"], [("ps", 6 + tt)])
        for tt in range(2):
            a, b = tt * NT, (tt + 1) * NT
            me, rs, tm = MEAN.sl(a, b), RSTD.sl(a, b), TMP.sl(a, b)
            P.dve(lambda e, tt=tt, me=me: e.tensor_scalar(out=me.ap, in0=self.ps[4 + tt][:, :], scalar1=1.0 / 512, scalar2=None, op0=ALU.mult),
                  [("ps", 4 + tt)], [me])
            P.dve(lambda e, me=me, tm=tm: e.tensor_tensor(out=tm.ap, in0=me.ap, in1=me.ap, op=ALU.mult), [me], [tm])
            P.dve(lambda e, tt=tt, tm=tm, rs=rs: e.scalar_tensor_tensor(out=rs.ap, in0=self.ps[6 + tt][:, :], scalar=1.0 / 512, in1=tm.ap,
                                                                    op0=ALU.mult, op1=ALU.subtract),
                  [("ps", 6 + tt), tm], [rs])
            P.act(lambda e, rs=rs: e.activation(out=rs.ap, in_=rs.ap, func=AF.Sqrt, bias=self.col("eps"), scale=1.0), [rs, "cols"], [rs])
            P.dve(lambda e, rs=rs: e.reciprocal(out=rs.ap, in_=rs.ap), [rs], [rs])

    def ln_apply(self, x, MEAN, RSTD, TMP, out, func, gcol, bcol):
        P = self.P
        P.dve(lambda e: e.tensor_tensor(out=TMP.ap, in0=x.ap, in1=MEAN.ap, op=ALU.subtract), [x, MEAN], [TMP])
        P.dve(lambda e: e.tensor_tensor(out=TMP.ap, in0=TMP.ap, in1=RSTD.ap, op=ALU.mult), [TMP, RSTD], [TMP])
        P.act(lambda e: e.activation(out=out.ap, in_=TMP.ap, func=func, bias=bcol, scale=gcol), [TMP, "cols"], [out])

    def mixer(self, l, hf, groups="abcd"):
        P = self.P
        A = self.A
        self.norm_pass("mix_norm_%d" % l, hf)
        self.start_wseq(l, groups)
        R = self.R
        R.reset()
        Y = R.bf16(DC * HALF)
        Yc = [Y.sl(c * HALF, (c + 1) * HALF) for c in range(DC)]
        base = R.o
        lw = self.LWB
        POOLW = lw.ap[:, LW_POOL:LW_POOL + 512].rearrange("p (g d) -> p g d", g=4)
        PWW = lw.ap[:, LW_PW:LW_PW + 2048].rearrange("p (c d) -> p c d", c=4)
        SGW = lw.ap[:, LW_SGU:LW_SGU + 512].rearrange("p (h t) -> p h t", h=4)
        WDT = lw.ap[:, LW_DT:LW_DT + 128].rearrange("p (c m) -> p c m", c=DC)
        T2 = (slice(0, NT), slice(NT, 2 * NT))

        def zero_y(c0, c1):
            for c in range(c0, c1):
                P.dve(lambda e, c=c: e.memset(Yc[c].ap, 0.0), [], [Yc[c]])

        if "a" in groups:
            AB = [R.f32(1040) for _ in range(4)]
            T = [R.f32(1040), R.f32(1040)]
            PB = [R.bf16(HALF) for _ in range(4)]
            TM = R.f32(16)
            for g in range(4):
                banks = self.proj(l, g)
                ab = AB[g]
                if hf == 0:
                    P.dve(lambda e, ab=ab: e.memset(ab.ap[:, 0:16], 0.0), [], [ab.sl(0, 16)])
                else:
                    P.dve(lambda e, ab=ab, g=g: e.tensor_copy(out=ab.ap[:, 0:16], in_=self.HALO_A[g].ap), [self.HALO_A[g]], [ab.sl(0, 16)])
                for tt in range(2):
                    d = ab.sl(16 + tt * NT, 16 + (tt + 1) * NT)
                    P.act(lambda e, d=d, b=banks[tt]: e.activation(out=d.ap, in_=self.ps[b][:, :], func=AF.Identity), [("ps", banks[tt])], [d])
                P.dve(lambda e, ab=ab, g=g: e.tensor_copy(out=self.HALO_A[g].ap, in_=ab.ap[:, 1024:1040]), [ab.sl(1024, 1040)], [self.HALO_A[g]])
                cur = ab
                for lev in range(g + 1):
                    k = 1 << lev
                    s0 = 2 * k - 1
                    dst = T[lev % 2]
                    P.dve(lambda e, cur=cur, dst=dst, k=k, s0=s0: e.tensor_tensor(out=dst.ap[:, s0:1040], in0=cur.ap[:, s0:1040],
                                                                               in1=cur.ap[:, s0 - k:1040 - k], op=ALU.add),
                          [cur], [dst])
                    cur = dst
                wdt = float(2 << g)
                P.dve(lambda e, cur=cur, ab=ab, g=g, wdt=wdt: e.scalar_tensor_tensor(out=PB[g].ap, in0=cur.ap[:, 16:1040], scalar=1.0 / wdt,
                                                                                  in1=ab.ap[:, 16:1040], op0=ALU.mult, op1=ALU.subtract),
                      [cur, ab], [PB[g]])
                if hf == 0:
                    P.dve(lambda e, cur=cur, g=g: e.tensor_tensor(out=TM.ap, in0=cur.ap[:, 16:32], in1=self.row("invc", g * 16, 16), op=ALU.mult),
                          [cur, "rows"], [TM])
                    P.dve(lambda e, ab=ab, g=g: e.tensor_tensor(out=PB[g].ap[:, 0:16], in0=TM.ap, in1=ab.ap[:, 16:32], op=ALU.subtract),
                          [TM, ab], [PB[g]])
                for tt in range(2):
                    P.pe(lambda e, g=g, tt=tt: e.matmul(self.ps[4 + tt][:, :], POOLW[:, g, :], PB[g].ap[:, T2[tt]], start=True, stop=True),
                         [PB[g], lw], [("ps", 4 + tt)])
                    d = Yc[g].sl(tt * NT, (tt + 1) * NT)
                    P.dve(lambda e, d=d, g=g, tt=tt: e.tensor_scalar(out=d.ap, in0=self.ps[4 + tt][:, :], scalar1=self.col("pool_scale_%d" % l, g),
                                                                  scalar2=None, op0=ALU.mult),
                          [("ps", 4 + tt), "cols"], [d])
        else:
            zero_y(0, 4)

        if "b" in groups:
            R.reset(base)
            HG = [R.bf16(1056) for _ in range(4)]
            CV = [R.f32(HALF) for _ in range(4)]
            SQt, MEAN, RSTD, TMP = R.f32(HALF), R.f32(HALF), R.f32(HALF), R.f32(HALF)
            SIGT = R.f32(NT)
            DG = [R.bf16(128) for _ in range(6)]
            LNS = [R.bf16(HALF) for _ in range(4)]
            ndg = 0
            for cc in range(4):
                bv = self.proj(l, 4 + cc)
                bgt = self.proj(l, 8 + cc)
                hg = HG[cc]
                if hf == 0:
                    P.dve(lambda e, hg=hg: e.memset(hg.ap[:, 0:32], 0.0), [], [hg.sl(0, 32)])
                else:
                    P.dve(lambda e, hg=hg, cc=cc: e.tensor_copy(out=hg.ap[:, 0:32], in_=self.HALO_Bb[cc].ap), [self.HALO_Bb[cc]], [hg.sl(0, 32)])
                for tt in range(2):
                    P.act(lambda e, b=bgt[tt]: e.activation(out=SIGT.ap, in_=self.ps[b][:, :], func=AF.Sigmoid), [("ps", bgt[tt])], [SIGT])
                    d = hg.sl(32 + tt * NT, 32 + (tt + 1) * NT)
                    P.dve(lambda e, d=d, b=bv[tt]: e.tensor_tensor(out=d.ap, in0=self.ps[b][:, :], in1=SIGT.ap, op=ALU.mult),
                          [("ps", bv[tt]), SIGT], [d])
                P.dve(lambda e, hg=hg, cc=cc: e.tensor_copy(out=self.HALO_Bb[cc].ap, in_=hg.ap[:, 1024:1056]), [hg.sl(1024, 1056)], [self.HALO_Bb[cc]])
                cb = (4 + (cc % 2) * 2, 5 + (cc % 2) * 2)
                for k in range(31):
                    dg = DG[ndg % 6]
                    ndg += 1
                    P.dve(lambda e, dg=dg, cc=cc, k=k: e.tensor_scalar(out=dg.ap, in0=self.ident, scalar1=self.col("cdw_w_%d" % l, cc * 31 + k),
                                                                   scalar2=None, op0=ALU.mult),
                          ["cmat", "cols"], [dg])

                    def tap(e, dg=dg, hg=hg, k=k, cb=cb):
                        ins = None
                        for tt in range(2):
                            ins = e.matmul(self.ps[cb[tt]][:, :], dg.ap, hg.ap[:, 2 + k + tt * NT:2 + k + (tt + 1) * NT], start=(k == 0), stop=(k == 30))
                        return ins
                    P.pe(tap, [dg, hg], [("ps", cb[0]), ("ps", cb[1])])
                cv = CV[cc]
                for tt in range(2):
                    d = cv.sl(tt * NT, (tt + 1) * NT)
                    P.act(lambda e, d=d, b=cb[tt], cc=cc: e.activation(out=d.ap, in_=self.ps[b][:, :], func=AF.Identity,
                                                                    bias=self.col("cdw_b_%d" % l, cc), scale=1.0),
                          [("ps", cb[tt]), "cols"], [d])
            self.ln_stats(CV, SQt, MEAN, RSTD, TMP)
            for cc in range(4):
                self.ln_apply(CV[cc], MEAN, RSTD, TMP, LNS[cc], AF.Silu, self.col("cln_g_%d" % l, cc), self.col("cln_b_%d" % l, cc))
            for dd in range(4):
                for tt in range(2):
                    bank = 4 + (dd % 2) * 2 + tt

                    def pw_chain(e, dd=dd, tt=tt, bank=bank):
                        ins = None
                        for cc in range(4):
                            ins = e.matmul(self.ps[bank][:, :], PWW[:, cc, dd * 128:(dd + 1) * 128], LNS[cc].ap[:, T2[tt]], start=(cc == 0), stop=(cc == 3))
                        return ins
                    P.pe(pw_chain, LNS + [lw], [("ps", bank)])
                    d = Yc[4 + dd].sl(tt * NT, (tt + 1) * NT)
                    P.dve(lambda e, d=d, bank=bank, dd=dd: e.tensor_scalar(out=d.ap, in0=self.ps[bank][:, :], scalar1=self.col("cpw_b_%d" % l, dd),
                                                                        scalar2=None, op0=ALU.add),
                          [("ps", bank), "cols"], [d])
        else:
            zero_y(4, 8)

        if "c" in groups:
            R.reset(base)
            G1, G2, G3, G4 = R.f32(HALF), R.f32(HALF), R.f32(HALF), R.f32(HALF)
            Ub = [R.bf16(HALF) for _ in range(4)]
            Vf = [R.f32(HALF) for _ in range(4)]
            VT = [R.bf16(NT) for _ in range(8)]
            GT = R.f32(NT)

            def gelu(banks, dst):
                for tt in range(2):
                    d = G1.sl(tt * NT, (tt + 1) * NT)
                    P.act(lambda e, d=d, b=banks[tt]: e.activation(out=d.ap, in_=self.ps[b][:, :], func=AF.Identity), [("ps", banks[tt])], [d])
                P.act(lambda e: e.activation(out=G2.ap, in_=G1.ap, func=AF.Square), [G1], [G2])
                P.dve(lambda e: e.tensor_scalar(out=G2.ap, in0=G2.ap, scalar1=0.044715, scalar2=1.0, op0=ALU.mult, op1=ALU.add), [G2], [G2])
                P.dve(lambda e: e.tensor_tensor(out=G2.ap, in0=G2.ap, in1=G1.ap, op=ALU.mult), [G1, G2], [G2])
                P.act(lambda e: e.activation(out=G2.ap, in_=G2.ap, func=AF.Sigmoid, scale=1.5957691216057308), [G2], [G2])
                P.dve(lambda e: e.tensor_tensor(out=dst.ap, in0=G1.ap, in1=G2.ap, op=ALU.mult), [G1, G2], [dst])

            for hd in range(4):
                gelu(self.proj(l, 12 + hd), Ub[hd])
            for hd in range(4):
                gelu(self.proj(l, 16 + hd), Vf[hd])
            self.ln_stats(Vf, G1, G2, G3, G4)
            for hd in range(4):
                self.ln_apply(Vf[hd], G2, G3, G4, Vf[hd], AF.Identity, self.col("sln_g_%d" % l, hd), self.col("sln_b_%d" % l, hd))
            for q in range(8):
                bank = 4 + (q % 2)

                def tr4(e, q=q, bank=bank):
                    ins = None
                    for hd in range(4):
                        ins = e.transpose(out=self.ps[bank][:, hd * 128:(hd + 1) * 128], in_=Vf[hd].ap[:, q * 128:(q + 1) * 128], identity=self.ident)
                    return ins
                P.pe(tr4, [v.sl(q * 128, (q + 1) * 128) for v in Vf], [("ps", bank)])
                P.act(lambda e, q=q, bank=bank: e.activation(out=VT[q].ap, in_=self.ps[bank][:, :], func=AF.Identity), [("ps", bank)], [VT[q]])
            for tt in range(2):
                for hd in range(4):
                    bank = 6 + (hd % 2)

                    def sp4(e, tt=tt, hd=hd, bank=bank):
                        ins = None
                        for qq in range(4):
                            ins = e.matmul(self.ps[bank][:, qq * 128:(qq + 1) * 128], VT[tt * 4 + qq].ap[:, hd * 128:(hd + 1) * 128], SGW[:, hd, :],
                                           start=True, stop=True)
                        return ins
                    P.pe(sp4, [VT[tt * 4 + qq] for qq in range(4)] + [lw], [("ps", bank)])
                    P.dve(lambda e, hd=hd, bank=bank: e.tensor_tensor(
                        out=GT.ap.rearrange("p (q t) -> p q t", q=4), in0=self.ps[bank][:, :].rearrange("p (q t) -> p q t", q=4),
                        in1=self.row("sgub_%d" % l, hd * 128, 128).rearrange("p (o t) -> p o t", o=1).broadcast_to([128, 4, 128]), op=ALU.add),
                        [("ps", bank), "rows"], [GT])
                    d = Yc[8 + hd].sl(tt * NT, (tt + 1) * NT)
                    P.dve(lambda e, d=d, hd=hd, tt=tt: e.tensor_tensor(out=d.ap, in0=GT.ap, in1=Ub[hd].ap[:, T2[tt]], op=ALU.mult), [GT, Ub[hd]], [d])
        else:
            zero_y(8, 12)

        if "d" in groups:
            self.ssd(l, hf, Yc, base)
        else:
            zero_y(12, 16)

        for i in range(DC):
            wb = self.next_w(("out", i))
            wv = wb.ap.rearrange("p (c f) -> p c f", c=DC)
            for tt in range(2):
                bank = (i % 2) * 2 + tt

                def mm_chain(e, wv=wv, tt=tt, bank=bank):
                    ins = None
                    for rc in range(DC):
                        ins = e.matmul(self.ps[bank][:, :], wv[:, rc, :], Yc[rc].ap[:, T2[tt]], start=(rc == 0), stop=(rc == DC - 1))
                    return ins
                P.pe(mm_chain, [wb, Y], [("ps", bank)])
            self.residual_pair(i, hf, ((i % 2) * 2, (i % 2) * 2 + 1), 1.0)

    def ssd(self, l, hf, Yc, base):
        P = self.P
        A = self.A
        R = self.R
        R.reset(base)
        lw = self.LWB
        WDT = lw.ap[:, LW_DT:LW_DT + 128].rearrange("p (c m) -> p c m", c=DC)
        T2 = (slice(0, NT), slice(NT, 2 * NT))
        SZ = [R.bf16(HALF) for _ in range(4)]
        XR = [R.f32(1028)]
        XC = [R.f32(HALF) for _ in range(6)]
        CT = R.f32(HALF)
        CF = [R.bf16(HALF) for _ in range(2)]
        BF = [R.bf16(HALF) for _ in range(2)]
        DTR = R.f32(HALF)
        dtr8 = A.big[0:8, DTR.lo:DTR.lo + HALF]
        for i in range(4):
            bz = self.proj(l, 20 + i)
            for tt in range(2):
                d = SZ[i].sl(tt * NT, (tt + 1) * NT)
                P.act(lambda e, d=d, b=bz[tt]: e.activation(out=d.ap, in_=self.ps[b][:, :], func=AF.Silu), [("ps", bz[tt])], [d])
        for i in range(8):
            bx = self.proj(l, 24 + i)
            xr = XR[0]
            if hf == 0:
                P.dve(lambda e, xr=xr: e.memset(xr.ap[:, 0:4], 0.0), [], [xr.sl(0, 4)])
            else:
                P.dve(lambda e, xr=xr, i=i: e.tensor_copy(out=xr.ap[:, 0:4], in_=self.HALO_D[i].ap), [self.HALO_D[i]], [xr.sl(0, 4)])
            for tt in range(2):
                d = xr.sl(4 + tt * NT, 4 + (tt + 1) * NT)
                P.act(lambda e, d=d, b=bx[tt]: e.activation(out=d.ap, in_=self.ps[b][:, :], func=AF.Identity), [("ps", bx[tt])], [d])
            P.dve(lambda e, xr=xr, i=i: e.tensor_copy(out=self.HALO_D[i].ap, in_=xr.ap[:, 1024:1028]), [xr.sl(1024, 1028)], [self.HALO_D[i]])
            dst = XC[i] if i < 6 else CT
            P.dve(lambda e, xr=xr, dst=dst, i=i: e.tensor_scalar(out=dst.ap, in0=xr.ap[:, 1:1025], scalar1=self.col("scw_%d" % l, i * 4),
                                                              scalar2=self.col("scb_%d" % l, i), op0=ALU.mult, op1=ALU.add),
                  [xr, "cols"], [dst])
            for k in range(1, 4):
                P.dve(lambda e, xr=xr, dst=dst, i=i, k=k: e.scalar_tensor_tensor(out=dst.ap, in0=xr.ap[:, 1 + k:1025 + k],
                                                                              scalar=self.col("scw_%d" % l, i * 4 + k), in1=dst.ap,
                                                                              op0=ALU.mult, op1=ALU.add),
                      [xr, dst, "cols"], [dst])
            if i < 6:
                P.act(lambda e, dst=dst: e.activation(out=dst.ap, in_=dst.ap, func=AF.Silu), [dst], [dst])
                if i >= 4:
                    P.dve(lambda e, dst=dst, i=i: e.tensor_copy(out=BF[i - 4].ap, in_=dst.ap), [dst], [BF[i - 4]])
            else:
                P.act(lambda e, i=i: e.activation(out=CF[i - 6].ap, in_=CT.ap, func=AF.Silu), [CT], [CF[i - 6]])
        for tt in range(2):
            bank = 4 + tt

            def dt_chain(e, tt=tt, bank=bank):
                ins = None
                for c2 in range(DC):
                    ins = e.matmul(self.ps[bank][0:8, :], WDT[:, c2, :], self.Hc[c2].ap[:, T2[tt]], start=(c2 == 0), stop=(c2 == DC - 1))
                return ins
            P.pe(dt_chain, [lw, self.H], [("ps", bank)])
            d = DTR.sl(tt * NT, (tt + 1) * NT)
            P.act(lambda e, tt=tt, bank=bank: e.activation(out=dtr8[:, T2[tt]], in_=self.ps[bank][0:8, :], func=AF.Identity), [("ps", bank)], [d])

        Hs = Bump(A, self.H.lo, self.H.hi)
        DT1, DTE, DTT, DA, ACS, DIF, DS, NACS = [Hs.f32(16) for _ in range(8)]
        XTOK = Hs.f32(512)
        SCT = Hs.f32(256)
        DAREP = Hs.f32(1024)
        LH = Hs.bf16(1024)
        s1_end = Hs.o
        SETS = []
        for _ in range(2):
            SETS.append(dict(EA=Hs.f32(16), ECD=Hs.f32(16), XDT=Hs.bf16(512), XDEC=Hs.bf16(512), XSK=Hs.f32(512),
                             BTOK=Hs.bf16(256), MH=Hs.bf16(1024)))
        YT = Hs.f32(512)
        HT = Hs.f32(512)
        SQg = A.f32(self.H.lo, HALF)
        RS = A.f32(self.H.lo + HALF, HALF)
        assert self.H.lo + 2 * HALF <= s1_end + 4096
        YF = A.big[:, XC[0].lo:XC[0].lo + 4 * HALF].rearrange("p (i t) -> p i t", i=4)
        ps = self.ps

        def b864(ap8):
            return ap8.rearrange("p (h o) -> p h o", o=1).broadcast_to([128, 8, 64])

        def v864(ap512):
            return ap512.rearrange("p (h q) -> p h q", h=8)

        def stage1(q):
            S = SETS[q % 2]
            EA, ECD, XDT, XDEC, XSK, BTOK, MH = S["EA"], S["ECD"], S["XDT"], S["XDEC"], S["XSK"], S["BTOK"], S["MH"]
            qs = slice(q * 128, (q + 1) * 128)
            P.pe(lambda e: e.transpose(out=ps[4][:, 0:8], in_=dtr8[:, qs], identity=self.ident[0:8, 0:8]), [DTR.sl(q * 128, (q + 1) * 128)], [("ps", 4)])
            P.dve(lambda e: e.tensor_tensor(out=DT1.ap[:, 0:8], in0=ps[4][:, 0:8], in1=self.row("dtb_%d" % l), op=ALU.add), [("ps", 4), "rows"], [DT1])
            P.act(lambda e: e.activation(out=DTE.ap[:, 0:8], in_=DT1.ap[:, 0:8], func=AF.Exp), [DT1], [DTE])
            P.act(lambda e: e.activation(out=DTT.ap[:, 0:8], in_=DTE.ap[:, 0:8], func=AF.Ln, bias=self.col("one"), scale=1.0), [DTE, "cols"], [DTT])
            P.dve(lambda e: e.tensor_tensor(out=DA.ap[:, 0:8], in0=DTT.ap[:, 0:8], in1=self.ABC.ap, op=ALU.mult), [DTT, self.ABC], [DA])
            P.pe(lambda e: e.matmul(ps[4][:, 16:24], self.U, DA.ap[:, 0:8], start=True, stop=True), [DA, "cmat"], [("ps", 4)])
            P.pe(lambda e: e.matmul(ps[4][:, 24:32], self.ones, DA.ap[:, 0:8], start=True, stop=True), [DA, "cmat"], [("ps", 4)])
            P.act(lambda e: e.activation(out=ACS.ap, in_=ps[4][:, 16:32], func=AF.Identity), [("ps", 4)], [ACS])
            P.dve(lambda e: e.tensor_tensor(out=DIF.ap[:, 0:8], in0=ACS.ap[:, 8:16], in1=ACS.ap[:, 0:8], op=ALU.subtract), [ACS], [DIF])
            P.act(lambda e: e.activation(out=DS.ap[:, 0:8], in_=DIF.ap[:, 0:8], func=AF.Exp), [DIF], [DS])
            P.act(lambda e: e.activation(out=EA.ap[:, 0:8], in_=ACS.ap[:, 0:8], func=AF.Exp), [ACS], [EA])
            P.act(lambda e: e.activation(out=ECD.ap[:, 0:8], in_=ACS.ap[:, 8:16], func=AF.Exp), [ACS], [ECD])
            P.dve(lambda e: e.tensor_scalar(out=NACS.ap[:, 0:8], in0=ACS.ap[:, 0:8], scalar1=-1.0, scalar2=None, op0=ALU.mult), [ACS], [NACS])

            def trx(e):
                ins = None
                for i in range(4):
                    ins = e.transpose(out=ps[5][:, i * 128:(i + 1) * 128], in_=XC[i].ap[:, qs], identity=self.ident)
                return ins
            P.pe(trx, [XC[i].sl(q * 128, (q + 1) * 128) for i in range(4)], [("ps", 5)])
            P.act(lambda e: e.activation(out=XTOK.ap, in_=ps[5][:, :], func=AF.Identity), [("ps", 5)], [XTOK])
            P.dve(lambda e: e.tensor_tensor(out=v864(XDT.ap), in0=v864(XTOK.ap), in1=b864(DTT.ap[:, 0:8]), op=ALU.mult), [XTOK, DTT], [XDT])
            P.dve(lambda e: e.tensor_tensor(out=v864(XDEC.ap), in0=v864(XDT.ap), in1=b864(DS.ap[:, 0:8]), op=ALU.mult), [XDT, DS], [XDEC])
            P.dve(lambda e: e.tensor_tensor(out=v864(XSK.ap), in0=v864(XTOK.ap), in1=b864(self.row("dskip_%d" % l)), op=ALU.mult), [XTOK, "rows"], [XSK])

            def trb(e):
                ins = None
                for g in range(2):
                    ins = e.transpose(out=ps[6][:, g * 128:(g + 1) * 128], in_=XC[4 + g].ap[:, qs], identity=self.ident)
                return ins
            P.pe(trb, [XC[4].sl(q * 128, (q + 1) * 128), XC[5].sl(q * 128, (q + 1) * 128)], [("ps", 6)])
            P.act(lambda e: e.activation(out=BTOK.ap, in_=ps[6][:, 0:256], func=AF.Identity), [("ps", 6)], [BTOK])

            def sc(e):
                ins = None
                for g in range(2):
                    ins = e.matmul(ps[6][:, 256 + g * 128:256 + (g + 1) * 128], BF[g].ap[:, qs], CF[g].ap[:, qs], start=True, stop=True)
                return ins
            P.pe(sc, [BF[0].sl(q * 128, (q + 1) * 128), BF[1].sl(q * 128, (q + 1) * 128),
                      CF[0].sl(q * 128, (q + 1) * 128), CF[1].sl(q * 128, (q + 1) * 128)], [("ps", 6)])
            P.act(lambda e: e.activation(out=SCT.ap, in_=ps[6][:, 256:512], func=AF.Identity), [("ps", 6)], [SCT])
            P.dve(lambda e: e.tensor_copy(out=DAREP.ap.rearrange("p (h s) -> p h s", h=8),
                                          in_=DA.ap[:, 0:8].rearrange("p (h o) -> p h o", o=1).broadcast_to([128, 8, 128])), [DA], [DAREP])
            for h in range(8):
                bank = 2 + h // 4
                cs = slice((h % 4) * 128, (h % 4 + 1) * 128)

                def lmm(e, h=h, bank=bank, cs=cs):
                    e.matmul(ps[bank][:, cs], DAREP.ap[:, h * 128:(h + 1) * 128], self.U, start=True, stop=False)
                    return e.matmul(ps[bank][:, cs], self.ident, self.NEG, start=False, stop=True)
                P.pe(lmm, [DAREP, "cmat"], [("ps", bank)])
                P.act(lambda e, h=h, bank=bank, cs=cs: e.activation(out=LH.ap[:, h * 128:(h + 1) * 128], in_=ps[bank][:, cs], func=AF.Exp,
                                                                 bias=NACS.ap[:, h:h + 1], scale=1.0),
                      [("ps", bank), NACS], [LH.sl(h * 128, (h + 1) * 128)])
            for g in range(2):
                P.dve(lambda e, g=g: e.tensor_tensor(out=MH.ap[:, g * 512:(g + 1) * 512].rearrange("p (h t) -> p h t", h=4),
                                                    in0=LH.ap[:, g * 512:(g + 1) * 512].rearrange("p (h t) -> p h t", h=4),
                                                    in1=SCT.ap[:, g * 128:(g + 1) * 128].rearrange("p (o t) -> p o t", o=1).broadcast_to([128, 4, 128]),
                                                    op=ALU.mult),
                      [LH.sl(g * 512, (g + 1) * 512), SCT], [MH.sl(g * 512, (g + 1) * 512)])

        def stage2(q):
            S = SETS[q % 2]
            EA, ECD, XDT, XDEC, XSK, BTOK, MH = S["EA"], S["ECD"], S["XDT"], S["XDEC"], S["XSK"], S["BTOK"], S["MH"]
            qs = slice(q * 128, (q + 1) * 128)

            def ydiag(e):
                ins = None
                for h in range(8):
                    ins = e.matmul(ps[0][:, h * 64:(h + 1) * 64], MH.ap[:, h * 128:(h + 1) * 128], XDT.ap[:, h * 64:(h + 1) * 64], start=True, stop=True)
                return ins
            P.pe(ydiag, [MH, XDT], [("ps", 0)])

            def yoff(e):
                ins = None
                for g in range(2):
                    ins = e.matmul(ps[1][:, g * 256:(g + 1) * 256], CF[g].ap[:, qs], self.HSTb.ap[:, g * 256:(g + 1) * 256], start=True, stop=True)
                return ins
            P.pe(yoff, [CF[0].sl(q * 128, (q + 1) * 128), CF[1].sl(q * 128, (q + 1) * 128), self.HSTb], [("ps", 1)])

            def stt(e):
                ins = None
                for g in range(2):
                    ins = e.matmul(ps[7][:, g * 256:(g + 1) * 256], BTOK.ap[:, g * 128:(g + 1) * 128], XDEC.ap[:, g * 256:(g + 1) * 256], start=True, stop=True)
                return ins
            P.pe(stt, [BTOK, XDEC], [("ps", 7)])
            P.dve(lambda e: e.tensor_tensor(out=v864(YT.ap), in0=v864(ps[1][:, :]), in1=b864(EA.ap[:, 0:8]), op=ALU.mult), [("ps", 1), EA], [YT])
            P.dve(lambda e: e.tensor_tensor(out=YT.ap, in0=YT.ap, in1=ps[0][:, :], op=ALU.add), [YT, ("ps", 0)], [YT])
            P.dve(lambda e: e.tensor_tensor(out=YT.ap, in0=YT.ap, in1=XSK.ap, op=ALU.add), [YT, XSK], [YT])
            P.dve(lambda e: e.tensor_tensor(out=v864(HT.ap), in0=v864(self.HST.ap), in1=b864(ECD.ap[:, 0:8]), op=ALU.mult), [self.HST, ECD], [HT])
            P.dve(lambda e: e.tensor_tensor(out=self.HST.ap, in0=HT.ap, in1=ps[7][:, :], op=ALU.add), [HT, ("ps", 7)], [self.HST])
            P.act(lambda e: e.activation(out=self.HSTb.ap, in_=self.HST.ap, func=AF.Identity), [self.HST], [self.HSTb])

            def try_(e):
                ins = None
                for i in range(4):
                    ins = e.transpose(out=ps[1][:, i * 128:(i + 1) * 128], in_=YT.ap[:, i * 128:(i + 1) * 128], identity=self.ident)
                return ins
            P.pe(try_, [YT], [("ps", 1)])
            P.act(lambda e: e.activation(out=YF[:, :, qs], in_=ps[1][:, :].rearrange("p (i t) -> p i t", i=4), func=AF.Identity),
                  [("ps", 1)], [XC[i].sl(q * 128, (q + 1) * 128) for i in range(4)])

        stage1(0)
        for q in range(8):
            if q + 1 < 8:
                stage1(q + 1)
            stage2(q)
        for i in range(4):
            P.dve(lambda e, i=i: e.tensor_tensor(out=XC[i].ap, in0=XC[i].ap, in1=SZ[i].ap, op=ALU.mult), [XC[i], SZ[i]], [XC[i]])
        for g in range(2):
            for ii in range(2):
                i = 2 * g + ii
                P.act(lambda e, i=i: e.activation(out=SQg.ap, in_=XC[i].ap, func=AF.Square), [XC[i]], [SQg])
                for tt in range(2):
                    P.pe(lambda e, tt=tt, ii=ii: e.matmul(ps[4 + tt][:, :], self.ones, SQg.ap[:, T2[tt]], start=(ii == 0), stop=(ii == 1)),
                         [SQg, "cmat"], [("ps", 4 + tt)])
            for tt in range(2):
                rs = RS.sl(tt * NT, (tt + 1) * NT)
                P.act(lambda e, tt=tt, rs=rs: e.activation(out=rs.ap, in_=ps[4 + tt][:, :], func=AF.Sqrt, bias=self.col("eps"), scale=1.0 / 256),
                      [("ps", 4 + tt), "cols"], [rs])
                P.dve(lambda e, rs=rs: e.reciprocal(out=rs.ap, in_=rs.ap), [rs], [rs])
            for ii in range(2):
                i = 2 * g + ii
                P.dve(lambda e, i=i: e.scalar_tensor_tensor(out=Yc[12 + i].ap, in0=XC[i].ap, scalar=self.col("snorm_%d" % l, i), in1=RS.ap,
                                                           op0=ALU.mult, op1=ALU.mult),
                      [XC[i], RS, "cols"], [Yc[12 + i]])

    def finish(self):
        P = self.P
        keys = [("OUT", i) for i in range(DC)] + [("OUT", i, hf) for i in range(DC) for hf in range(self.NH)]
        P.pool(lambda e: e.memset(self.XS[0].ap[:, 0:1], 0.0), keys + [self.XS[0]], [self.XS[0]])

    def build(self):
        self.wslot = 0
        self.xslot = 0
        self.pbank = 0
        with contextlib.ExitStack() as stack:
            self.plan(stack)
            self.load_consts()
            self.copy_x_in()
            for st in self.stages:
                kind = st[0]
                if kind == "ffn":
                    self.ffn(st[1], st[2], st[3])
                elif kind == "setup":
                    self.layer_setup(st[1])
                elif kind == "mix":
                    self.mixer(st[1], st[2], *(st[3:]))
                elif kind == "final":
                    self.final_norm(st[1])
                elif kind == "dump":
                    self.dump()
            self.finish()
            self.P.emit(self.nc, stack)
        return self.nc


def run(inputs, stages, NH=8, ncores=2, trace=False):
    inp = {k: np.asarray(v) for k, v in inputs.items()}
    cp, rp = make_packs(inp)
    cols = cp.build()
    rows = rp.build()
    W = prep_weights(inp)
    b = Builder(stages, NH, cp, rp)
    nc = b.build()
    T = NH * HALF
    cm = make_cmat()
    in_maps = []
    for c in range(ncores):
        xin = np.ascontiguousarray(inp["x"][c, :T, :].T).reshape(DC, 128, T)
        m = {"xin": xin, "cols": cols, "rows": rows, "cmat": cm}
        m.update({k: v for k, v in W.items() if k in b.w})
        in_maps.append(m)
    res = run_bass_kernel_spmd(nc, in_maps, core_ids=list(range(ncores)), **({"trace": True} if trace else {}))
    if trace:
        print("EXEC_NS", res.exec_time_ns, flush=True)
    outs = [np.asarray(r["out"]).reshape(D, T).T for r in res.results]
    return np.ascontiguousarray(np.stack(outs)).astype(np.float32)


def full_stages(NH=8):
    st = []
    for l in range(L):
        st.append(("setup", l))
        for hf in range(NH):
            st.append(("ffn", l, "ffn1", hf))
            st.append(("mix", l, hf))
            st.append(("ffn", l, "ffn2", hf))
    for hf in range(NH):
        st.append(("final", hf))
    return st


def kernel(**inputs):
    return run(inputs, full_stages(8), NH=8, ncores=2)
```

```python
import contextlib
import numpy as np
import concourse.bass as bass
import concourse.mybir as mybir
from concourse.bass_utils import run_bass_kernel_spmd

F32 = mybir.dt.float32
BF16 = mybir.dt.bfloat16
AF = mybir.ActivationFunctionType
ALU = mybir.AluOpType

D = 2048
DC = 16
HALF = 1024
NT = 512
DFF = 5632
FC = 44
L = 4
EPS = 1e-6
SEM_LIMIT = 12000
SAME_ENGINE_SYNC = True
GR = 64
ARENA = 53184


class V:
    def __init__(self, ap, lo, hi, es):
        self.ap = ap
        self.lo = lo
        self.hi = hi
        self.es = es

    def sl(self, e0, e1):
        lo = self.lo + (e0 * self.es) // 4
        hi = self.lo + (e1 * self.es + 3) // 4
        return V(self.ap[:, e0:e1], lo, hi, self.es)


class Op:
    __slots__ = ("eng", "fn", "reads", "writes", "dma_key", "deps", "needs_inc", "sem", "val", "idx")

    def __init__(self, eng, fn, reads, writes, dma_key):
        self.eng = eng
        self.fn = fn
        self.reads = reads
        self.writes = writes
        self.dma_key = dma_key
        self.deps = ()
        self.needs_inc = dma_key is not None
        self.sem = None
        self.val = 0


def _keys(items):
    out = []
    for it in items:
        if isinstance(it, V):
            for g in range(it.lo // GR, (it.hi + GR - 1) // GR):
                out.append(g)
        elif isinstance(it, (list, tuple)) and it and isinstance(it[0], V):
            out.extend(_keys(it))
        else:
            out.append(it)
    return out


class Prog:
    ENGS = ("pe", "act", "dve", "pool", "sp")

    def __init__(self):
        self.ops = []

    def add(self, eng, fn, reads=(), writes=(), dma_key=None):
        op = Op(eng, fn, _keys(reads), _keys(writes), dma_key)
        self.ops.append(op)
        return op

    def pe(self, fn, reads=(), writes=()):
        return self.add("pe", fn, reads, writes)

    def act(self, fn, reads=(), writes=()):
        return self.add("act", fn, reads, writes)

    def dve(self, fn, reads=(), writes=()):
        return self.add("dve", fn, reads, writes)

    def pool(self, fn, reads=(), writes=()):
        return self.add("pool", fn, reads, writes)

    def dma(self, eng, out, in_, reads, writes, key):
        return self.add(eng, lambda e: e.dma_start(out=out, in_=in_), reads, writes, dma_key=key)

    def analyze(self):
        last_w = {}
        readers = {}
        last_dma = {}
        for idx, op in enumerate(self.ops):
            op.idx = idx
            deps = {}
            for r in op.reads:
                w = last_w.get(r)
                if w is not None:
                    deps[id(w)] = w
            for r in op.writes:
                w = last_w.get(r)
                if w is not None:
                    deps[id(w)] = w
                rl = readers.get(r)
                if rl:
                    for rd in rl:
                        deps[id(rd)] = rd
            if op.dma_key is not None:
                p = last_dma.get(op.dma_key)
                if p is not None:
                    deps[id(p)] = p
                last_dma[op.dma_key] = op
            deps.pop(id(op), None)
            dl = []
            for d in deps.values():
                if d.dma_key is None and d.eng == op.eng:
                    if op.eng == "pe" or not SAME_ENGINE_SYNC:
                        continue
                dl.append(d)
            op.deps = dl
            for d in dl:
                d.needs_inc = True
            for r in op.writes:
                last_w[r] = op
                readers[r] = None
            for r in op.reads:
                rl = readers.get(r)
                if rl is None:
                    readers[r] = [op]
                elif rl[-1] is not op:
                    rl.append(op)

    def emit(self, nc, stack):
        self.analyze()
        cur = {}
        cnt = {}
        dsem = {}
        dcnt = {}
        nsem = [0]

        def newsem():
            nsem[0] += 1
            return stack.enter_context(nc.semaphore("s%d" % nsem[0]))

        for op in self.ops:
            if op.dma_key is not None:
                k = op.dma_key
                if k not in dsem or dcnt[k] + 16 > SEM_LIMIT:
                    dsem[k] = newsem()
                    dcnt[k] = 0
                dcnt[k] += 16
                op.sem = dsem[k]
                op.val = dcnt[k]
            elif op.needs_inc:
                e = op.eng
                if e not in cur or cnt[e] + 1 > SEM_LIMIT:
                    cur[e] = newsem()
                    cnt[e] = 0
                cnt[e] += 1
                op.sem = cur[e]
                op.val = cnt[e]
        self.nsem = nsem[0]
        per_eng = {e: [o for o in self.ops if o.eng == e] for e in self.ENGS}

        def run_engine(eng_obj, ops):
            waited = {}
            for op in ops:
                for d in op.deps:
                    key = id(d.sem)
                    if waited.get(key, 0) >= d.val:
                        continue
                    eng_obj.wait_ge(d.sem, d.val)
                    waited[key] = d.val
                ins = op.fn(eng_obj)
                if op.sem is not None:
                    ins.then_inc(op.sem, 16 if op.dma_key is not None else 1)

        with nc.Block() as block:
            @block.tensor
            def _(e):
                run_engine(e, per_eng["pe"])

            @block.scalar
            def _(e):
                run_engine(e, per_eng["act"])

            @block.vector
            def _(e):
                run_engine(e, per_eng["dve"])

            @block.gpsimd
            def _(e):
                run_engine(e, per_eng["pool"])

            @block.sync
            def _(e):
                run_engine(e, per_eng["sp"])


class Arena:
    def __init__(self, big):
        self.big = big

    def f32(self, off, n):
        assert off % GR == 0 and off + n <= ARENA, (off, n)
        return V(self.big[:, off:off + n], off, off + n, 4)

    def bf16(self, off, n):
        assert off % GR == 0 and n % 2 == 0 and off + n // 2 <= ARENA, (off, n)
        return V(self.big[:, off:off + n // 2].bitcast(BF16), off, off + n // 2, 2)


class Bump:
    def __init__(self, arena, lo, hi):
        self.a = arena
        self.lo = lo
        self.hi = hi
        self.o = lo

    def reset(self, to=None):
        self.o = self.lo if to is None else to

    def f32(self, n):
        v = self.a.f32(self.o, n)
        self.o += (n + GR - 1) // GR * GR
        assert self.o <= self.hi, (self.o, self.hi)
        return v

    def bf16(self, n):
        v = self.a.bf16(self.o, n)
        self.o += (n // 2 + GR - 1) // GR * GR
        assert self.o <= self.hi, (self.o, self.hi)
        return v


class ColPack:
    def __init__(self):
        self.cols = []
        self.index = {}
        self.n = 0

    def add(self, name, arr):
        arr = np.ascontiguousarray(arr, dtype=np.float32).reshape(128, -1)
        self.index[name] = (self.n, arr.shape[1])
        self.cols.append(arr)
        self.n += arr.shape[1]

    def build(self):
        return np.ascontiguousarray(np.concatenate(self.cols, axis=1))


def chunk_cols(v):
    v = np.asarray(v, dtype=np.float32)
    return np.ascontiguousarray(v.reshape(-1, 128).T)


def rep_rows(v):
    v = np.asarray(v, dtype=np.float32).reshape(1, -1)
    return np.ascontiguousarray(np.repeat(v, 128, axis=0))


def make_packs(inp):
    cp = ColPack()
    cp.add("eps", np.full((128, 1), EPS, np.float32))
    cp.add("one", np.full((128, 1), 1.0, np.float32))
    cp.add("final_norm", chunk_cols(inp["final_norm"]))
    for l in range(L):
        for nm in ("ffn1_norm", "ffn2_norm", "mix_norm"):
            cp.add("%s_%d" % (nm, l), chunk_cols(inp[nm][l]))
        cp.add("pool_scale_%d" % l, chunk_cols(inp["pool_scale"][l]))
        w = np.asarray(inp["conv_dw_w"][l])
        cp.add("cdw_w_%d" % l, w.reshape(31, 4, 128).transpose(2, 1, 0).reshape(128, 124))
        cp.add("cdw_b_%d" % l, chunk_cols(inp["conv_dw_b"][l]))
        cp.add("cln_g_%d" % l, chunk_cols(inp["conv_ln_g"][l]))
        cp.add("cln_b_%d" % l, chunk_cols(inp["conv_ln_b"][l]))
        cp.add("cpw_b_%d" % l, chunk_cols(inp["conv_pw_b"][l]))
        cp.add("sln_g_%d" % l, chunk_cols(inp["sgu_ln_g"][l]))
        cp.add("sln_b_%d" % l, chunk_cols(inp["sgu_ln_b"][l]))
        w = np.asarray(inp["ssm_conv_w"][l])
        cp.add("scw_%d" % l, w.reshape(4, 8, 128).transpose(2, 1, 0).reshape(128, 32))
        cp.add("scb_%d" % l, chunk_cols(inp["ssm_conv_b"][l]))
        cp.add("snorm_%d" % l, chunk_cols(inp["ssm_norm"][l]))
    rp = ColPack()
    k = np.arange(16)
    invc = np.stack([1.0 / np.minimum(k + 1, w) for w in (2, 4, 8, 16)]).astype(np.float32)
    rp.add("invc", rep_rows(invc))
    for l in range(L):
        rp.add("dtb_%d" % l, rep_rows(inp["ssm_dt_bias"][l]))
        rp.add("alog_%d" % l, rep_rows(inp["ssm_a_log"][l]))
        rp.add("dskip_%d" % l, rep_rows(inp["ssm_d"][l]))
        rp.add("sgub_%d" % l, rep_rows(inp["sgu_b"][l]))
    return cp, rp


LW_POOL, LW_PW, LW_SGU, LW_DT = 0, 512, 2560, 3072
LW_N = 3200


def prep_weights(inp):
    out = {}
    for nm in ("ffn1", "ffn2"):
        wg = np.asarray(inp[nm + "_w_gate"]).reshape(L, DC, 128, FC, 128)
        wu = np.asarray(inp[nm + "_w_up"]).reshape(L, DC, 128, FC, 128)
        wgu = np.empty((L, FC, 128, 2, DC, 128), np.float32)
        wgu[:, :, :, 0] = wg.transpose(0, 3, 2, 1, 4)
        wgu[:, :, :, 1] = wu.transpose(0, 3, 2, 1, 4)
        out[nm + "_wgu"] = wgu.reshape(L, FC, 128, 2 * DC * 128)
        wd = np.asarray(inp[nm + "_w_down"]).reshape(L, 2, 22, 128, DC, 128)
        out[nm + "_wd"] = np.ascontiguousarray(wd.transpose(0, 4, 1, 3, 2, 5)).reshape(L, DC, 2, 128, 22 * 128)
    win = np.asarray(inp["w_in"])
    w32 = win[:, :, :4096].reshape(L, DC, 128, 32, 128)
    out["w_in"] = np.ascontiguousarray(w32.transpose(0, 3, 2, 1, 4)).reshape(L, 32, 128, DC * 128)
    wo = np.asarray(inp["w_out"]).reshape(L, DC, 128, DC, 128)
    out["w_out"] = np.ascontiguousarray(wo.transpose(0, 3, 2, 1, 4)).reshape(L, DC, 128, DC * 128)
    lw = np.empty((L, 128, LW_N), np.float32)
    lw[:, :, LW_POOL:LW_POOL + 512] = np.asarray(inp["pool_w"]).transpose(0, 2, 1, 3).reshape(L, 128, 512)
    lw[:, :, LW_PW:LW_PW + 2048] = np.asarray(inp["conv_pw_w"]).reshape(L, 4, 128, 512).transpose(0, 2, 1, 3).reshape(L, 128, 2048)
    lw[:, :, LW_SGU:LW_SGU + 512] = np.asarray(inp["sgu_w_s"]).transpose(0, 3, 1, 2).reshape(L, 128, 512)
    lw[:, :, LW_DT:LW_DT + 128] = win[:, :, 4096:4104].reshape(L, DC, 128, 8).transpose(0, 2, 1, 3).reshape(L, 128, 128)
    out["lw"] = lw
    return out


def make_cmat():
    c = np.zeros((128, 512), np.float32)
    c[:, 0:128] = 1.0
    c[:, 128:256] = np.eye(128, dtype=np.float32)
    k = np.arange(128)
    c[:, 256:384] = (k[:, None] <= k[None, :]).astype(np.float32)
    c[:, 384:512] = np.where(k[:, None] > k[None, :], -30000.0, 0.0)
    return c


class Builder:
    def __init__(self, stages, NH, cp, rp, skip_copy=False):
        self.skip_copy = skip_copy
        self.stages = stages
        self.NH = NH
        self.T = NH * HALF
        self.ci = cp.index
        self.ri = rp.index
        self.ncp = (cp.n + GR - 1) // GR * GR
        self.nrp = (rp.n + GR - 1) // GR * GR
        nc = bass.Bass("TRN2", target_bir_lowering=False)
        self.nc = nc
        self.P = Prog()
        dt = nc.dram_tensor
        T = self.T
        self.xin = dt("xin", [DC, 128, T], F32, kind="ExternalInput").ap()
        self.out = dt("out", [DC, 128, T], F32, kind="ExternalOutput").ap()
        self.xt = dt("xt", [DC, 128, T], F32).ap()
        self.cols_d = dt("cols", [128, cp.n], F32, kind="ExternalInput").ap()
        self.rows_d = dt("rows", [128, rp.n], F32, kind="ExternalInput").ap()
        self.cmat_d = dt("cmat", [128, 512], F32, kind="ExternalInput").ap()
        self.w = {}
        for nm in ("ffn1", "ffn2"):
            if not any(st[0] == "ffn" and st[2] == nm for st in stages):
                continue
            self.w[nm + "_wgu"] = dt(nm + "_wgu", [L, FC, 128, 2 * DC * 128], F32, kind="ExternalInput").ap()
            self.w[nm + "_wd"] = dt(nm + "_wd", [L, DC, 2, 128, 22 * 128], F32, kind="ExternalInput").ap()
        self.w["w_in"] = dt("w_in", [L, 32, 128, DC * 128], F32, kind="ExternalInput").ap()
        self.w["w_out"] = dt("w_out", [L, DC, 128, DC * 128], F32, kind="ExternalInput").ap()
        self.w["lw"] = dt("lw", [L, 128, LW_N], F32, kind="ExternalInput").ap()

    def plan(self, stack):
        nc = self.nc
        big = stack.enter_context(nc.sbuf_tensor("arena", [128, ARENA], F32))
        A = Arena(big)
        self.A = A
        pb = Bump(A, 0, ARENA)
        self.cols = pb.f32(self.ncp)
        self.rows = pb.f32(self.nrp)
        self.cmat = pb.f32(512)
        c = self.cmat.ap
        self.ones, self.ident, self.U, self.NEG = c[:, 0:128], c[:, 128:256], c[:, 256:384], c[:, 384:512]
        self.LWB = pb.bf16(LW_N)
        self.HALO_A = [pb.f32(16) for _ in range(4)]
        self.HALO_D = [pb.f32(4) for _ in range(8)]
        self.HST = pb.f32(512)
        self.HSTb = pb.bf16(512)
        self.ABC = pb.f32(8)
        self.ONESB = pb.bf16(128)
        self.HALO_Bb = [pb.bf16(32) for _ in range(4)]
        self.H = pb.bf16(DC * HALF)
        self.Hc = [self.H.sl(i * HALF, (i + 1) * HALF) for i in range(DC)]
        rlo = pb.o
        self.ACTb = pb.bf16(FC * HALF)
        self.R = Bump(A, rlo, pb.o)
        self.XH = A.f32(rlo, DC * HALF)
        self.XHc = [self.XH.sl(i * HALF, (i + 1) * HALF) for i in range(DC)]
        self.SQ = [A.bf16(rlo + DC * HALF + k * HALF, HALF) for k in range(2)]
        self.RSTD = A.f32(rlo + DC * HALF + 2 * HALF, HALF)
        self.STMP = A.f32(rlo + DC * HALF + 3 * HALF, HALF)
        self.WST = [pb.f32(4096) for _ in range(2)]
        self.WBF = [pb.bf16(4096) for _ in range(2)]
        self.XS = [pb.f32(NT) for _ in range(4)]
        self.SG = [pb.bf16(NT) for _ in range(2)]
        self.ps = [stack.enter_context(nc.psum_tensor("ps%d" % b, [128, 512], F32)) for b in range(8)]

    def col(self, name, k=0, n=1):
        off, w = self.ci[name]
        assert k + n <= w, (name, k, n, w)
        return self.cols.ap[:, off + k: off + k + n]

    def row(self, name, k=0, n=None):
        off, w = self.ri[name]
        n = w - k if n is None else n
        assert k + n <= w
        return self.rows.ap[:, off + k: off + k + n]

    def load_consts(self):
        P = self.P
        P.dma("sp", self.cols.ap[:, 0:self.cols_d.shape[1]], self.cols_d, [], [self.cols, "cols"], "c0")
        P.dma("sp", self.rows.ap[:, 0:self.rows_d.shape[1]], self.rows_d, [], [self.rows, "rows"], "c1")
        P.dma("sp", self.cmat.ap, self.cmat_d, [], [self.cmat, "cmat"], "c2")
        P.dve(lambda e: e.tensor_copy(out=self.ONESB.ap, in_=self.ones), ["cmat"], [self.ONESB, "onesb"])

    def copy_x_in(self):
        P = self.P
        for i in range(DC):
            P.dma("pool", self.xt[i], self.xin[i], [], [("XT", i, gt) for gt in range(2 * self.NH)], ("xcp", i % 4))

    def dump(self):
        P = self.P
        for i in range(DC):
            P.dma("pool", self.out[i], self.xt[i], [("XT", i, gt) for gt in range(2 * self.NH)], [("OUT", i)], ("xcp", i % 4))

    def rms_stats(self, hf, src=None):
        P = self.P
        t0 = hf * HALF
        for i in range(DC):
            if src is None:
                P.dma("pool", self.XHc[i].ap, self.xt[i, :, t0:t0 + HALF],
                      [("XT", i, 2 * hf), ("XT", i, 2 * hf + 1)], [self.XHc[i]], ("xh", i % 4))
            else:
                P.dma("pool", self.XHc[i].ap, src[i, :, t0:t0 + HALF], [], [self.XHc[i]], ("xh", i % 4))
        for i in range(DC):
            sq = self.SQ[i % 2]
            P.act(lambda e, i=i, sq=sq: e.activation(out=sq.ap, in_=self.XHc[i].ap, func=AF.Square),
                  [self.XHc[i]], [sq])
            for tt in range(2):
                P.pe(lambda e, i=i, sq=sq, tt=tt: e.matmul(self.ps[tt][:, :], self.ONESB.ap, sq.ap[:, tt * NT:(tt + 1) * NT],
                                                      start=(i == 0), stop=(i == DC - 1)),
                     [sq, "onesb"], [("ps", tt)])
        for tt in range(2):
            st = self.STMP.sl(tt * NT, (tt + 1) * NT)
            rs = self.RSTD.sl(tt * NT, (tt + 1) * NT)
            P.act(lambda e, tt=tt, st=st: e.activation(out=st.ap, in_=self.ps[tt][:, :], func=AF.Sqrt,
                                                      bias=self.col("eps"), scale=1.0 / D),
                  [("ps", tt), "cols"], [st])
            P.dve(lambda e, st=st, rs=rs: e.reciprocal(out=rs.ap, in_=st.ap), [st], [rs])

    def norm_pass(self, gname, hf, src=None):
        P = self.P
        self.rms_stats(hf, src)
        for i in range(DC):
            P.dve(lambda e, i=i: e.scalar_tensor_tensor(out=self.Hc[i].ap, in0=self.XHc[i].ap,
                                                       scalar=self.col(gname, i), in1=self.RSTD.ap,
                                                       op0=ALU.mult, op1=ALU.mult),
                  [self.XHc[i], self.RSTD, "cols"], [self.Hc[i]])

    def final_norm(self, hf):
        P = self.P
        t0 = hf * HALF
        self.rms_stats(hf)
        for i in range(DC):
            P.dve(lambda e, i=i: e.scalar_tensor_tensor(out=self.XHc[i].ap, in0=self.XHc[i].ap,
                                                       scalar=self.col("final_norm", i), in1=self.RSTD.ap,
                                                       op0=ALU.mult, op1=ALU.mult),
                  [self.XHc[i], self.RSTD, "cols"], [self.XHc[i]])
            P.dma("pool", self.out[i, :, t0:t0 + HALF], self.XHc[i].ap, [self.XHc[i]], [("OUT", i, hf)], ("xh", i % 4))

    def residual_pair(self, i, hf, banks, factor, src=None):
        P = self.P
        ks = (self.xslot, self.xslot + 1)
        self.xslot = (self.xslot + 2) % 4
        drs = [self.xt[i, :, (2 * hf + tt) * NT:(2 * hf + tt + 1) * NT] for tt in range(2)]
        for tt in range(2):
            if src is None:
                P.dma("pool", self.XS[ks[tt]].ap, drs[tt], [("XT", i, 2 * hf + tt)], [self.XS[ks[tt]]], ("xs", ks[tt]))
            else:
                P.dma("pool", self.XS[ks[tt]].ap, src[i, :, (2 * hf + tt) * NT:(2 * hf + tt + 1) * NT], [], [self.XS[ks[tt]]], ("xs", ks[tt]))
        for tt in range(2):
            xs = self.XS[ks[tt]]
            P.dve(lambda e, xs=xs, bank=banks[tt]: e.scalar_tensor_tensor(out=xs.ap, in0=self.ps[bank][:, :], scalar=float(factor),
                                                                       in1=xs.ap, op0=ALU.mult, op1=ALU.add),
                  [("ps", banks[tt]), xs], [xs])
            P.dma("pool", drs[tt], xs.ap, [xs], [("XT", i, 2 * hf + tt)], ("xs", ks[tt]))

    def residual_update(self, i, gt, bank, factor):
        P = self.P
        k = self.xslot
        self.xslot = (self.xslot + 1) % 4
        xs = self.XS[k]
        dr = self.xt[i, :, gt * NT:(gt + 1) * NT]
        P.dma("pool", xs.ap, dr, [("XT", i, gt)], [xs], ("xs", k))
        P.dve(lambda e, xs=xs, bank=bank: e.scalar_tensor_tensor(out=xs.ap, in0=self.ps[bank][:, :], scalar=float(factor),
                                                               in1=xs.ap, op0=ALU.mult, op1=ALU.add),
              [("ps", bank), xs], [xs])
        P.dma("pool", dr, xs.ap, [xs], [("XT", i, gt)], ("xs", k))

    def stream_w(self, src, ncols, split=(0.5, 0.75)):
        P = self.P
        s = self.wslot
        self.wslot ^= 1
        st = self.WST[s].sl(0, ncols)
        wb = self.WBF[s].sl(0, ncols)
        P.dma("sp", st.ap, src, [], [st], ("wst", s))
        a = int(ncols * split[0]) // 128 * 128
        b = int(ncols * split[1]) // 128 * 128
        if a > 0:
            P.act(lambda e: e.activation(out=wb.ap[:, 0:a], in_=st.ap[:, 0:a], func=AF.Identity), [st.sl(0, a)], [wb.sl(0, a)])
        if b > a:
            P.pool(lambda e: e.tensor_copy(out=wb.ap[:, a:b], in_=st.ap[:, a:b]), [st.sl(a, b)], [wb.sl(a, b)])
        if ncols > b:
            P.dve(lambda e: e.tensor_copy(out=wb.ap[:, b:ncols], in_=st.ap[:, b:ncols]), [st.sl(b, ncols)], [wb.sl(b, ncols)])
        return wb

    def ffn(self, l, nm, hf):
        P = self.P
        wgu = self.w[nm + "_wgu"]
        wd = self.w[nm + "_wd"]
        xsrc = self.xin if (self.skip_copy and l == 0 and nm == "ffn1") else None
        self.norm_pass("%s_norm_%d" % (nm, l), hf, xsrc)
        ACTc = [self.ACTb.sl(j * HALF, (j + 1) * HALF) for j in range(FC)]
        tasks = [("gu", j, 0) for j in range(FC)] + [("dn", i, fh) for i in range(DC) for fh in range(2)]

        def fetch(t):
            kind, a, b = tasks[t]
            if kind == "gu":
                return self.stream_w(wgu[l, a], 4096)
            return self.stream_w(wd[l, a, b], 2816, split=(1.0, 1.0))

        nxt = fetch(0)
        for t, (kind, a, b) in enumerate(tasks):
            wb = nxt
            if t + 1 < len(tasks):
                nxt = fetch(t + 1)
            if kind == "gu":
                j = a
                wv = wb.ap.rearrange("p (g c f) -> p g c f", g=2, c=DC)
                for tt in range(2):
                    bg = (j % 2) * 4 + tt
                    bu = (j % 2) * 4 + 2 + tt
                    sl = slice(tt * NT, (tt + 1) * NT)

                    def mm_chain(e, g, bank, wv=wv, sl=sl):
                        ins = None
                        for c in range(DC):
                            ins = e.matmul(self.ps[bank][:, :], wv[:, g, c, :], self.Hc[c].ap[:, sl],
                                           start=(c == 0), stop=(c == DC - 1))
                        return ins
                    P.pe(lambda e, f=mm_chain, b_=bg: f(e, 0, b_), [wb.sl(0, 2048), self.H], [("ps", bg)])
                    P.pe(lambda e, f=mm_chain, b_=bu: f(e, 1, b_), [wb.sl(2048, 4096), self.H], [("ps", bu)])
                    sg = self.SG[tt]
                    P.act(lambda e, sg=sg, bg=bg: e.activation(out=sg.ap, in_=self.ps[bg][:, :], func=AF.Silu),
                          [("ps", bg)], [sg])
                    dst = ACTc[j].sl(tt * NT, (tt + 1) * NT)
                    P.dve(lambda e, sg=sg, bu=bu, dst=dst: e.tensor_tensor(out=dst.ap, in0=self.ps[bu][:, :], in1=sg.ap, op=ALU.mult),
                          [("ps", bu), sg], [dst])
            else:
                i, fh = a, b
                wv = wb.ap.rearrange("p (c f) -> p c f", c=22)
                for tt in range(2):
                    bank = (i % 2) * 2 + tt
                    sl = slice(tt * NT, (tt + 1) * NT)

                    def mm_chain(e, wv=wv, sl=sl, bank=bank, fh=fh):
                        ins = None
                        for c in range(22):
                            ins = e.matmul(self.ps[bank][:, :], wv[:, c, :], ACTc[fh * 22 + c].ap[:, sl],
                                           start=(fh == 0 and c == 0), stop=(fh == 1 and c == 21))
                        return ins
                    P.pe(mm_chain, [wb] + [ACTc[fh * 22 + c].sl(tt * NT, (tt + 1) * NT) for c in range(22)], [("ps", bank)])
                if fh == 1:
                    self.residual_pair(i, hf, ((i % 2) * 2, (i % 2) * 2 + 1), 0.5, xsrc)

    def layer_setup(self, l):
        P = self.P
        s = self.wslot
        self.wslot ^= 1
        st = self.WST[s].sl(0, LW_N)
        P.dma("sp", st.ap, self.w["lw"][l], [], [st], ("wst", s))
        lw = self.LWB
        for (a, b) in ((LW_POOL, LW_PW + 2048), (LW_DT, LW_N)):
            P.dve(lambda e, a=a, b=b: e.tensor_copy(out=lw.ap[:, a:b], in_=st.ap[:, a:b]), [st], [lw.sl(a, b)])
        P.dve(lambda e: e.tensor_tensor(out=lw.ap[:, LW_SGU:LW_SGU + 512].rearrange("p (h t) -> p h t", h=4),
                                        in0=st.ap[:, LW_SGU:LW_SGU + 512].rearrange("p (h t) -> p h t", h=4),
                                        in1=self.U.rearrange("p (o t) -> p o t", o=1).broadcast_to([128, 4, 128]),
                                        op=ALU.mult),
              [st, "cmat"], [lw.sl(LW_SGU, LW_SGU + 512)])
        P.act(lambda e: e.activation(out=self.ABC.ap, in_=self.row("alog_%d" % l), func=AF.Exp), ["rows"], [self.ABC])
        P.dve(lambda e: e.tensor_scalar(out=self.ABC.ap, in0=self.ABC.ap, scalar1=-1.0, scalar2=None, op0=ALU.mult),
              [self.ABC], [self.ABC])
        P.dve(lambda e: e.memset(self.HST.ap, 0.0), [], [self.HST])
        P.dve(lambda e: e.memset(self.HSTb.ap, 0.0), [], [self.HSTb])

    def start_wseq(self, l, groups):
        order = []
        if "a" in groups:
            order += [0, 1, 2, 3]
        if "b" in groups:
            for cc in range(4):
                order += [4 + cc, 8 + cc]
        if "c" in groups:
            order += list(range(12, 20))
        if "d" in groups:
            order += list(range(20, 32))
        self.wseq = [("in", c) for c in order] + [("out", i) for i in range(DC)]
        self.wl = l
        self.wpos = 0
        self.wpending = self._fetch_w(0)

    def _fetch_w(self, pos):
        kind, c = self.wseq[pos]
        if kind == "in":
            return self.stream_w(self.w["w_in"][self.wl, c], 2048, split=(0.375, 1.0))
        return self.stream_w(self.w["w_out"][self.wl, c], 2048, split=(1.0, 1.0))

    def next_w(self, tag):
        assert self.wseq[self.wpos] == tag, (self.wseq[self.wpos], tag)
        wb = self.wpending
        self.wpos += 1
        if self.wpos < len(self.wseq):
            self.wpending = self._fetch_w(self.wpos)
        return wb

    def proj(self, l, c):
        P = self.P
        wb = self.next_w(("in", c))
        wv = wb.ap.rearrange("p (c f) -> p c f", c=DC)
        pb = self.pbank
        self.pbank ^= 1
        banks = (pb * 2, pb * 2 + 1)
        for tt in range(2):
            sl = slice(tt * NT, (tt + 1) * NT)

            def mm_chain(e, wv=wv, sl=sl, bank=banks[tt]):
                ins = None
                for c2 in range(DC):
                    ins = e.matmul(self.ps[bank][:, :], wv[:, c2, :], self.Hc[c2].ap[:, sl], start=(c2 == 0), stop=(c2 == DC - 1))
                return ins
            P.pe(mm_chain, [wb, self.H], [("ps", banks[tt])])
        return banks

    def ln_stats(self, X, SQt, MEAN, RSTD, TMP):
        P = self.P
        for cc in range(4):
            P.act(lambda e, cc=cc: e.activation(out=SQt.ap, in_=X[cc].ap, func=AF.Square), [X[cc]], [SQt])
            for tt in range(2):
                sl = slice(tt * NT, (tt + 1) * NT)
                P.pe(lambda e, cc=cc, tt=tt, sl=sl: e.matmul(self.ps[4 + tt][:, :], self.ones, X[cc].ap[:, sl], start=(cc == 0), stop=(cc == 3)),
                     [X[cc], "cmat"], [("ps", 4 + tt)])
                P.pe(lambda e, cc=cc, tt=tt, sl=sl: e.matmul(self.ps[6 + tt][:, :], self.ones, SQt.ap[:, sl], start=(cc == 0), stop=(cc == 3)),
                     [SQt, "cmat"], [("ps", 6 + tt)])
        for tt in range(2):
            a, b = tt * NT, (tt + 1) * NT
            me, rs, tm = MEAN.sl(a, b), RSTD.sl(a, b), TMP.sl(a, b)
            P.dve(lambda e, tt=tt, me=me: e.tensor_scalar(out=me.ap, in0=self.ps[4 + tt][:, :], scalar1=1.0 / 512, scalar2=None, op0=ALU.mult),
                  [("ps", 4 + tt)], [me])
            P.dve(lambda e, me=me, tm=tm: e.tensor_tensor(out=tm.ap, in0=me.ap, in1=me.ap, op=ALU.mult), [me], [tm])
            P.dve(lambda e, tt=tt, tm=tm, rs=rs: e.scalar_tensor_tensor(out=rs.ap, in0=self.ps[6 + tt][:, :], scalar=1.0 / 512, in1=tm.ap,
                                                                    op0=ALU.mult, op1=ALU.subtract),
                  [("ps", 6 + tt), tm], [rs])
            P.act(lambda e, rs=rs: e.activation(out=rs.ap, in_=rs.ap, func=AF.Sqrt, bias=self.col("eps"), scale=1.0), [rs, "cols"], [rs])
            P.dve(lambda e, rs=rs: e.reciprocal(out=rs.ap, in_=rs.ap), [rs], [rs])

    def ln_apply(self, x, MEAN, RSTD, TMP, out, func, gcol, bcol):
        P = self.P
        P.dve(lambda e: e.tensor_tensor(out=TMP.ap, in0=x.ap, in1=MEAN.ap, op=ALU.subtract), [x, MEAN], [TMP])
        P.dve(lambda e: e.tensor_tensor(out=TMP.ap, in0=TMP.ap, in1=RSTD.ap, op=ALU.mult), [TMP, RSTD], [TMP])
        P.act(lambda e: e.activation(out=out.ap, in_=TMP.ap, func=func, bias=bcol, scale=gcol), [TMP, "cols"], [out])

    def mixer(self, l, hf, groups="abcd"):
        P = self.P
        A = self.A
        self.norm_pass("mix_norm_%d" % l, hf)
        self.start_wseq(l, groups)
        R = self.R
        R.reset()
        Y = R.bf16(DC * HALF)
        Yc = [Y.sl(c * HALF, (c + 1) * HALF) for c in range(DC)]
        base = R.o
        lw = self.LWB
        POOLW = lw.ap[:, LW_POOL:LW_POOL + 512].rearrange("p (g d) -> p g d", g=4)
        PWW = lw.ap[:, LW_PW:LW_PW + 2048].rearrange("p (c d) -> p c d", c=4)
        SGW = lw.ap[:, LW_SGU:LW_SGU + 512].rearrange("p (h t) -> p h t", h=4)
        WDT = lw.ap[:, LW_DT:LW_DT + 128].rearrange("p (c m) -> p c m", c=DC)
        T2 = (slice(0, NT), slice(NT, 2 * NT))

        def zero_y(c0, c1):
            for c in range(c0, c1):
                P.dve(lambda e, c=c: e.memset(Yc[c].ap, 0.0), [], [Yc[c]])

        if "a" in groups:
            AB = [R.f32(1040) for _ in range(4)]
            T = [R.f32(1040), R.f32(1040)]
            PB = [R.bf16(HALF) for _ in range(4)]
            TM = R.f32(16)
            for g in range(4):
                banks = self.proj(l, g)
                ab = AB[g]
                if hf == 0:
                    P.dve(lambda e, ab=ab: e.memset(ab.ap[:, 0:16], 0.0), [], [ab.sl(0, 16)])
                else:
                    P.dve(lambda e, ab=ab, g=g: e.tensor_copy(out=ab.ap[:, 0:16], in_=self.HALO_A[g].ap), [self.HALO_A[g]], [ab.sl(0, 16)])
                for tt in range(2):
                    d = ab.sl(16 + tt * NT, 16 + (tt + 1) * NT)
                    P.act(lambda e, d=d, b=banks[tt]: e.activation(out=d.ap, in_=self.ps[b][:, :], func=AF.Identity), [("ps", banks[tt])], [d])
                P.dve(lambda e, ab=ab, g=g: e.tensor_copy(out=self.HALO_A[g].ap, in_=ab.ap[:, 1024:1040]), [ab.sl(1024, 1040)], [self.HALO_A[g]])
                cur = ab
                for lev in range(g + 1):
                    k = 1 << lev
                    s0 = 2 * k - 1
                    dst = T[lev % 2]
                    P.dve(lambda e, cur=cur, dst=dst, k=k, s0=s0: e.tensor_tensor(out=dst.ap[:, s0:1040], in0=cur.ap[:, s0:1040],
                                                                               in1=cur.ap[:, s0 - k:1040 - k], op=ALU.add),
                          [cur], [dst])
                    cur = dst
                wdt = float(2 << g)
                P.dve(lambda e, cur=cur, ab=ab, g=g, wdt=wdt: e.scalar_tensor_tensor(out=PB[g].ap, in0=cur.ap[:, 16:1040], scalar=1.0 / wdt,
                                                                                  in1=ab.ap[:, 16:1040], op0=ALU.mult, op1=ALU.subtract),
                      [cur, ab], [PB[g]])
                if hf == 0:
                    P.dve(lambda e, cur=cur, g=g: e.tensor_tensor(out=TM.ap, in0=cur.ap[:, 16:32], in1=self.row("invc", g * 16, 16), op=ALU.mult),
                          [cur, "rows"], [TM])
                    P.dve(lambda e, ab=ab, g=g: e.tensor_tensor(out=PB[g].ap[:, 0:16], in0=TM.ap, in1=ab.ap[:, 16:32], op=ALU.subtract),
                          [TM, ab], [PB[g]])
                for tt in range(2):
                    P.pe(lambda e, g=g, tt=tt: e.matmul(self.ps[4 + tt][:, :], POOLW[:, g, :], PB[g].ap[:, T2[tt]], start=True, stop=True),
                         [PB[g], lw], [("ps", 4 + tt)])
                    d = Yc[g].sl(tt * NT, (tt + 1) * NT)
                    P.dve(lambda e, d=d, g=g, tt=tt: e.tensor_scalar(out=d.ap, in0=self.ps[4 + tt][:, :], scalar1=self.col("pool_scale_%d" % l, g),
                                                                  scalar2=None, op0=ALU.mult),
                          [("ps", 4 + tt), "cols"], [d])
        else:
            zero_y(0, 4)

        if "b" in groups:
            R.reset(base)
            HG = [R.bf16(1056) for _ in range(4)]
            CV = [R.f32(HALF) for _ in range(4)]
            SQt, MEAN, RSTD, TMP = R.f32(HALF), R.f32(HALF), R.f32(HALF), R.f32(HALF)
            SIGT = R.f32(NT)
            DG = [R.bf16(128) for _ in range(6)]
            LNS = [R.bf16(HALF) for _ in range(4)]
            ndg = 0
            for cc in range(4):
                bv = self.proj(l, 4 + cc)
                bgt = self.proj(l, 8 + cc)
                hg = HG[cc]
                if hf == 0:
                    P.dve(lambda e, hg=hg: e.memset(hg.ap[:, 0:32], 0.0), [], [hg.sl(0, 32)])
                else:
                    P.dve(lambda e, hg=hg, cc=cc: e.tensor_copy(out=hg.ap[:, 0:32], in_=self.HALO_Bb[cc].ap), [self.HALO_Bb[cc]], [hg.sl(0, 32)])
                for tt in range(2):
                    P.act(lambda e, b=bgt[tt]: e.activation(out=SIGT.ap, in_=self.ps[b][:, :], func=AF.Sigmoid), [("ps", bgt[tt])], [SIGT])
                    d = hg.sl(32 + tt * NT, 32 + (tt + 1) * NT)
                    P.dve(lambda e, d=d, b=bv[tt]: e.tensor_tensor(out=d.ap, in0=self.ps[b][:, :], in1=SIGT.ap, op=ALU.mult),
                          [("ps", bv[tt]), SIGT], [d])
                P.dve(lambda e, hg=hg, cc=cc: e.tensor_copy(out=self.HALO_Bb[cc].ap, in_=hg.ap[:, 1024:1056]), [hg.sl(1024, 1056)], [self.HALO_Bb[cc]])
                cb = (4 + (cc % 2) * 2, 5 + (cc % 2) * 2)
                for k in range(31):
                    dg = DG[ndg % 6]
                    ndg += 1
                    P.dve(lambda e, dg=dg, cc=cc, k=k: e.tensor_scalar(out=dg.ap, in0=self.ident, scalar1=self.col("cdw_w_%d" % l, cc * 31 + k),
                                                                   scalar2=None, op0=ALU.mult),
                          ["cmat", "cols"], [dg])

                    def tap(e, dg=dg, hg=hg, k=k, cb=cb):
                        ins = None
                        for tt in range(2):
                            ins = e.matmul(self.ps[cb[tt]][:, :], dg.ap, hg.ap[:, 2 + k + tt * NT:2 + k + (tt + 1) * NT], start=(k == 0), stop=(k == 30))
                        return ins
                    P.pe(tap, [dg, hg], [("ps", cb[0]), ("ps", cb[1])])
                cv = CV[cc]
                for tt in range(2):
                    d = cv.sl(tt * NT, (tt + 1) * NT)
                    P.act(lambda e, d=d, b=cb[tt], cc=cc: e.activation(out=d.ap, in_=self.ps[b][:, :], func=AF.Identity,
                                                                    bias=self.col("cdw_b_%d" % l, cc), scale=1.0),
                          [("ps", cb[tt]), "cols"], [d])
            self.ln_stats(CV, SQt, MEAN, RSTD, TMP)
            for cc in range(4):
                self.ln_apply(CV[cc], MEAN, RSTD, TMP, LNS[cc], AF.Silu, self.col("cln_g_%d" % l, cc), self.col("cln_b_%d" % l, cc))
            for dd in range(4):
                for tt in range(2):
                    bank = 4 + (dd % 2) * 2 + tt

                    def pw_chain(e, dd=dd, tt=tt, bank=bank):
                        ins = None
                        for cc in range(4):
                            ins = e.matmul(self.ps[bank][:, :], PWW[:, cc, dd * 128:(dd + 1) * 128], LNS[cc].ap[:, T2[tt]], start=(cc == 0), stop=(cc == 3))
                        return ins
                    P.pe(pw_chain, LNS + [lw], [("ps", bank)])
                    d = Yc[4 + dd].sl(tt * NT, (tt + 1) * NT)
                    P.dve(lambda e, d=d, bank=bank, dd=dd: e.tensor_scalar(out=d.ap, in0=self.ps[bank][:, :], scalar1=self.col("cpw_b_%d" % l, dd),
                                                                        scalar2=None, op0=ALU.add),
                          [("ps", bank), "cols"], [d])
        else:
            zero_y(4, 8)

        if "c" in groups:
            R.reset(base)
            G1, G2, G3, G4 = R.f32(HALF), R.f32(HALF), R.f32(HALF), R.f32(HALF)
            Ub = [R.bf16(HALF) for _ in range(4)]
            Vf = [R.f32(HALF) for _ in range(4)]
            VT = [R.bf16(NT) for _ in range(8)]
            GT = R.f32(NT)

            def gelu(banks, dst):
                for tt in range(2):
                    d = G1.sl(tt * NT, (tt + 1) * NT)
                    P.act(lambda e, d=d, b=banks[tt]: e.activation(out=d.ap, in_=self.ps[b][:, :], func=AF.Identity), [("ps", banks[tt])], [d])
                P.act(lambda e: e.activation(out=G2.ap, in_=G1.ap, func=AF.Square), [G1], [G2])
                P.dve(lambda e: e.tensor_scalar(out=G2.ap, in0=G2.ap, scalar1=0.044715, scalar2=1.0, op0=ALU.mult, op1=ALU.add), [G2], [G2])
                P.dve(lambda e: e.tensor_tensor(out=G2.ap, in0=G2.ap, in1=G1.ap, op=ALU.mult), [G1, G2], [G2])
                P.act(lambda e: e.activation(out=G2.ap, in_=G2.ap, func=AF.Sigmoid, scale=1.5957691216057308), [G2], [G2])
                P.dve(lambda e: e.tensor_tensor(out=dst.ap, in0=G1.ap, in1=G2.ap, op=ALU.mult), [G1, G2], [dst])

            for hd in range(4):
                gelu(self.proj(l, 12 + hd), Ub[hd])
            for hd in range(4):
                gelu(self.proj(l, 16 + hd), Vf[hd])
            self.ln_stats(Vf, G1, G2, G3, G4)
            for hd in range(4):
                self.ln_apply(Vf[hd], G2, G3, G4, Vf[hd], AF.Identity, self.col("sln_g_%d" % l, hd), self.col("sln_b_%d" % l, hd))
            for q in range(8):
                bank = 4 + (q % 2)

                def tr4(e, q=q, bank=bank):
                    ins = None
                    for hd in range(4):
                        ins = e.transpose(out=self.ps[bank][:, hd * 128:(hd + 1) * 128], in_=Vf[hd].ap[:, q * 128:(q + 1) * 128], identity=self.ident)
                    return ins
                P.pe(tr4, [v.sl(q * 128, (q + 1) * 128) for v in Vf], [("ps", bank)])
                P.act(lambda e, q=q, bank=bank: e.activation(out=VT[q].ap, in_=self.ps[bank][:, :], func=AF.Identity), [("ps", bank)], [VT[q]])
            for tt in range(2):
                for hd in range(4):
                    bank = 6 + (hd % 2)

                    def sp4(e, tt=tt, hd=hd, bank=bank):
                        ins = None
                        for qq in range(4):
                            ins = e.matmul(self.ps[bank][:, qq * 128:(qq + 1) * 128], VT[tt * 4 + qq].ap[:, hd * 128:(hd + 1) * 128], SGW[:, hd, :],
                                           start=True, stop=True)
                        return ins
                    P.pe(sp4, [VT[tt * 4 + qq] for qq in range(4)] + [lw], [("ps", bank)])
                    P.dve(lambda e, hd=hd, bank=bank: e.tensor_tensor(
                        out=GT.ap.rearrange("p (q t) -> p q t", q=4), in0=self.ps[bank][:, :].rearrange("p (q t) -> p q t", q=4),
                        in1=self.row("sgub_%d" % l, hd * 128, 128).rearrange("p (o t) -> p o t", o=1).broadcast_to([128, 4, 128]), op=ALU.add),
                        [("ps", bank), "rows"], [GT])
                    d = Yc[8 + hd].sl(tt * NT, (tt + 1) * NT)
                    P.dve(lambda e, d=d, hd=hd, tt=tt: e.tensor_tensor(out=d.ap, in0=GT.ap, in1=Ub[hd].ap[:, T2[tt]], op=ALU.mult), [GT, Ub[hd]], [d])
        else:
            zero_y(8, 12)

        if "d" in groups:
            self.ssd(l, hf, Yc, base)
        else:
            zero_y(12, 16)

        for i in range(DC):
            wb = self.next_w(("out", i))
            wv = wb.ap.rearrange("p (c f) -> p c f", c=DC)
            for tt in range(2):
                bank = (i % 2) * 2 + tt

                def mm_chain(e, wv=wv, tt=tt, bank=bank):
                    ins = None
                    for rc in range(DC):
                        ins = e.matmul(self.ps[bank][:, :], wv[:, rc, :], Yc[rc].ap[:, T2[tt]], start=(rc == 0), stop=(rc == DC - 1))
                    return ins
                P.pe(mm_chain, [wb, Y], [("ps", bank)])
            self.residual_pair(i, hf, ((i % 2) * 2, (i % 2) * 2 + 1), 1.0)

    def ssd(self, l, hf, Yc, base):
        P = self.P
        A = self.A
        R = self.R
        R.reset(base)
        lw = self.LWB
        WDT = lw.ap[:, LW_DT:LW_DT + 128].rearrange("p (c m) -> p c m", c=DC)
        T2 = (slice(0, NT), slice(NT, 2 * NT))
        SZ = [R.bf16(HALF) for _ in range(4)]
        XR = [R.f32(1028)]
        XC = [R.f32(HALF) for _ in range(6)]
        CT = R.f32(HALF)
        CF = [R.bf16(HALF) for _ in range(2)]
        BF = [R.bf16(HALF) for _ in range(2)]
        DTR = R.f32(HALF)
        dtr8 = A.big[0:8, DTR.lo:DTR.lo + HALF]
        for i in range(4):
            bz = self.proj(l, 20 + i)
            for tt in range(2):
                d = SZ[i].sl(tt * NT, (tt + 1) * NT)
                P.act(lambda e, d=d, b=bz[tt]: e.activation(out=d.ap, in_=self.ps[b][:, :], func=AF.Silu), [("ps", bz[tt])], [d])
        for i in range(8):
            bx = self.proj(l, 24 + i)
            xr = XR[0]
            if hf == 0:
                P.dve(lambda e, xr=xr: e.memset(xr.ap[:, 0:4], 0.0), [], [xr.sl(0, 4)])
            else:
                P.dve(lambda e, xr=xr, i=i: e.tensor_copy(out=xr.ap[:, 0:4], in_=self.HALO_D[i].ap), [self.HALO_D[i]], [xr.sl(0, 4)])
            for tt in range(2):
                d = xr.sl(4 + tt * NT, 4 + (tt + 1) * NT)
                P.act(lambda e, d=d, b=bx[tt]: e.activation(out=d.ap, in_=self.ps[b][:, :], func=AF.Identity), [("ps", bx[tt])], [d])
            P.dve(lambda e, xr=xr, i=i: e.tensor_copy(out=self.HALO_D[i].ap, in_=xr.ap[:, 1024:1028]), [xr.sl(1024, 1028)], [self.HALO_D[i]])
            dst = XC[i] if i < 6 else CT
            P.dve(lambda e, xr=xr, dst=dst, i=i: e.tensor_scalar(out=dst.ap, in0=xr.ap[:, 1:1025], scalar1=self.col("scw_%d" % l, i * 4),
                                                              scalar2=self.col("scb_%d" % l, i), op0=ALU.mult, op1=ALU.add),
                  [xr, "cols"], [dst])
            for k in range(1, 4):
                P.dve(lambda e, xr=xr, dst=dst, i=i, k=k: e.scalar_tensor_tensor(out=dst.ap, in0=xr.ap[:, 1 + k:1025 + k],
                                                                              scalar=self.col("scw_%d" % l, i * 4 + k), in1=dst.ap,
                                                                              op0=ALU.mult, op1=ALU.add),
                      [xr, dst, "cols"], [dst])
            if i < 6:
                P.act(lambda e, dst=dst: e.activation(out=dst.ap, in_=dst.ap, func=AF.Silu), [dst], [dst])
                if i >= 4:
                    P.dve(lambda e, dst=dst, i=i: e.tensor_copy(out=BF[i - 4].ap, in_=dst.ap), [dst], [BF[i - 4]])
            else:
                P.act(lambda e, i=i: e.activation(out=CF[i - 6].ap, in_=CT.ap, func=AF.Silu), [CT], [CF[i - 6]])
        for tt in range(2):
            bank = 4 + tt

            def dt_chain(e, tt=tt, bank=bank):
                ins = None
                for c2 in range(DC):
                    ins = e.matmul(self.ps[bank][0:8, :], WDT[:, c2, :], self.Hc[c2].ap[:, T2[tt]], start=(c2 == 0), stop=(c2 == DC - 1))
                return ins
            P.pe(dt_chain, [lw, self.H], [("ps", bank)])
            d = DTR.sl(tt * NT, (tt + 1) * NT)
            P.act(lambda e, tt=tt, bank=bank: e.activation(out=dtr8[:, T2[tt]], in_=self.ps[bank][0:8, :], func=AF.Identity), [("ps", bank)], [d])

        Hs = Bump(A, self.H.lo, self.H.hi)
        DT1, DTE, DTT, DA, ACS, DIF, DS, NACS = [Hs.f32(16) for _ in range(8)]
        XTOK = Hs.f32(512)
        SCT = Hs.f32(256)
        DAREP = Hs.f32(1024)
        LH = Hs.bf16(1024)
        s1_end = Hs.o
        SETS = []
        for _ in range(2):
            SETS.append(dict(EA=Hs.f32(16), ECD=Hs.f32(16), XDT=Hs.bf16(512), XDEC=Hs.bf16(512), XSK=Hs.f32(512),
                             BTOK=Hs.bf16(256), MH=Hs.bf16(1024)))
        YT = Hs.f32(512)
        HT = Hs.f32(512)
        SQg = A.f32(self.H.lo, HALF)
        RS = A.f32(self.H.lo + HALF, HALF)
        assert self.H.lo + 2 * HALF <= s1_end + 4096
        YF = A.big[:, XC[0].lo:XC[0].lo + 4 * HALF].rearrange("p (i t) -> p i t", i=4)
        ps = self.ps

        def b864(ap8):
            return ap8.rearrange("p (h o) -> p h o", o=1).broadcast_to([128, 8, 64])

        def v864(ap512):
            return ap512.rearrange("p (h q) -> p h q", h=8)

        def stage1(q):
            S = SETS[q % 2]
            EA, ECD, XDT, XDEC, XSK, BTOK, MH = S["EA"], S["ECD"], S["XDT"], S["XDEC"], S["XSK"], S["BTOK"], S["MH"]
            qs = slice(q * 128, (q + 1) * 128)
            P.pe(lambda e: e.transpose(out=ps[4][:, 0:8], in_=dtr8[:, qs], identity=self.ident[0:8, 0:8]), [DTR.sl(q * 128, (q + 1) * 128)], [("ps", 4)])
            P.dve(lambda e: e.tensor_tensor(out=DT1.ap[:, 0:8], in0=ps[4][:, 0:8], in1=self.row("dtb_%d" % l), op=ALU.add), [("ps", 4), "rows"], [DT1])
            P.act(lambda e: e.activation(out=DTE.ap[:, 0:8], in_=DT1.ap[:, 0:8], func=AF.Exp), [DT1], [DTE])
            P.act(lambda e: e.activation(out=DTT.ap[:, 0:8], in_=DTE.ap[:, 0:8], func=AF.Ln, bias=self.col("one"), scale=1.0), [DTE, "cols"], [DTT])
            P.dve(lambda e: e.tensor_tensor(out=DA.ap[:, 0:8], in0=DTT.ap[:, 0:8], in1=self.ABC.ap, op=ALU.mult), [DTT, self.ABC], [DA])
            P.pe(lambda e: e.matmul(ps[4][:, 16:24], self.U, DA.ap[:, 0:8], start=True, stop=True), [DA, "cmat"], [("ps", 4)])
            P.pe(lambda e: e.matmul(ps[4][:, 24:32], self.ones, DA.ap[:, 0:8], start=True, stop=True), [DA, "cmat"], [("ps", 4)])
            P.act(lambda e: e.activation(out=ACS.ap, in_=ps[4][:, 16:32], func=AF.Identity), [("ps", 4)], [ACS])
            P.dve(lambda e: e.tensor_tensor(out=DIF.ap[:, 0:8], in0=ACS.ap[:, 8:16], in1=ACS.ap[:, 0:8], op=ALU.subtract), [ACS], [DIF])
            P.act(lambda e: e.activation(out=DS.ap[:, 0:8], in_=DIF.ap[:, 0:8], func=AF.Exp), [DIF], [DS])
            P.act(lambda e: e.activation(out=EA.ap[:, 0:8], in_=ACS.ap[:, 0:8], func=AF.Exp), [ACS], [EA])
            P.act(lambda e: e.activation(out=ECD.ap[:, 0:8], in_=ACS.ap[:, 8:16], func=AF.Exp), [ACS], [ECD])
            P.dve(lambda e: e.tensor_scalar(out=NACS.ap[:, 0:8], in0=ACS.ap[:, 0:8], scalar1=-1.0, scalar2=None, op0=ALU.mult), [ACS], [NACS])

            def trx(e):
                ins = None
                for i in range(4):
                    ins = e.transpose(out=ps[5][:, i * 128:(i + 1) * 128], in_=XC[i].ap[:, qs], identity=self.ident)
                return ins
            P.pe(trx, [XC[i].sl(q * 128, (q + 1) * 128) for i in range(4)], [("ps", 5)])
            P.act(lambda e: e.activation(out=XTOK.ap, in_=ps[5][:, :], func=AF.Identity), [("ps", 5)], [XTOK])
            P.dve(lambda e: e.tensor_tensor(out=v864(XDT.ap), in0=v864(XTOK.ap), in1=b864(DTT.ap[:, 0:8]), op=ALU.mult), [XTOK, DTT], [XDT])
            P.dve(lambda e: e.tensor_tensor(out=v864(XDEC.ap), in0=v864(XDT.ap), in1=b864(DS.ap[:, 0:8]), op=ALU.mult), [XDT, DS], [XDEC])
            P.dve(lambda e: e.tensor_tensor(out=v864(XSK.ap), in0=v864(XTOK.ap), in1=b864(self.row("dskip_%d" % l)), op=ALU.mult), [XTOK, "rows"], [XSK])

            def trb(e):
                ins = None
                for g in range(2):
                    ins = e.transpose(out=ps[6][:, g * 128:(g + 1) * 128], in_=XC[4 + g].ap[:, qs], identity=self.ident)
                return ins
            P.pe(trb, [XC[4].sl(q * 128, (q + 1) * 128), XC[5].sl(q * 128, (q + 1) * 128)], [("ps", 6)])
            P.act(lambda e: e.activation(out=BTOK.ap, in_=ps[6][:, 0:256], func=AF.Identity), [("ps", 6)], [BTOK])

            def sc(e):
                ins = None
                for g in range(2):
                    ins = e.matmul(ps[6][:, 256 + g * 128:256 + (g + 1) * 128], BF[g].ap[:, qs], CF[g].ap[:, qs], start=True, stop=True)
                return ins
            P.pe(sc, [BF[0].sl(q * 128, (q + 1) * 128), BF[1].sl(q * 128, (q + 1) * 128),
                      CF[0].sl(q * 128, (q + 1) * 128), CF[1].sl(q * 128, (q + 1) * 128)], [("ps", 6)])
            P.act(lambda e: e.activation(out=SCT.ap, in_=ps[6][:, 256:512], func=AF.Identity), [("ps", 6)], [SCT])
            P.dve(lambda e: e.tensor_copy(out=DAREP.ap.rearrange("p (h s) -> p h s", h=8),
                                          in_=DA.ap[:, 0:8].rearrange("p (h o) -> p h o", o=1).broadcast_to([128, 8, 128])), [DA], [DAREP])
            for h in range(8):
                bank = 2 + h // 4
                cs = slice((h % 4) * 128, (h % 4 + 1) * 128)

                def lmm(e, h=h, bank=bank, cs=cs):
                    e.matmul(ps[bank][:, cs], DAREP.ap[:, h * 128:(h + 1) * 128], self.U, start=True, stop=False)
                    return e.matmul(ps[bank][:, cs], self.ident, self.NEG, start=False, stop=True)
                P.pe(lmm, [DAREP, "cmat"], [("ps", bank)])
                P.act(lambda e, h=h, bank=bank, cs=cs: e.activation(out=LH.ap[:, h * 128:(h + 1) * 128], in_=ps[bank][:, cs], func=AF.Exp,
                                                                 bias=NACS.ap[:, h:h + 1], scale=1.0),
                      [("ps", bank), NACS], [LH.sl(h * 128, (h + 1) * 128)])
            for g in range(2):
                P.dve(lambda e, g=g: e.tensor_tensor(out=MH.ap[:, g * 512:(g + 1) * 512].rearrange("p (h t) -> p h t", h=4),
                                                    in0=LH.ap[:, g * 512:(g + 1) * 512].rearrange("p (h t) -> p h t", h=4),
                                                    in1=SCT.ap[:, g * 128:(g + 1) * 128].rearrange("p (o t) -> p o t", o=1).broadcast_to([128, 4, 128]),
                                                    op=ALU.mult),
                      [LH.sl(g * 512, (g + 1) * 512), SCT], [MH.sl(g * 512, (g + 1) * 512)])

        def stage2(q):
            S = SETS[q % 2]
            EA, ECD, XDT, XDEC, XSK, BTOK, MH = S["EA"], S["ECD"], S["XDT"], S["XDEC"], S["XSK"], S["BTOK"], S["MH"]
            qs = slice(q * 128, (q + 1) * 128)

            def ydiag(e):
                ins = None
                for h in range(8):
                    ins = e.matmul(ps[0][:, h * 64:(h + 1) * 64], MH.ap[:, h * 128:(h + 1) * 128], XDT.ap[:, h * 64:(h + 1) * 64], start=True, stop=True)
                return ins
            P.pe(ydiag, [MH, XDT], [("ps", 0)])

            def yoff(e):
                ins = None
                for g in range(2):
                    ins = e.matmul(ps[1][:, g * 256:(g + 1) * 256], CF[g].ap[:, qs], self.HSTb.ap[:, g * 256:(g + 1) * 256], start=True, stop=True)
                return ins
            P.pe(yoff, [CF[0].sl(q * 128, (q + 1) * 128), CF[1].sl(q * 128, (q + 1) * 128), self.HSTb], [("ps", 1)])

            def stt(e):
                ins = None
                for g in range(2):
                    ins = e.matmul(ps[7][:, g * 256:(g + 1) * 256], BTOK.ap[:, g * 128:(g + 1) * 128], XDEC.ap[:, g * 256:(g + 1) * 256], start=True, stop=True)
                return ins
            P.pe(stt, [BTOK, XDEC], [("ps", 7)])
            P.dve(lambda e: e.tensor_tensor(out=v864(YT.ap), in0=v864(ps[1][:, :]), in1=b864(EA.ap[:, 0:8]), op=ALU.mult), [("ps", 1), EA], [YT])
            P.dve(lambda e: e.tensor_tensor(out=YT.ap, in0=YT.ap, in1=ps[0][:, :], op=ALU.add), [YT, ("ps", 0)], [YT])
            P.dve(lambda e: e.tensor_tensor(out=YT.ap, in0=YT.ap, in1=XSK.ap, op=ALU.add), [YT, XSK], [YT])
            P.dve(lambda e: e.tensor_tensor(out=v864(HT.ap), in0=v864(self.HST.ap), in1=b864(ECD.ap[:, 0:8]), op=ALU.mult), [self.HST, ECD], [HT])
            P.dve(lambda e: e.tensor_tensor(out=self.HST.ap, in0=HT.ap, in1=ps[7][:, :], op=ALU.add), [HT, ("ps", 7)], [self.HST])
            P.act(lambda e: e.activation(out=self.HSTb.ap, in_=self.HST.ap, func=AF.Identity), [self.HST], [self.HSTb])

            def try_(e):
                ins = None
                for i in range(4):
                    ins = e.transpose(out=ps[1][:, i * 128:(i + 1) * 128], in_=YT.ap[:, i * 128:(i + 1) * 128], identity=self.ident)
                return ins
            P.pe(try_, [YT], [("ps", 1)])
            P.act(lambda e: e.activation(out=YF[:, :, qs], in_=ps[1][:, :].rearrange("p (i t) -> p i t", i=4), func=AF.Identity),
                  [("ps", 1)], [XC[i].sl(q * 128, (q + 1) * 128) for i in range(4)])

        stage1(0)
        for q in range(8):
            if q + 1 < 8:
                stage1(q + 1)
            stage2(q)
        for i in range(4):
            P.dve(lambda e, i=i: e.tensor_tensor(out=XC[i].ap, in0=XC[i].ap, in1=SZ[i].ap, op=ALU.mult), [XC[i], SZ[i]], [XC[i]])
        for g in range(2):
            for ii in range(2):
                i = 2 * g + ii
                P.act(lambda e, i=i: e.activation(out=SQg.ap, in_=XC[i].ap, func=AF.Square), [XC[i]], [SQg])
                for tt in range(2):
                    P.pe(lambda e, tt=tt, ii=ii: e.matmul(ps[4 + tt][:, :], self.ones, SQg.ap[:, T2[tt]], start=(ii == 0), stop=(ii == 1)),
                         [SQg, "cmat"], [("ps", 4 + tt)])
            for tt in range(2):
                rs = RS.sl(tt * NT, (tt + 1) * NT)
                P.act(lambda e, tt=tt, rs=rs: e.activation(out=rs.ap, in_=ps[4 + tt][:, :], func=AF.Sqrt, bias=self.col("eps"), scale=1.0 / 256),
                      [("ps", 4 + tt), "cols"], [rs])
                P.dve(lambda e, rs=rs: e.reciprocal(out=rs.ap, in_=rs.ap), [rs], [rs])
            for ii in range(2):
                i = 2 * g + ii
                P.dve(lambda e, i=i: e.scalar_tensor_tensor(out=Yc[12 + i].ap, in0=XC[i].ap, scalar=self.col("snorm_%d" % l, i), in1=RS.ap,
                                                           op0=ALU.mult, op1=ALU.mult),
                      [XC[i], RS, "cols"], [Yc[12 + i]])

    def finish(self):
        P = self.P
        keys = [("OUT", i) for i in range(DC)] + [("OUT", i, hf) for i in range(DC) for hf in range(self.NH)]
        P.pool(lambda e: e.memset(self.XS[0].ap[:, 0:1], 0.0), keys + [self.XS[0]], [self.XS[0]])

    def build(self):
        self.wslot = 0
        self.xslot = 0
        self.pbank = 0
        with contextlib.ExitStack() as stack:
            self.plan(stack)
            self.load_consts()
            if not self.skip_copy:
                self.copy_x_in()
            for st in self.stages:
                kind = st[0]
                if kind == "ffn":
                    self.ffn(st[1], st[2], st[3])
                elif kind == "setup":
                    self.layer_setup(st[1])
                elif kind == "mix":
                    self.mixer(st[1], st[2], *(st[3:]))
                elif kind == "final":
                    self.final_norm(st[1])
                elif kind == "dump":
                    self.dump()
            self.finish()
            self.P.emit(self.nc, stack)
        return self.nc


def run(inputs, stages, NH=8, ncores=2, trace=False, skip_copy=False):
    inp = {k: np.asarray(v) for k, v in inputs.items()}
    cp, rp = make_packs(inp)
    cols = cp.build()
    rows = rp.build()
    W = prep_weights(inp)
    b = Builder(stages, NH, cp, rp, skip_copy=skip_copy)
    nc = b.build()
    T = NH * HALF
    cm = make_cmat()
    in_maps = []
    for c in range(ncores):
        xin = np.ascontiguousarray(inp["x"][c, :T, :].T).reshape(DC, 128, T)
        m = {"xin": xin, "cols": cols, "rows": rows, "cmat": cm}
        m.update({k: v for k, v in W.items() if k in b.w})
        in_maps.append(m)
    res = run_bass_kernel_spmd(nc, in_maps, core_ids=list(range(ncores)), **({"trace": True} if trace else {}))
    if trace:
        print("EXEC_NS", res.exec_time_ns, flush=True)
    outs = [np.asarray(r["out"]).reshape(D, T).T for r in res.results]
    return np.ascontiguousarray(np.stack(outs)).astype(np.float32)


def full_stages(NH=8):
    st = []
    for l in range(L):
        st.append(("setup", l))
        for hf in range(NH):
            st.append(("ffn", l, "ffn1", hf))
            st.append(("mix", l, hf))
            st.append(("ffn", l, "ffn2", hf))
    for hf in range(NH):
        st.append(("final", hf))
    return st


def kernel(**inputs):
    return run(inputs, full_stages(8), NH=8, ncores=2, skip_copy=True)
```
